# Optimizing a Trainium2 kernel written in Bass

```python
import jax, jax.numpy as jnp
from jax import lax
import numpy as np

D_MODEL = 1024
BATCH = 4
SEQ = 4096
DEPTH = 2

MEM_LEN = 256
D_MIX = D_MODEL
HEAD_DIM = 64
SG_WIDTH = D_MIX // 2
SG_GROUPS = SG_WIDTH // HEAD_DIM
SG_CHUNK = 128
NSA_WIDTH = D_MIX - SG_WIDTH
NSA_HEADS = NSA_WIDTH // HEAD_DIM
NSA_KV_HEADS = 2
NSA_GROUP = NSA_HEADS // NSA_KV_HEADS
KV_WIDTH = NSA_KV_HEADS * HEAD_DIM
N_BRANCH = 3
CMP_BLOCK = 32
CMP_STRIDE = 16
CMP_HIDDEN = 256
SLC_BLOCK = 64
SLC_TOPK = 16
WINDOW = 512
Q_BLOCK = 128
FORCE_SCORE = 1e4
IN_COLS = 2 * SG_WIDTH + NSA_WIDTH + 6 * KV_WIDTH + NSA_HEADS * N_BRANCH
MEM_HEADS = 4
MEM_HEAD_DIM = 128
D_FF = 4 * D_MODEL
EPS = 1e-6

kernel_name = "hybrid_sgmlp_nsa_memory_block"


def rms_norm(x, g):
    xf = x.astype(jnp.float32)
    y = xf * lax.rsqrt(jnp.mean(xf * xf, axis=-1, keepdims=True) + EPS)
    return (y * g.astype(jnp.float32)).astype(x.dtype)


def layer_norm(x, g, b):
    xf = x.astype(jnp.float32)
    mu = jnp.mean(xf, axis=-1, keepdims=True)
    xc = xf - mu
    y = xc * lax.rsqrt(jnp.mean(xc * xc, axis=-1, keepdims=True) + EPS)
    return (y * g.astype(jnp.float32) + b.astype(jnp.float32)).astype(x.dtype)


def masked_softmax(s, mask):
    s = jnp.where(mask, s.astype(jnp.float32), -jnp.inf)
    m = jnp.max(s, axis=-1, keepdims=True)
    m = jnp.where(jnp.isfinite(m), m, 0.0)
    e = jnp.exp(s - m)
    d = jnp.sum(e, axis=-1, keepdims=True)
    return e / jnp.where(d > 0, d, 1.0)


def spatial_gating_unit(u, v, ln_g, ln_b, w_s, b_s):
    B, T, _ = u.shape
    v = layer_norm(v, ln_g, ln_b)
    vb = v.reshape(B, T // SG_CHUNK, SG_CHUNK, SG_GROUPS, HEAD_DIM)
    causal = jnp.tril(jnp.ones((SG_CHUNK, SG_CHUNK), dtype=bool))
    w = jnp.where(causal[None], w_s, 0)
    s = jnp.einsum('gts,bcsgd->bctgd', w, vb) + b_s.T[None, None, :, :, None]
    return u * s.reshape(B, T, SG_WIDTH)


def compress_blocks(k, pos, w1, b1, w2, b2):
    B, T, H, d = k.shape
    kc = k.reshape(B, T // CMP_STRIDE, CMP_STRIDE, H, d)
    blocks = jnp.concatenate([kc[:, :-1], kc[:, 1:]], axis=2) + pos[None, None, :, None, :]
    flat = blocks.transpose(0, 1, 3, 2, 4).reshape(B, -1, H, CMP_BLOCK * d)
    h = jax.nn.gelu(flat @ w1 + b1)
    return h @ w2 + b2


def nsa_attention(q, kv, gate_logits, q_norm_g, k_norm_g, cmp_pos, cmp_w1, cmp_b1, cmp_w2, cmp_b2):
    B, T = q.shape[:2]
    H, G, d = NSA_KV_HEADS, NSA_GROUP, HEAD_DIM
    n_cmp = T // CMP_STRIDE - 1
    n_slc = T // SLC_BLOCK
    n_sel = min(SLC_TOPK, n_slc)
    n_qb = T // Q_BLOCK
    scale = HEAD_DIM ** -0.5
    t_pos = jnp.arange(T)

    q = rms_norm(q.reshape(B, T, H, G, d), q_norm_g)
    k_cmp, v_cmp, k_slc, v_slc, k_win, v_win = [
        a.reshape(B, T, H, d) for a in jnp.split(kv, 6, axis=-1)]

    kc = rms_norm(compress_blocks(k_cmp, cmp_pos[0], cmp_w1[0], cmp_b1[0], cmp_w2[0], cmp_b2[0]), k_norm_g[0])
    vc = compress_blocks(v_cmp, cmp_pos[1], cmp_w1[1], cmp_b1[1], cmp_w2[1], cmp_b2[1])
    s_cmp = jnp.einsum('bthgd,bnhd->bhgtn', q, kc) * scale
    cmp_end = jnp.arange(n_cmp) * CMP_STRIDE + CMP_BLOCK - 1
    p_cmp = masked_softmax(s_cmp, cmp_end[None, :] <= t_pos[:, None])
    o_cmp = jnp.einsum('bhgtn,bnhd->bthgd', p_cmp.astype(vc.dtype), vc)

    cs = jnp.arange(n_cmp) * CMP_STRIDE
    ss = jnp.arange(n_slc) * SLC_BLOCK
    overlap = jnp.clip(jnp.minimum(cs[:, None] + CMP_BLOCK, ss[None, :] + SLC_BLOCK)
                       - jnp.maximum(cs[:, None], ss[None, :]), 0, None).astype(jnp.float32) / CMP_BLOCK
    imp = jnp.einsum('bhgtn,nj->bhtj', p_cmp, overlap)
    t_blk = t_pos[:, None] // SLC_BLOCK
    j = jnp.arange(n_slc)[None, :]
    forced = (j == 0) | (j == t_blk) | (j == t_blk - 1)
    imp = jnp.where(forced, FORCE_SCORE, jnp.where(j <= t_blk, imp, -FORCE_SCORE))
    _, sel = lax.top_k(imp, n_sel)

    k_slc = rms_norm(k_slc, k_norm_g[1])
    kb = k_slc.reshape(B, n_slc, SLC_BLOCK, H, d).transpose(0, 3, 1, 2, 4)
    vb = v_slc.reshape(B, n_slc, SLC_BLOCK, H, d).transpose(0, 3, 1, 2, 4)
    qb = q.reshape(B, n_qb, Q_BLOCK, H, G, d).transpose(1, 0, 2, 3, 4, 5)
    sb = sel.reshape(B, H, n_qb, Q_BLOCK, n_sel).transpose(2, 0, 1, 3, 4)
    tb = t_pos.reshape(n_qb, Q_BLOCK)
    bi = jnp.arange(B)[:, None, None, None]
    hi = jnp.arange(H)[None, :, None, None]

    def selected_block(args):
        qc, ic, tc = args
        kg = kb[bi, hi, ic].reshape(B, H, Q_BLOCK, n_sel * SLC_BLOCK, d)
        vg = vb[bi, hi, ic].reshape(B, H, Q_BLOCK, n_sel * SLC_BLOCK, d)
        key_pos = (ic[..., None] * SLC_BLOCK + jnp.arange(SLC_BLOCK)).reshape(B, H, Q_BLOCK, n_sel * SLC_BLOCK)
        mask = (key_pos <= tc[None, None, :, None])[:, :, None]
        s = jnp.einsum('bthgd,bhtmd->bhgtm', qc, kg) * scale
        p = masked_softmax(s, mask)
        return jnp.einsum('bhgtm,bhtmd->bthgd', p.astype(vg.dtype), vg)

    o_slc = lax.map(selected_block, (qb, sb, tb)).transpose(1, 0, 2, 3, 4, 5).reshape(B, T, H, G, d)

    k_win = rms_norm(k_win, k_norm_g[2])
    pad = ((0, 0), (WINDOW, 0), (0, 0), (0, 0))
    kp = jnp.pad(k_win, pad)
    vp = jnp.pad(v_win, pad)
    win_idx = jnp.arange(n_qb)[:, None] * Q_BLOCK + jnp.arange(WINDOW + Q_BLOCK)[None, :]
    kw = kp[:, win_idx]
    vw = vp[:, win_idx]
    qw = q.reshape(B, n_qb, Q_BLOCK, H, G, d)
    s_win = jnp.einsum('bcthgd,bckhd->bhgctk', qw, kw) * scale
    q_abs = tb[:, :, None]
    k_abs = (win_idx - WINDOW)[:, None, :]
    win_mask = (k_abs <= q_abs) & (k_abs > q_abs - WINDOW) & (k_abs >= 0)
    p_win = masked_softmax(s_win, win_mask)
    o_win = jnp.einsum('bhgctk,bckhd->bcthgd', p_win.astype(vw.dtype), vw).reshape(B, T, H, G, d)

    g = jax.nn.sigmoid(gate_logits.astype(jnp.float32)).reshape(B, T, H, G, N_BRANCH).astype(q.dtype)
    o = g[..., 0:1] * o_cmp + g[..., 1:2] * o_slc + g[..., 2:3] * o_win
    return o.reshape(B, T, NSA_WIDTH)


def memory_cross_attention(h, mem, kv_norm_g, w_mq, w_mkv, q_g, k_g, w_mo):
    B, T, _ = h.shape
    M = mem.shape[1]
    q = rms_norm((h @ w_mq).reshape(B, T, MEM_HEADS, MEM_HEAD_DIM), q_g)
    k, v = jnp.split(rms_norm(mem, kv_norm_g) @ w_mkv, 2, axis=-1)
    k = rms_norm(k.reshape(B, M, MEM_HEADS, MEM_HEAD_DIM), k_g)
    v = v.reshape(B, M, MEM_HEADS, MEM_HEAD_DIM)
    s = jnp.einsum('bthd,bmhd->bhtm', q, k) * MEM_HEAD_DIM ** -0.5
    p = jax.nn.softmax(s.astype(jnp.float32), axis=-1).astype(v.dtype)
    o = jnp.einsum('bhtm,bmhd->bthd', p, v).reshape(B, T, MEM_HEADS * MEM_HEAD_DIM)
    return o @ w_mo


def setup_inputs(seed: int = 0) -> dict:
    key = jax.random.key(seed)
    ks = jax.random.split(key, 32)

    def nrm(k, shape, scale):
        return jax.random.normal(k, shape, jnp.float32) * scale

    def gain(k, shape):
        return 1.0 + 0.05 * jax.random.normal(k, shape, jnp.float32)

    L = DEPTH
    return {
        'x': nrm(ks[0], (BATCH, SEQ, D_MODEL), 1.0),
        'mem': nrm(ks[1], (BATCH, MEM_LEN, D_MODEL), 1.0),
        'norm_mix_g': gain(ks[2], (L, D_MODEL)),
        'w_in': nrm(ks[3], (L, D_MODEL, IN_COLS), D_MODEL ** -0.5),
        'sg_ln_g': gain(ks[4], (L, SG_WIDTH)),
        'sg_ln_b': nrm(ks[5], (L, SG_WIDTH), 0.02),
        'sg_w': nrm(ks[6], (L, SG_GROUPS, SG_CHUNK, SG_CHUNK), SG_CHUNK ** -0.5),
        'sg_b': 1.0 + nrm(ks[7], (L, SG_GROUPS, SG_CHUNK), 0.1),
        'q_norm_g': gain(ks[8], (L, HEAD_DIM)),
        'k_norm_g': gain(ks[9], (L, N_BRANCH, HEAD_DIM)),
        'cmp_pos': nrm(ks[10], (L, 2, CMP_BLOCK, HEAD_DIM), 0.5),
        'cmp_w1': nrm(ks[11], (L, 2, CMP_BLOCK * HEAD_DIM, CMP_HIDDEN), (CMP_BLOCK * HEAD_DIM) ** -0.5),
        'cmp_b1': nrm(ks[12], (L, 2, CMP_HIDDEN), 0.02),
        'cmp_w2': nrm(ks[13], (L, 2, CMP_HIDDEN, HEAD_DIM), CMP_HIDDEN ** -0.5),
        'cmp_b2': nrm(ks[14], (L, 2, HEAD_DIM), 0.02),
        'mix_out_g': gain(ks[15], (L, 2, SG_WIDTH)),
        'w_out': nrm(ks[16], (L, D_MIX, D_MODEL), D_MIX ** -0.5),
        'norm_mem_g': gain(ks[17], (L, D_MODEL)),
        'mem_kv_norm_g': gain(ks[18], (L, D_MODEL)),
        'w_mq': nrm(ks[19], (L, D_MODEL, MEM_HEADS * MEM_HEAD_DIM), D_MODEL ** -0.5),
        'w_mkv': nrm(ks[20], (L, D_MODEL, 2 * MEM_HEADS * MEM_HEAD_DIM), D_MODEL ** -0.5),
        'mem_q_norm_g': gain(ks[21], (L, MEM_HEAD_DIM)),
        'mem_k_norm_g': gain(ks[22], (L, MEM_HEAD_DIM)),
        'w_mo': nrm(ks[23], (L, MEM_HEADS * MEM_HEAD_DIM, D_MODEL), (MEM_HEADS * MEM_HEAD_DIM) ** -0.5),
        'norm_ffn_g': gain(ks[24], (L, D_MODEL)),
        'w_ff1': nrm(ks[25], (L, D_MODEL, D_FF), D_MODEL ** -0.5),
        'w_ff2': nrm(ks[26], (L, D_FF, D_MODEL), D_FF ** -0.5),
    }


def reference(x, mem, norm_mix_g, w_in, sg_ln_g, sg_ln_b, sg_w, sg_b, q_norm_g, k_norm_g,
              cmp_pos, cmp_w1, cmp_b1, cmp_w2, cmp_b2, mix_out_g, w_out,
              norm_mem_g, mem_kv_norm_g, w_mq, w_mkv, mem_q_norm_g, mem_k_norm_g, w_mo,
              norm_ffn_g, w_ff1, w_ff2):
    splits = [SG_WIDTH, 2 * SG_WIDTH, 2 * SG_WIDTH + NSA_WIDTH,
              2 * SG_WIDTH + NSA_WIDTH + 6 * KV_WIDTH]
    for l in range(DEPTH):
        h = rms_norm(x, norm_mix_g[l])
        z = h @ w_in[l]
        u, v, q, kv, gl = jnp.split(z, splits, axis=-1)
        a = spatial_gating_unit(jax.nn.gelu(u), jax.nn.gelu(v), sg_ln_g[l], sg_ln_b[l], sg_w[l], sg_b[l])
        b = nsa_attention(q, kv, gl, q_norm_g[l], k_norm_g[l], cmp_pos[l], cmp_w1[l], cmp_b1[l],
                          cmp_w2[l], cmp_b2[l])
        mixed = jnp.concatenate([rms_norm(a, mix_out_g[l, 0]), rms_norm(b, mix_out_g[l, 1])], axis=-1)
        x = x + mixed @ w_out[l]
        x = x + memory_cross_attention(rms_norm(x, norm_mem_g[l]), mem, mem_kv_norm_g[l], w_mq[l],
                                       w_mkv[l], mem_q_norm_g[l], mem_k_norm_g[l], w_mo[l])
        h = rms_norm(x, norm_ffn_g[l])
        x = x + jnp.square(jax.nn.relu(h @ w_ff1[l])) @ w_ff2[l]
    return x
```

```python
import contextlib
import os
import numpy as np
import concourse.bass as bass
import concourse.mybir as mybir
from concourse.bass_utils import run_bass_kernel_spmd

F32 = mybir.dt.float32
BF16 = mybir.dt.bfloat16
ALU = mybir.AluOpType
AF = mybir.ActivationFunctionType
AX = mybir.AxisListType

ENGS = ("pe", "act", "dve", "pool", "sp")
EPS = 1e-6
NEG = -30000.0


class Prog:
    def __init__(self, nc):
        self.nc = nc
        self.ops = []
        self.last_writer = {}
        self.readers = {}
        self.barrier_deps = set()
        self.last_eng_op = {}
        self.last_dma_op = {}

    def op(self, eng, fn, reads=(), writes=(), dma=None, inc=None):
        idx = len(self.ops)
        deps = set(self.barrier_deps)
        for r in reads:
            w = self.last_writer.get(r)
            if w is not None:
                deps.add(w)
        for w_ in writes:
            w = self.last_writer.get(w_)
            if w is not None:
                deps.add(w)
            for rd in self.readers.get(w_, ()):
                deps.add(rd)
        if dma is not None and dma in self.last_dma_op:
            deps.add(self.last_dma_op[dma])
        self.ops.append(dict(eng=eng, fn=fn, deps=deps, dma=dma, sig=False, inc=(inc if inc is not None else (16 if dma is not None else 1))))
        for r in reads:
            self.readers.setdefault(r, []).append(idx)
        for w_ in writes:
            self.last_writer[w_] = idx
            self.readers[w_] = []
        if dma is None:
            self.last_eng_op[eng] = idx
        else:
            self.last_dma_op[dma] = idx
        return idx

    def barrier(self):
        self.barrier_deps = set(self.last_eng_op.values()) | set(self.last_dma_op.values())

    def _skip(self, p, o):
        return p["dma"] is None and o["dma"] is None and p["eng"] == o["eng"] and p["eng"] == "pe"

    def emit(self, final_wait_eng="sp"):
        nc = self.nc
        ops = self.ops
        for o in ops:
            for d in o["deps"]:
                p = ops[d]
                if not self._skip(p, o):
                    p["sig"] = True
            if o["dma"] is not None:
                o["sig"] = True
        counters = {}
        for o in ops:
            if not o["sig"]:
                continue
            sname = ("dma:" + o["dma"]) if o["dma"] is not None else ("eng:" + o["eng"])
            counters[sname] = counters.get(sname, 0) + o["inc"]
            o["signal"] = (sname, counters[sname])
        with contextlib.ExitStack() as es:
            sems = {}
            for sn in sorted(counters):
                sems[sn] = es.enter_context(nc.semaphore(sn.replace(":", "_")))
            block = es.enter_context(nc.Block())
            per_eng = {e: [] for e in ENGS}
            for i, o in enumerate(ops):
                per_eng[o["eng"]].append(i)

            def body(ename, engine):
                waited = {}
                for i in per_eng[ename]:
                    o = ops[i]
                    need = {}
                    for d in o["deps"]:
                        p = ops[d]
                        if not p["sig"] or self._skip(p, o):
                            continue
                        sn, val = p["signal"]
                        if need.get(sn, 0) < val:
                            need[sn] = val
                    for sn, val in need.items():
                        if waited.get(sn, 0) < val:
                            engine.wait_ge(sems[sn], val)
                            waited[sn] = val
                    ins = o["fn"](engine)
                    if o["sig"]:
                        sn, val = o["signal"]
                        ins.then_inc(sems[sn], o["inc"])
                if ename == final_wait_eng:
                    for sn, val in counters.items():
                        if sn.startswith("dma:") and waited.get(sn, 0) < val:
                            engine.wait_ge(sems[sn], val)

            regs = dict(pe=block.tensor, act=block.scalar, dve=block.vector, pool=block.gpsimd, sp=block.sync)
            for ename in ENGS:
                if not per_eng[ename] and ename != final_wait_eng:
                    continue

                def mk(en):
                    def f(engine):
                        body(en, engine)
                    return f
                regs[ename](mk(ename))
        return counters


class Rot:
    def __init__(self, items):
        self.items = items
        self.i = 0

    def next(self):
        it = self.items[self.i % len(self.items)]
        self.i += 1
        return it


class Arena:
    def __init__(self, t, nbytes):
        self.t = t
        self.nbytes = nbytes
        self.off = 0
        self.cnt = 0

    def alloc(self, free_shape, dt, name):
        n = int(np.prod(free_shape))
        esz = 4 if dt == F32 else 2
        nb = (n * esz + 63) // 64 * 64
        assert self.off + nb <= self.nbytes, (name, self.off, nb, self.nbytes)
        a = self.off // 2
        v = self.t[:, a:a + (n * esz) // 2]
        if dt == F32:
            v = v.bitcast(F32)
        self.off += nb
        self.cnt += 1
        key = "%s@%d" % (name, self.cnt)
        if len(free_shape) == 2:
            v = v.rearrange("p (a b) -> p a b", b=free_shape[1])
        elif len(free_shape) == 3:
            v = v.rearrange("p (a b c) -> p a b c", b=free_shape[1], c=free_shape[2])
        return v, key


GV = {}
_o = 0
for _n, _s in [("mixg", 1024), ("memg", 1024), ("ffng", 1024), ("mkvg", 1024), ("lng", 512), ("lnb", 512),
               ("mog0", 512), ("mog1", 512), ("qg", 512), ("kg", 256), ("kcg", 64), ("b2k", 64), ("b2v", 64),
               ("mqg", 512), ("mkg", 512)]:
    GV[_n] = (_o, _s)
    _o += _s
NGV = _o


def host_consts(half):
    c = {}
    c["identf"] = np.eye(128, dtype=np.float32)
    k = np.arange(4096)
    c["Econst"] = (k[None, :] // 64 == np.arange(64)[:, None]).astype(np.float32)
    OFF = 240
    cc = np.arange(512) - OFF - 8 * half
    A = np.zeros((64, 512), np.float32)
    C = np.zeros((64, 128), np.float32)
    tl = np.arange(128)
    for j in range(8):
        m = j - 1
        A[j] = (cc == m)
        C[j] = np.where(16 * m + 31 > tl, NEG, 0.0)
    A[8] = (cc >= 7)
    C[8] = NEG
    c["Astat"] = A
    c["Cst"] = np.tile(C, (1, 4)).astype(np.float32)
    kl = np.arange(128)[:, None]
    ql = np.arange(128)[None, :]
    tri = (kl <= ql).astype(np.float32)
    left = (kl > ql).astype(np.float32)
    ones = np.ones((128, 128), np.float32)
    zeros = np.zeros((128, 128), np.float32)
    if half == 0:
        mD = [tri, zeros]
        mW = [left, ones, tri, zeros]
    else:
        mD = [ones, tri]
        mW = [zeros, left, ones, tri]
    c["maskD"] = np.stack([np.tile(m, (1, 4)) for m in mD], 1).astype(np.float32)
    c["maskW"] = np.stack([np.tile(m, (1, 4)) for m in mW], 1).astype(np.float32)
    Mv = np.zeros((128, 16, 64), np.float32)
    Ma = np.zeros((128, 16, 64), np.float32)
    j = np.arange(64)[None, :]
    for i in range(16):
        G = 2 * i + half
        t = 128 * G + np.arange(128)[:, None]
        tb = t // 64
        forced = (j == 0) | (j == tb) | (j == tb - 1)
        invalid = (j > tb)
        Mv[:, i, :] = (~forced & ~invalid)
        Ma[:, i, :] = np.where(forced, 1e4, np.where(invalid, -1e4, 0.0))
    c["Mvalid"] = Mv
    c["Madd"] = Ma
    n = np.arange(256)
    cs = n * 16
    ss = np.arange(64) * 64
    ov = np.clip(np.minimum(cs[:, None] + 32, ss[None, :] + 64) - np.maximum(cs[:, None], ss[None, :]), 0, None) / 32.0
    ov[255] = 0.0
    c["ovc"] = ov.reshape(2, 128, 64).transpose(1, 0, 2).astype(np.float32).copy()
    c["trisg"] = tri.copy()
    return c


def host_layer_weights(inp, l):
    w = {}
    wi = inp["w_in"][l]
    kv = wi[:, 1536:2304]
    kc, vc, ks, vs, kw, vw = [kv[:, i * 128:(i + 1) * 128] for i in range(6)]
    w["w_in_p"] = np.concatenate([wi[:, 0:1536], wi[:, 2304:2328], ks, kw, vs, vw, kc, vc], axis=1)
    gv = np.zeros((1, NGV), np.float32)

    def put(name, v):
        o, s = GV[name]
        gv[0, o:o + s] = np.asarray(v).reshape(-1)
    put("mixg", inp["norm_mix_g"][l]); put("memg", inp["norm_mem_g"][l]); put("ffng", inp["norm_ffn_g"][l])
    put("mkvg", inp["mem_kv_norm_g"][l]); put("lng", inp["sg_ln_g"][l]); put("lnb", inp["sg_ln_b"][l])
    put("mog0", inp["mix_out_g"][l, 0]); put("mog1", inp["mix_out_g"][l, 1])
    put("qg", np.tile(inp["q_norm_g"][l], 8))
    put("kg", np.concatenate([np.tile(inp["k_norm_g"][l, 1], 2), np.tile(inp["k_norm_g"][l, 2], 2)]))
    put("kcg", inp["k_norm_g"][l, 0]); put("b2k", inp["cmp_b2"][l, 0]); put("b2v", inp["cmp_b2"][l, 1])
    put("mqg", np.tile(inp["mem_q_norm_g"][l], 4)); put("mkg", np.tile(inp["mem_k_norm_g"][l], 4))
    w["gvec"] = gv
    w["sg_wT"] = np.ascontiguousarray(inp["sg_w"][l].transpose(2, 0, 1))
    w["sg_bT"] = np.ascontiguousarray(inp["sg_b"][l].T)
    w["cmp_w1"] = np.ascontiguousarray(inp["cmp_w1"][l])
    w["cmp_b1T"] = np.ascontiguousarray(inp["cmp_b1"][l].reshape(2, 2, 128).transpose(0, 2, 1))
    w["cmp_posT"] = np.ascontiguousarray(inp["cmp_pos"][l].transpose(0, 2, 1))
    w["cmp_w2"] = np.ascontiguousarray(inp["cmp_w2"][l])
    for k in ("w_out", "w_mq", "w_mkv", "w_mo", "w_ff1", "w_ff2"):
        w[k] = np.ascontiguousarray(inp[k][l])
    return w


WSHAPES = dict(w_in_p=[1024, 2328], gvec=[1, NGV], sg_wT=[128, 8, 128], sg_bT=[128, 8], cmp_w1=[2, 2048, 256],
               cmp_b1T=[2, 128, 2], cmp_posT=[2, 64, 32], cmp_w2=[2, 256, 64], w_out=[1024, 1024],
               w_mq=[1024, 512], w_mkv=[1024, 1024], w_mo=[512, 1024], w_ff1=[1024, 4096], w_ff2=[4096, 1024])
CSHAPES = dict(identf=[128, 128], Econst=[64, 4096], Astat=[64, 512], Cst=[64, 512], maskD=[128, 2, 512],
               maskW=[128, 4, 512], Mvalid=[128, 16, 64], Madd=[128, 16, 64], ovc=[128, 2, 64], trisg=[128, 128])


def build(nl=1, stop=None, dbg=None, ncores=8):
    nc = bass.Bass("TRN2", target_bir_lowering=False)
    D = {}
    D["x_own"] = nc.dram_tensor("x_own", [2048, 1024], F32, kind="ExternalInput").ap()
    hx_in = [nc.dram_tensor("hxin%d" % g, [512, 1024], BF16) for g in range(4)]
    hx_out = [nc.dram_tensor("hxout%d" % g, [1024, 1024], BF16) for g in range(4)]
    RG = [[2 * p, 2 * p + 1] for p in range(ncores // 2)]
    D["mem"] = nc.dram_tensor("mem", [256, 1024], F32, kind="ExternalInput").ap()
    for k, s in WSHAPES.items():
        D[k] = nc.dram_tensor(k, [nl] + s, F32, kind="ExternalInput").ap()
    for k, s in CSHAPES.items():
        D[k] = nc.dram_tensor(k, s, F32, kind="ExternalInput").ap()
    D["y"] = nc.dram_tensor("y", [2048, 1024], F32, kind="ExternalOutput").ap()
    dbg_out = {}

    with contextlib.ExitStack() as es:
        def sb(name, shape, dt=F32):
            return es.enter_context(nc.sbuf_tensor("s_" + name, shape, dt))

        x_res = sb("x_res", [128, 16, 1024], F32)
        ARB = 128 * 1024
        arena_t = sb("arena", [128, ARB // 2], BF16)
        identb = sb("identb", [128, 128], BF16)
        Astat = sb("Astat", [64, 512], BF16)
        Cst = sb("Cst", [64, 512], BF16)
        maskD = sb("maskD", [128, 2, 512], BF16)
        maskW = sb("maskW", [128, 4, 512], BF16)
        Mvalid = sb("Mvalid", [128, 16, 64], BF16)
        Madd = sb("Madd", [128, 16, 64], BF16)
        trisg = sb("trisg", [128, 128], BF16)
        zerob = sb("zerob", [128, 512], BF16)
        ps = [es.enter_context(nc.psum_tensor("ps%d" % b, [128, 512], F32)) for b in range(8)]

        P = Prog(nc)
        AR = Arena(arena_t, ARB)
        dmac = [0]

        def dma(eng, out, in_, reads=(), writes=(), stream=None):
            if stream is None:
                dmac[0] += 1
                stream = "q%s%d" % (eng, dmac[0] % 12)
            P.op(eng, lambda e, out=out, in_=in_: e.dma_start(out=out, in_=in_), reads=reads, writes=writes, dma=stream)

        def I(eng, meth, kw, reads=(), writes=()):
            P.op(eng, lambda e, kw=kw, meth=meth: getattr(e, meth)(**kw), reads=reads, writes=writes)

        def dump(name, ap, shape, reads):
            if dbg is None or name not in dbg:
                return
            d = nc.dram_tensor("dbg_" + name, shape, F32 if ap.dtype == F32 else BF16, kind="ExternalOutput").ap()
            dbg_out[name] = d
            dma("sp", d, ap, reads=reads, stream="dbg")

        for t, nm in ((identb, "identf"), (Astat, "Astat"), (Cst, "Cst"), (maskD, "maskD"), (maskW, "maskW"),
                      (Mvalid, "Mvalid"), (Madd, "Madd"), (trisg, "trisg")):
            dma("pool", t[:], D[nm], writes=[nm])
        I("dve", "memset", dict(ap=zerob[:], constant=0.0), writes=["zerob"])
        xo = D["x_own"].rearrange("(i p) d -> p i d", p=128)
        for i4 in range(4):
            dma("sp", x_res[:, i4 * 4:(i4 + 1) * 4, :], xo[:, i4 * 4:(i4 + 1) * 4, :],
                writes=["x%d" % i for i in range(i4 * 4, i4 * 4 + 4)], stream="xr%d" % i4)

        def gload(l, name, dst, key):
            o, s = GV[name]
            dma("sp", dst, D["gvec"][l, 0:1, o:o + s].partition_broadcast(128), writes=[key])

        small = {}

        def rms_rstd(x_ap, n, xkey, junk, junk_key, st, st_key, col):
            I("act", "activation", dict(out=junk, in_=x_ap, func=AF.Square, accum_out=st[:, col:col + 1]),
                 reads=[xkey], writes=[junk_key, st_key])
            I("act", "activation", dict(out=st[:, col + 1:col + 2], in_=st[:, col:col + 1], func=AF.Sqrt,
                                               scale=1.0 / n, bias=EPS), reads=[st_key], writes=[st_key])
            I("dve", "reciprocal", dict(out=st[:, col + 2:col + 3], in_=st[:, col + 1:col + 2]),
                 reads=[st_key], writes=[st_key])
            return st[:, col + 2:col + 3]

        def transposes(src, src_key, n, dst, dst_key, psb, evac="act"):
            pst = ps[psb][:, :].bitcast(BF16)
            for c in range(n):
                I("pe", "transpose", dict(out=pst[:, c * 128:(c + 1) * 128], in_=src[:, c * 128:(c + 1) * 128], identity=identb[:]),
                     reads=[src_key, "identf"], writes=["ps%d" % psb])
            v = pst[:, 0:n * 128].rearrange("p (a b) -> p a b", b=128)
            if evac == "act":
                I("act", "activation", dict(out=dst, in_=v, func=AF.Copy), reads=["ps%d" % psb], writes=[dst_key])
            else:
                I("dve", "tensor_copy", dict(out=dst, in_=v), reads=["ps%d" % psb], writes=[dst_key])

        def norm_to_hT(x_ap, xkey, g_tile, g_key, h, h_key, hT, hT_key, junk, junk_key, st, st_key, psb):
            rstd = rms_rstd(x_ap, 1024, xkey, junk, junk_key, st, st_key, 0)
            I("dve", "scalar_tensor_tensor", dict(out=h, in0=x_ap, scalar=rstd, in1=g_tile, op0=ALU.mult, op1=ALU.mult),
                 reads=[xkey, st_key, g_key], writes=[h_key])
            transposes(h, h_key, 8, hT, hT_key, psb)

        def wload(dst, src, key, nsplit=1):
            kc = dst.shape[1]
            v = src.rearrange("(kc p) n -> p kc n", p=128)
            step = (kc + nsplit - 1) // nsplit
            for a in range(0, kc, step):
                b = min(kc, a + step)
                dma("pool", dst[:, a:b, :], v[:, a:b, :], writes=[key])

        for l in range(nl):
            AR.off = 0
            q_tok, K_qtok = AR.alloc([16, 512], BF16, "q_tok")
            a_tok, K_atok = AR.alloc([16, 512], BF16, "a_tok")
            gates, K_gates = AR.alloc([16, 24], F32, "gates")
            mark_kv = AR.off
            W_own, K_Wown = AR.alloc([8, 1560], BF16, "W_own")
            sgw, K_sgw = AR.alloc([8, 128], BF16, "sgw")
            sgwf, K_sgwf = AR.alloc([8, 128], BF16, "sgwf")
            sgb, K_sgb = AR.alloc([8], F32, "sgb")
            gmix, K_gmix = AR.alloc([1024], F32, "gmix")
            lng, K_lng = AR.alloc([512], F32, "lng")
            lnb, K_lnb = AR.alloc([512], F32, "lnb")
            mog0, K_mog0 = AR.alloc([512], F32, "mog0")
            qg, K_qg = AR.alloc([512], F32, "qg")
            hs = Rot([AR.alloc([1024], BF16, "h") for _ in range(2)])
            hTs = Rot([AR.alloc([8, 128], BF16, "hT") for _ in range(2)])
            junks = Rot([AR.alloc([1024], F32, "junk") for _ in range(2)])
            sts = Rot([AR.alloc([32], F32, "st") for _ in range(2)])
            ugs = Rot([AR.alloc([512], BF16, "ug") for _ in range(2)])
            vgs = Rot([AR.alloc([512], F32, "vg") for _ in range(2)])
            vns = Rot([AR.alloc([512], BF16, "vn") for _ in range(2)])
            t1s = Rot([AR.alloc([512], F32, "t1") for _ in range(2)])
            t2s = Rot([AR.alloc([512], F32, "t2") for _ in range(2)])
            bns = Rot([AR.alloc([16], F32, "bn") for _ in range(2)])

            wload(W_own, D["w_in_p"][l, :, 0:1560], K_Wown, nsplit=4)
            dma("pool", sgwf, D["sg_wT"][l], writes=[K_sgwf])
            dma("sp", sgb, D["sg_bT"][l], writes=[K_sgb])
            gload(l, "mixg", gmix, K_gmix); gload(l, "lng", lng, K_lng); gload(l, "lnb", lnb, K_lnb)
            gload(l, "mog0", mog0, K_mog0); gload(l, "qg", qg, K_qg)
            I("dve", "tensor_tensor", dict(out=sgw, in0=sgwf, in1=trisg[:].unsqueeze(1).to_broadcast([128, 8, 128]), op=ALU.mult),
                 reads=[K_sgwf, "trisg"], writes=[K_sgw])

            for i in range(16):
                xk = "x%d" % i
                h, hk = hs.next(); hT, hTk = hTs.next(); junk, jk = junks.next(); st, stk = sts.next()
                pT = 0 if i % 2 == 0 else 7
                norm_to_hT(x_res[:, i, :], xk, gmix, K_gmix, h, hk, hT, hTk, junk, jk, st, stk, pT)
                gq = i // 4
                dma("sp", hx_in[gq][(i % 4) * 128:(i % 4 + 1) * 128, :], h, reads=[hk], writes=["hxin%d" % gq], stream="hxs%d" % gq)
                if i % 4 == 3:
                    P.op("pool", lambda e, gq=gq: e.collective_compute("AllGather", ALU.bypass, replica_groups=RG,
                                                                         ins=[hx_in[gq].ap().opt()], outs=[hx_out[gq].ap().opt()]),
                         reads=["hxin%d" % gq], writes=["hxout%d" % gq], dma="cc%d" % gq, inc=1)
                pb = [1, 2, 3] if i % 2 == 0 else [4, 5, 6]
                for ci, (c0, c1) in enumerate(((0, 512), (512, 1024), (1024, 1536))):
                    for kc in range(8):
                        I("pe", "matmul", dict(out=ps[pb[ci]][:, :], lhsT=hT[:, kc, :], rhs=W_own[:, kc, c0:c1],
                                                                                  start=(kc == 0), stop=(kc == 7)),
                             reads=[hTk, K_Wown], writes=["ps%d" % pb[ci]])
                ug, ugk = ugs.next(); vg, vgk = vgs.next(); vn, vnk = vns.next()
                t1, t1k = t1s.next(); t2, t2k = t2s.next(); bn, bnk = bns.next()
                if i == 0:
                    dump("h0", h, [128, 1024], [hk]); dump("hT0", hT, [128, 8, 128], [hTk]); dump("st0", st, [128, 32], [stk])
                    dump("gmix", gmix, [128, 1024], [K_gmix]); dump("W0", W_own[:, 0, :], [128, 1560], [K_Wown])
                I("act", "activation", dict(out=ug, in_=ps[pb[0]][:, :], func=AF.Gelu_apprx_tanh), reads=["ps%d" % pb[0]], writes=[ugk])
                I("act", "activation", dict(out=vg, in_=ps[pb[1]][:, :], func=AF.Gelu_apprx_tanh), reads=["ps%d" % pb[1]], writes=[vgk])
                if i == 0:
                    dump("ug0", ug, [128, 512], [ugk]); dump("vg0", vg, [128, 512], [vgk])
                I("act", "activation", dict(out=t1, in_=ps[pb[2]][:, :], func=AF.Square), reads=["ps%d" % pb[2]], writes=[t1k])
                I("dve", "tensor_reduce", dict(out=st[:, 8:16], in_=t1.rearrange("p (a b) -> p a b", b=64), axis=AX.X, op=ALU.add),
                     reads=[t1k], writes=[stk + "q"])
                I("act", "activation", dict(out=st[:, 16:24], in_=st[:, 8:16], func=AF.Sqrt, scale=1.0 / 64, bias=EPS),
                     reads=[stk + "q"], writes=[stk + "q"])
                I("dve", "reciprocal", dict(out=st[:, 24:32], in_=st[:, 16:24]), reads=[stk + "q"], writes=[stk + "q"])
                I("dve", "tensor_tensor", dict(out=t2.rearrange("p (a b) -> p a b", b=64),
                                                              in0=ps[pb[2]][:, :].rearrange("p (a b) -> p a b", b=64),
                                                              in1=st[:, 24:32].unsqueeze(2).to_broadcast([128, 8, 64]), op=ALU.mult),
                     reads=["ps%d" % pb[2], stk + "q"], writes=[t2k])
                I("dve", "tensor_tensor", dict(out=q_tok[:, i, :], in0=t2, in1=qg, op=ALU.mult),
                     reads=[t2k, K_qg], writes=[K_qtok + "_%d" % i])
                for kc in range(8):
                    I("pe", "matmul", dict(out=ps[pb[2]][:, 0:24], lhsT=hT[:, kc, :], rhs=W_own[:, kc, 1536:1560],
                                                                  start=(kc == 0), stop=(kc == 7)),
                         reads=[hTk, K_Wown], writes=["ps%d" % pb[2]])
                I("act", "activation", dict(out=gates[:, i, :], in_=ps[pb[2]][:, 0:24], func=AF.Sigmoid),
                     reads=["ps%d" % pb[2]], writes=[K_gates + "_%d" % i])
                I("dve", "bn_stats", dict(out=bn[:, 0:6], in_=vg), reads=[vgk], writes=[bnk])
                I("dve", "bn_aggr", dict(out=bn[:, 8:10], in_=bn[:, 0:6]), reads=[bnk], writes=[bnk])
                I("act", "activation", dict(out=bn[:, 10:11], in_=bn[:, 9:10], func=AF.Sqrt, scale=1.0, bias=EPS), reads=[bnk], writes=[bnk])
                I("dve", "reciprocal", dict(out=bn[:, 11:12], in_=bn[:, 10:11]), reads=[bnk], writes=[bnk])
                I("dve", "tensor_scalar", dict(out=t1, in0=vg, scalar1=bn[:, 8:9], scalar2=bn[:, 11:12], op0=ALU.subtract, op1=ALU.mult),
                     reads=[vgk, bnk], writes=[t1k])
                I("dve", "tensor_tensor", dict(out=t1, in0=t1, in1=lng, op=ALU.mult), reads=[t1k, K_lng], writes=[t1k])
                I("dve", "tensor_tensor", dict(out=vn, in0=t1, in1=lnb, op=ALU.add), reads=[t1k, K_lnb], writes=[vnk])
                for g in range(8):
                    I("pe", "matmul", dict(out=ps[pb[1]][:, g * 64:(g + 1) * 64], lhsT=sgw[:, g, :], rhs=vn[:, g * 64:(g + 1) * 64],
                                                               start=True, stop=True),
                         reads=[vnk, K_sgw], writes=["ps%d" % pb[1]])
                I("dve", "tensor_tensor", dict(out=t2.rearrange("p (a b) -> p a b", b=64),
                                                              in0=ps[pb[1]][:, :].rearrange("p (a b) -> p a b", b=64),
                                                              in1=sgb.unsqueeze(2).to_broadcast([128, 8, 64]), op=ALU.add),
                     reads=["ps%d" % pb[1], K_sgb], writes=[t2k])
                I("dve", "tensor_tensor", dict(out=t2, in0=t2, in1=ug, op=ALU.mult), reads=[t2k, ugk], writes=[t2k])
                I("act", "activation", dict(out=t1, in_=t2, func=AF.Square, accum_out=bn[:, 12:13]), reads=[t2k], writes=[t1k, bnk])
                I("act", "activation", dict(out=bn[:, 13:14], in_=bn[:, 12:13], func=AF.Sqrt, scale=1.0 / 512, bias=EPS), reads=[bnk], writes=[bnk])
                I("dve", "reciprocal", dict(out=bn[:, 14:15], in_=bn[:, 13:14]), reads=[bnk], writes=[bnk])
                I("dve", "scalar_tensor_tensor", dict(out=a_tok[:, i, :], in0=t2, scalar=bn[:, 14:15], in1=mog0, op0=ALU.mult, op1=ALU.mult),
                     reads=[t2k, bnk, K_mog0], writes=[K_atok + "_%d" % i])
            dump("q_tok", q_tok, [128, 16, 512], [K_qtok + "_%d" % i for i in range(16)])
            dump("a_tok", a_tok, [128, 16, 512], [K_atok + "_%d" % i for i in range(16)])
            dump("gates", gates, [128, 16, 24], [K_gates + "_%d" % i for i in range(16)])
            if stop in ("P1", "%d:P1" % l):
                break
            P.barrier()
            AR.off = mark_kv
            KE, K_KE = AR.alloc([2, 4096], BF16, "KE")
            KW, K_KW = AR.alloc([4096], BF16, "KW")
            VS, K_VS = AR.alloc([32, 2, 65], BF16, "VS")
            VW, K_VW = AR.alloc([32, 2, 65], BF16, "VW")
            kcT, K_kcT = AR.alloc([2, 256], BF16, "kcT")
            vcov, K_vcov = AR.alloc([2, 2, 128], BF16, "vcov")
            mark_tr = AR.off
            cmpTk, K_cTk = AR.alloc([4096], BF16, "cmpTk")
            cmpTv, K_cTv = AR.alloc([4096], BF16, "cmpTv")
            mark_p3 = AR.off
            W_kv, K_Wkv = AR.alloc([8, 768], BF16, "W_kv")
            kg, K_kg = AR.alloc([256], F32, "kg")
            hs = Rot([AR.alloc([1024], BF16, "h") for _ in range(3)])
            hTs = Rot([AR.alloc([8, 128], BF16, "hT") for _ in range(2)])
            sts = Rot([AR.alloc([32], F32, "st") for _ in range(2)])
            kns = Rot([AR.alloc([256], BF16, "kn") for _ in range(2)])
            cbs = Rot([AR.alloc([256], BF16, "cb") for _ in range(2)])
            t1s = Rot([AR.alloc([256], F32, "t1") for _ in range(1)])
            t2s = t1s

            SK = os.environ.get("SKIP", "") if l == 0 else os.environ.get("SKIP1", "")
            wload(W_kv, D["w_in_p"][l, :, 1560:2328], K_Wkv, nsplit=2)
            gload(l, "kg", kg, K_kg)
            for hh in range(2):
                if "d" not in SK:
                    dma("pool", KE[64:128, hh, :], D["Econst"], writes=[K_KE + "E%d" % hh])
                if "e" not in SK:
                    dma("pool", vcov[:, :, hh, 64:128], D["ovc"], writes=[K_vcov + "ov%d" % hh])
            if "a" not in SK:
                I("dve", "memset", dict(ap=VS[:, :, :, 64:65], constant=1.0), writes=[K_VS + "one"])
                I("dve", "memset", dict(ap=VW[:, :, :, 64:65], constant=1.0), writes=[K_VW + "one"])
            if "f" not in SK:
                I("dve", "memset", dict(ap=kcT[0:64, :, :], constant=0.0), writes=[K_kcT])
                I("dve", "memset", dict(ap=vcov[:, :, :, 0:64], constant=0.0), writes=[K_vcov])

            for G in range(int(os.environ.get("NG", "32")) if l == 0 else int(os.environ.get("NG1", "32"))):
                h, hk = hs.next(); hT, hTk = hTs.next(); st, stk = sts.next()
                kn, knk = kns.next(); cb, cbk = cbs.next(); t1, t1k = t1s.next(); t2, t2k = t2s.next()
                io, rk = G // 2, G % 2
                dma("sp", h, hx_out[io // 4][rk * 512 + (io % 4) * 128: rk * 512 + (io % 4 + 1) * 128, :], reads=["hxout%d" % (io // 4)], writes=[hk],
                    stream="hin%d" % (G % 3))
                pT, pA, pB, pC = (0, 1, 2, 3) if G % 2 == 0 else (7, 4, 5, 6)
                transposes(h, hk, 8, hT, hTk, pT)
                if "g" in SK:
                    continue
                for kc in range(8):
                    I("pe", "matmul", dict(out=ps[pA][:, :], lhsT=hT[:, kc, :], rhs=W_kv[:, kc, 0:512], start=(kc == 0), stop=(kc == 7)),
                      reads=[hTk, K_Wkv], writes=["ps%d" % pA])
                for kc in range(8):
                    I("pe", "matmul", dict(out=ps[pB][:, 0:256], lhsT=hT[:, kc, :], rhs=W_kv[:, kc, 512:768], start=(kc == 0), stop=(kc == 7)),
                      reads=[hTk, K_Wkv], writes=["ps%d" % pB])
                I("act", "activation", dict(out=t1, in_=ps[pA][:, 0:256], func=AF.Square), reads=["ps%d" % pA], writes=[t1k])
                I("dve", "tensor_reduce", dict(out=st[:, 8:12], in_=t1.rearrange("p (a b) -> p a b", b=64), axis=AX.X, op=ALU.add),
                  reads=[t1k], writes=[stk + "q"])
                I("act", "activation", dict(out=st[:, 12:16], in_=st[:, 8:12], func=AF.Sqrt, scale=1.0 / 64, bias=EPS), reads=[stk + "q"], writes=[stk + "q"])
                I("dve", "reciprocal", dict(out=st[:, 16:20], in_=st[:, 12:16]), reads=[stk + "q"], writes=[stk + "q"])
                I("dve", "tensor_tensor", dict(out=t2.rearrange("p (a b) -> p a b", b=64), in0=ps[pA][:, 0:256].rearrange("p (a b) -> p a b", b=64),
                                               in1=st[:, 16:20].unsqueeze(2).to_broadcast([128, 4, 64]), op=ALU.mult),
                  reads=["ps%d" % pA, stk + "q"], writes=[t2k])
                I("dve", "tensor_tensor", dict(out=kn, in0=t2, in1=kg, op=ALU.mult), reads=[t2k, K_kg], writes=[knk])
                if "b" not in SK:
                    I("act", "activation", dict(out=VS[:, G, :, 0:64], in_=ps[pA][:, 256:384].rearrange("p (a b) -> p a b", b=64), func=AF.Copy),
                      reads=["ps%d" % pA], writes=[K_VS + "_%d" % G])
                    I("act", "activation", dict(out=VW[:, G, :, 0:64], in_=ps[pA][:, 384:512].rearrange("p (a b) -> p a b", b=64), func=AF.Copy),
                      reads=["ps%d" % pA], writes=[K_VW + "_%d" % G])
                I("act", "activation", dict(out=cb, in_=ps[pB][:, 0:256], func=AF.Copy), reads=["ps%d" % pB], writes=[cbk])
                if "h" in SK:
                    continue
                pst = ps[pC][:, :].bitcast(BF16)
                for c in range(2):
                    I("pe", "transpose", dict(out=pst[:, c * 128:(c + 1) * 128], in_=kn[:, c * 128:(c + 1) * 128], identity=identb[:]),
                      reads=[knk, "identf"], writes=["ps%d" % pC])
                for c in range(2):
                    I("pe", "transpose", dict(out=pst[:, (2 + c) * 128:(3 + c) * 128], in_=cb[:, c * 128:(c + 1) * 128], identity=identb[:]),
                      reads=[cbk, "identf"], writes=["ps%d" % pC])
                tl = slice(G * 128, (G + 1) * 128)
                if "i" in SK:
                    continue
                I("dve", "tensor_copy", dict(out=KE[0:64, 0, tl], in_=pst[0:64, 0:128]), reads=["ps%d" % pC], writes=[K_KE + "_%d" % G])
                if "c" not in SK:
                    I("dve", "tensor_copy", dict(out=KE[0:64, 1, tl], in_=pst[64:128, 0:128]), reads=["ps%d" % pC], writes=[K_KE + "_%d" % G])
                if "j" in SK:
                    continue
                if "k" not in SK:
                    I("dve", "tensor_copy", dict(out=KW[:, tl], in_=pst[:, 128:256]), reads=["ps%d" % pC], writes=[K_KW + "_%d" % G])
                if "m" not in SK:
                    I("dve", "tensor_copy", dict(out=cmpTk[:, tl], in_=pst[:, 256:384]), reads=["ps%d" % pC], writes=[K_cTk])
                if "n" not in SK:
                    I("dve", "tensor_copy", dict(out=cmpTv[:, tl], in_=pst[:, 384:512]), reads=["ps%d" % pC], writes=[K_cTv])
            dump("KE", KE, [128, 2, 4096], [K_KE + "_%d" % G for G in range(32)] + [K_KE + "E0", K_KE + "E1"])
            dump("KW", KW, [128, 4096], [K_KW + "_%d" % G for G in range(32)])
            dump("VS", VS, [128, 32, 2, 65], [K_VS + "_%d" % G for G in range(32)] + [K_VS + "one"])
            dump("VW", VW, [128, 32, 2, 65], [K_VW + "_%d" % G for G in range(32)] + [K_VW + "one"])
            if stop in ("P2", "%d:P2" % l):
                break
            P.barrier()
            AR.off = mark_p3
            w1, K_w1 = AR.alloc([32, 256], BF16, "w1")
            posT, K_posT = AR.alloc([32], BF16, "posT")
            b1T, K_b1T = AR.alloc([2], F32, "b1T")
            c1b, K_c1b = AR.alloc([2], F32, "c1b")
            w2, K_w2 = AR.alloc([2, 64], BF16, "w2")
            h1T, K_h1T = AR.alloc([2, 2, 256], BF16, "h1T")
            b2t, K_b2t = AR.alloc([64], F32, "b2t")
            kcgt, K_kcgt = AR.alloc([64], F32, "kcgt")
            t1s = Rot([AR.alloc([64], F32, "t1") for _ in range(2)])
            t2s = Rot([AR.alloc([64], BF16, "t2") for _ in range(2)])
            sts = Rot([AR.alloc([8], F32, "st") for _ in range(2)])
            junk, jk = AR.alloc([64], F32, "junk")
            gload(l, "kcg", kcgt, K_kcgt)
            I("dve", "memset", dict(ap=h1T, constant=0.0), writes=[K_h1T])
            bank = Rot([0, 1, 2, 3, 4, 5, 6, 7])
            for kvi in range(2):
                cT, cTk = (cmpTk, K_cTk) if kvi == 0 else (cmpTv, K_cTv)
                w1src = D["cmp_w1"][l, kvi].rearrange("(s d) j -> d s j", d=64)
                for cp in range(2):
                    for s4 in range(4):
                        dma("pool", w1[cp * 64:(cp + 1) * 64, s4 * 8:(s4 + 1) * 8, :], w1src[:, s4 * 8:(s4 + 1) * 8, :], writes=[K_w1])
                    dma("pool", posT[cp * 64:(cp + 1) * 64, :], D["cmp_posT"][l, kvi], writes=[K_posT])
                dma("sp", b1T, D["cmp_b1T"][l, kvi], writes=[K_b1T])
                dma("pool", w2, D["cmp_w2"][l, kvi].rearrange("(jc j) d -> j jc d", j=128), writes=[K_w2])
                gload(l, "b2k" if kvi == 0 else "b2v", b2t, K_b2t)
                for jc in range(2):
                    b = bank.next()
                    for s in range(32):
                        I("pe", "matmul", dict(out=ps[b][:, 0:1], lhsT=w1[0:64, s, jc * 128:(jc + 1) * 128], rhs=posT[0:64, s:s + 1],
                                               start=(s == 0), stop=(s == 31)), reads=[K_w1, K_posT], writes=["ps%d" % b])
                    I("dve", "tensor_tensor", dict(out=c1b[:, jc:jc + 1], in0=ps[b][:, 0:1], in1=b1T[:, jc:jc + 1], op=ALU.add),
                      reads=["ps%d" % b, K_b1T], writes=[K_c1b])
                for hh in range(2):
                    for jc in range(2):
                        b = bank.next()
                        for s in range(32):
                            I("pe", "matmul", dict(out=ps[b][:, 0:255], lhsT=w1[hh * 64:(hh + 1) * 64, s, jc * 128:(jc + 1) * 128],
                                                   rhs=cT[hh * 64:(hh + 1) * 64, s:s + 16 * 254 + 1:16], start=(s == 0), stop=(s == 31)),
                              reads=[K_w1, cTk], writes=["ps%d" % b])
                        I("act", "activation", dict(out=h1T[:, hh, jc, 0:255], in_=ps[b][:, 0:255], func=AF.Gelu_apprx_tanh, bias=c1b[:, jc:jc + 1]),
                          reads=["ps%d" % b, K_c1b], writes=[K_h1T])
                for hh in range(2):
                    for nt in range(2):
                        b = bank.next()
                        t1, t1k = t1s.next(); t2, t2k = t2s.next(); st, stk = sts.next()
                        for jc in range(2):
                            I("pe", "matmul", dict(out=ps[b][:, 0:64], lhsT=h1T[:, hh, jc, nt * 128:(nt + 1) * 128], rhs=w2[:, jc, :],
                                                   start=(jc == 0), stop=(jc == 1)), reads=[K_h1T, K_w2], writes=["ps%d" % b])
                        if kvi == 1:
                            I("dve", "tensor_tensor", dict(out=vcov[:, nt, hh, 0:64], in0=ps[b][:, 0:64], in1=b2t, op=ALU.add),
                              reads=["ps%d" % b, K_b2t], writes=[K_vcov])
                        else:
                            I("dve", "tensor_tensor", dict(out=t1, in0=ps[b][:, 0:64], in1=b2t, op=ALU.add), reads=["ps%d" % b, K_b2t], writes=[t1k])
                            rstd = rms_rstd(t1, 64, t1k, junk, jk, st, stk, 0)
                            I("dve", "scalar_tensor_tensor", dict(out=t2, in0=t1, scalar=rstd, in1=kcgt, op0=ALU.mult, op1=ALU.mult),
                              reads=[t1k, stk, K_kcgt], writes=[t2k])
                            b2_ = bank.next()
                            pst = ps[b2_][:, :].bitcast(BF16)
                            I("pe", "transpose", dict(out=pst[0:64, 0:128], in_=t2, identity=identb[:]), reads=[t2k, "identf"], writes=["ps%d" % b2_])
                            I("act", "activation", dict(out=kcT[0:64, hh, nt * 128:(nt + 1) * 128], in_=pst[0:64, 0:128], func=AF.Copy),
                              reads=["ps%d" % b2_], writes=[K_kcT])
            dump("kcT", kcT[0:64, :, :], [64, 2, 256], [K_kcT])
            dump("vcov", vcov, [128, 2, 2, 128], [K_vcov, K_vcov + "ov0", K_vcov + "ov1"])
            if stop in ("P3", "%d:P3" % l):
                break
            P.barrier()
            AR.off = mark_tr
            W_out, K_Wout = AR.alloc([8, 1024], BF16, "W_out")
            mog1, K_mog1 = AR.alloc([512], F32, "mog1")
            QBs = [Rot([AR.alloc([512], BF16, "QB%d" % hh) for _ in range(2)]) for hh in range(2)]
            QW1s = Rot([AR.alloc([512], BF16, "QW1") for _ in range(2)])
            PTs = Rot([AR.alloc([512], BF16, "PT") for _ in range(4)])
            oaccs = Rot([AR.alloc([512], F32, "oacc") for _ in range(2)])
            tmps = Rot([AR.alloc([256], F32, "tmp") for _ in range(2)])
            imp3s = Rot([AR.alloc([256], F32, "imp3") for _ in range(2)])
            smalls = Rot([AR.alloc([256], F32, "small") for _ in range(2)])
            bnegs = Rot([AR.alloc([64], BF16, "bneg") for _ in range(2)])
            bns = Rot([AR.alloc([512], BF16, "b_n") for _ in range(2)])
            mixTs = Rot([AR.alloc([8, 128], BF16, "mixT") for _ in range(2)])
            junk, jk = AR.alloc([512], BF16, "junk")
            sts = Rot([AR.alloc([8], F32, "st") for _ in range(2)])
            wload(W_out, D["w_out"][l], K_Wout, nsplit=4)
            gload(l, "mog1", mog1, K_mog1)
            sbank = Rot([0, 1])
            pst6 = ps[6][:, :].bitcast(BF16)
            NT4 = int(os.environ.get("NT4", "16"))
            for i in range(NT4):
                xk = "x%d" % i
                oacc, oak = oaccs.next()
                QB = []
                for hh in range(2):
                    QB.append(QBs[hh].next())
                QW1, QW1k = QW1s.next()
                for c in range(4):
                    I("pe", "transpose", dict(out=pst6[:, c * 128:(c + 1) * 128], in_=q_tok[:, i, c * 128:(c + 1) * 128], identity=identb[:]),
                      reads=[K_qtok + "_%d" % i, "identf"], writes=["ps6q"])
                for hh in range(2):
                    qb, qbk = QB[hh]
                    qv = qb[0:64, :].rearrange("p (c e t) -> p c e t", c=2, e=2)
                    src_e = pst6[0:64, hh * 256:(hh + 1) * 256].rearrange("p (c t) -> p c t", c=2)
                    src_o = pst6[64:128, hh * 256:(hh + 1) * 256].rearrange("p (c t) -> p c t", c=2)
                    I("dve", "tensor_copy", dict(out=qv[:, :, 0, :], in_=src_e), reads=["ps6q"], writes=[qbk + "Q"])
                    I("dve", "tensor_copy", dict(out=qv[:, :, 1, :], in_=src_o), reads=["ps6q"], writes=[qbk + "Q"])
                    if hh == 1:
                        qv1 = QW1[64:128, :].rearrange("p (c e t) -> p c e t", c=2, e=2)
                        I("dve", "tensor_copy", dict(out=qv1[:, :, 0, :], in_=src_e), reads=["ps6q"], writes=[QW1k])
                        I("dve", "tensor_copy", dict(out=qv1[:, :, 1, :], in_=src_o), reads=["ps6q"], writes=[QW1k])
                gview = gates[:, i, :].rearrange("p (hd r) -> p hd r", r=3)

                def finish_branch(acc_view64, den_ap, r, hh, first, sm, smk, tmp, tmpk, deps):
                    I("dve", "tensor_scalar", dict(out=sm[:, 8:12], in0=den_ap, scalar1=1e-30, scalar2=None, op0=ALU.max),
                      reads=deps, writes=[smk + "f"])
                    I("dve", "reciprocal", dict(out=sm[:, 12:16], in_=sm[:, 8:12]), reads=[smk + "f"], writes=[smk + "f"])
                    I("dve", "tensor_tensor", dict(out=sm[:, 16:20], in0=sm[:, 12:16], in1=gview[:, 4 * hh:4 * hh + 4, r], op=ALU.mult),
                      reads=[smk + "f", K_gates + "_%d" % i], writes=[smk + "f"])
                    ov_ = oacc[:, hh * 256:(hh + 1) * 256].rearrange("p (g d) -> p g d", d=64)
                    cf = sm[:, 16:20].unsqueeze(2).to_broadcast([128, 4, 64])
                    if first:
                        I("dve", "tensor_tensor", dict(out=ov_, in0=acc_view64, in1=cf, op=ALU.mult), reads=deps + [smk + "f"], writes=[oak + "_%d" % hh])
                    else:
                        tv = tmp.rearrange("p (g d) -> p g d", d=64)
                        I("dve", "tensor_tensor", dict(out=tv, in0=acc_view64, in1=cf, op=ALU.mult), reads=deps + [smk + "f"], writes=[tmpk])
                        I("dve", "tensor_tensor", dict(out=ov_, in0=ov_, in1=tv, op=ALU.add), reads=[tmpk, oak + "_%d" % hh], writes=[oak + "_%d" % hh])

                for hh in range(2):
                    qb, qbk = QB[hh]
                    sm, smk = smalls.next(); tmp, tmpk = tmps.next(); imp3, imp3k = imp3s.next(); bneg, bnegk = bnegs.next()
                    ptc = []
                    for nt in range(2):
                        sbk = sbank.next()
                        w0 = nt * 128 - 16 * i + 240
                        I("pe", "matmul", dict(out=ps[sbk][:, :], lhsT=kcT[0:64, hh, nt * 128:(nt + 1) * 128], rhs=qb[0:64, :], start=True, stop=False),
                          reads=[K_kcT, qbk + "Q"], writes=["ps%d" % sbk])
                        I("pe", "matmul", dict(out=ps[sbk][:, :], lhsT=Astat[0:64, w0:w0 + 128], rhs=Cst[0:64, :], start=False, stop=True),
                          reads=["Astat", "Cst"], writes=["ps%d" % sbk])
                        pt, ptk = PTs.next()
                        I("act", "activation", dict(out=pt, in_=ps[sbk][:, :], func=AF.Exp, scale=0.125), reads=["ps%d" % sbk], writes=[ptk])
                        ptc.append((pt, ptk))
                    for g in range(4):
                        for nt in range(2):
                            pt, ptk = ptc[nt]
                            I("pe", "matmul", dict(out=ps[3][:, g * 128:(g + 1) * 128], lhsT=pt[:, g * 128:(g + 1) * 128], rhs=vcov[:, nt, hh, :],
                                                   start=(nt == 0), stop=(nt == 1)), reads=[ptk, K_vcov, K_vcov + "ov%d" % hh], writes=["ps3"])
                    acc3 = ps[3][:, :].rearrange("p (g c) -> p g c", c=128)
                    I("dve", "tensor_reduce", dict(out=sm[:, 0:4], in_=acc3[:, :, 64:128], axis=AX.X, op=ALU.add), reads=["ps3"], writes=[smk + "d"])
                    I("dve", "tensor_scalar", dict(out=sm[:, 4:8], in0=sm[:, 0:4], scalar1=1e-30, scalar2=None, op0=ALU.max), reads=[smk + "d"], writes=[smk + "d"])
                    I("dve", "reciprocal", dict(out=sm[:, 0:4], in_=sm[:, 4:8]), reads=[smk + "d"], writes=[smk + "d"])
                    i3v = imp3.rearrange("p (g j) -> p g j", j=64)
                    I("dve", "tensor_tensor", dict(out=i3v, in0=acc3[:, :, 64:128], in1=sm[:, 0:4].unsqueeze(2).to_broadcast([128, 4, 64]), op=ALU.mult),
                      reads=["ps3", smk + "d"], writes=[imp3k])
                    I("dve", "tensor_reduce", dict(out=sm[:, 64:128], in_=imp3.rearrange("p (g j) -> p j g", j=64), axis=AX.X, op=ALU.add),
                      reads=[imp3k], writes=[smk + "i"])
                    I("dve", "tensor_tensor", dict(out=sm[:, 64:128], in0=sm[:, 64:128], in1=Mvalid[:, i, :], op=ALU.mult), reads=[smk + "i", "Mvalid"], writes=[smk + "i"])
                    I("dve", "tensor_tensor", dict(out=sm[:, 64:128], in0=sm[:, 64:128], in1=Madd[:, i, :], op=ALU.add), reads=[smk + "i", "Madd"], writes=[smk + "i"])
                    I("dve", "max", dict(out=sm[:, 32:40], in_=sm[:, 64:128]), reads=[smk + "i"], writes=[smk + "m"])
                    I("dve", "match_replace", dict(out=sm[:, 128:192], in_to_replace=sm[:, 32:40], in_values=sm[:, 64:128], imm_value=-1e30),
                      reads=[smk + "i", smk + "m"], writes=[smk + "w"])
                    I("dve", "max", dict(out=sm[:, 40:48], in_=sm[:, 128:192]), reads=[smk + "w"], writes=[smk + "m"])
                    I("dve", "tensor_scalar", dict(out=bneg, in0=sm[:, 64:128], scalar1=sm[:, 47:48], scalar2=NEG, op0=ALU.is_lt, op1=ALU.mult),
                      reads=[smk + "i", smk + "m"], writes=[bnegk])
                    if i == 0:
                        dump("impm%d" % hh, sm[:, 64:128], [128, 64], [smk + "i"])
                        dump("bneg%d" % hh, bneg, [128, 64], [bnegk])
                    blk = 4 + hh
                    I("pe", "transpose", dict(out=pst6[0:64, blk * 128:(blk + 1) * 128], in_=bneg, identity=identb[:]), reads=[bnegk, "identf"], writes=["ps6b%d" % hh])
                    I("dve", "tensor_copy", dict(out=qb[64:128, :].rearrange("p (g t) -> p g t", g=4),
                                                 in_=pst6[0:64, blk * 128:(blk + 1) * 128].unsqueeze(1).to_broadcast([64, 4, 128])),
                      reads=["ps6b%d" % hh], writes=[qbk + "B"])
                    finish_branch(acc3[:, :, 0:64], sm[:, 4:8], 0, hh, True, sm, smk, tmp, tmpk, ["ps3", smk + "d"])
                    def run_branch(accb, tiles, qk, mask_of, vsrc, vkeys):
                        I("pe", "matmul", dict(out=ps[accb][:, 0:260], lhsT=zerob[:, 0:128], rhs=zerob[:, 0:260], start=True, stop=False),
                          reads=["zerob"], writes=["ps%d" % accb])
                        pend = None
                        for n_, kt in enumerate(tiles):
                            sbk = sbank.next()
                            lhsT, lreads, rhs, rreads = qk(kt)
                            I("pe", "matmul", dict(out=ps[sbk][:, :], lhsT=lhsT, rhs=rhs, start=True, stop=True), reads=lreads + rreads, writes=["ps%d" % sbk])
                            pt, ptk = PTs.next()
                            I("act", "activation", dict(out=pt, in_=ps[sbk][:, :], func=AF.Exp, scale=0.125), reads=["ps%d" % sbk], writes=[ptk])
                            mk = mask_of(n_, kt)
                            if mk is not None:
                                I("pool", "tensor_tensor", dict(out=pt, in0=pt, in1=mk[0], op=ALU.mult), reads=[ptk, mk[1]], writes=[ptk])
                            if pend is not None:
                                pv(accb, pend, vsrc, vkeys, False)
                            pend = (pt, ptk, kt)
                        pv(accb, pend, vsrc, vkeys, True)

                    def pv(accb, pend, vsrc, vkeys, last):
                        pt, ptk, kt = pend
                        for g in range(4):
                            I("pe", "matmul", dict(out=ps[accb][:, g * 65:(g + 1) * 65], lhsT=pt[:, g * 128:(g + 1) * 128], rhs=vsrc[:, kt, hh, :],
                                                   start=False, stop=(last and g == 3)),
                              reads=[ptk, vkeys[0] + "_%d" % kt, vkeys[0] + "one"], writes=["ps%d" % accb])

                    kts = [(o, 2 * i - 4 + o) for o in range(6) if 2 * i - 4 + o >= 0]
                    o_of = {kt: o for (o, kt) in kts}
                    rq = qb[0:64, :] if hh == 0 else QW1[64:128, :]
                    rqk = (qbk + "Q") if hh == 0 else QW1k

                    def qk_w(kt):
                        return KW[hh * 64:(hh + 1) * 64, kt * 128:(kt + 1) * 128], [K_KW + "_%d" % kt], rq, [rqk]

                    def mask_w(n_, kt):
                        o = o_of[kt]
                        if o in (0, 1, 4, 5):
                            return maskW[:, {0: 0, 1: 1, 4: 2, 5: 3}[o], :], "maskW"
                        return None
                    run_branch(5, [kt for (_, kt) in kts], qk_w, mask_w, VW, [K_VW])
                    acc5 = ps[5][:, 0:260].rearrange("p (g c) -> p g c", c=65)
                    finish_branch(acc5[:, :, 0:64], acc5[:, :, 64], 2, hh, False, sm, smk, tmp, tmpk, ["ps5"])
                    nkt = 2 * i + 2

                    def qk_s(kt):
                        return KE[:, hh, kt * 128:(kt + 1) * 128], [K_KE + "_%d" % kt, K_KE + "E%d" % hh], qb[:, :], [qbk + "Q", qbk + "B"]

                    def mask_s(n_, kt):
                        if kt >= 2 * i:
                            return maskD[:, kt - 2 * i, :], "maskD"
                        return None
                    run_branch(4, list(range(nkt)), qk_s, mask_s, VS, [K_VS])
                    acc4 = ps[4][:, 0:260].rearrange("p (g c) -> p g c", c=65)
                    finish_branch(acc4[:, :, 0:64], acc4[:, :, 64], 1, hh, False, sm, smk, tmp, tmpk, ["ps4"])
                if i == 0 or dbg is not None:
                    pass
                b_n, bnk = bns.next(); mixT, mixTk = mixTs.next(); st, stk = sts.next()
                okeys = [oak + "_0", oak + "_1"]
                I("act", "activation", dict(out=junk, in_=oacc, func=AF.Square, accum_out=st[:, 0:1]), reads=okeys, writes=[jk, stk])
                I("act", "activation", dict(out=st[:, 1:2], in_=st[:, 0:1], func=AF.Sqrt, scale=1.0 / 512, bias=EPS), reads=[stk], writes=[stk])
                I("dve", "reciprocal", dict(out=st[:, 2:3], in_=st[:, 1:2]), reads=[stk], writes=[stk])
                I("dve", "scalar_tensor_tensor", dict(out=b_n, in0=oacc, scalar=st[:, 2:3], in1=mog1, op0=ALU.mult, op1=ALU.mult),
                  reads=okeys + [stk, K_mog1], writes=[bnk])
                if dbg is not None and "b_tok" in dbg:
                    if i == 0:
                        dbg_b = nc.dram_tensor("dbg_b_tok", [128, 16, 512], F32, kind="ExternalOutput").ap()
                    dma("sp", dbg_b[:, i, :], oacc, reads=okeys, stream="dbg")
                pst7 = ps[7][:, :].bitcast(BF16)
                for c in range(4):
                    I("pe", "transpose", dict(out=pst7[:, c * 128:(c + 1) * 128], in_=a_tok[:, i, c * 128:(c + 1) * 128], identity=identb[:]),
                      reads=[K_atok + "_%d" % i, "identf"], writes=["ps7"])
                for c in range(4):
                    I("pe", "transpose", dict(out=pst7[:, (4 + c) * 128:(5 + c) * 128], in_=b_n[:, c * 128:(c + 1) * 128], identity=identb[:]),
                      reads=[bnk, "identf"], writes=["ps7"])
                I("dve", "tensor_copy", dict(out=mixT, in_=pst7[:, :].rearrange("p (a b) -> p a b", b=128)), reads=["ps7"], writes=[mixTk])
                for hf in range(2):
                    yb = 2 if hf == 0 else 7
                    for kc in range(8):
                        I("pe", "matmul", dict(out=ps[yb][:, :], lhsT=mixT[:, kc, :], rhs=W_out[:, kc, hf * 512:(hf + 1) * 512], start=(kc == 0), stop=(kc == 7)),
                          reads=[mixTk, K_Wout], writes=["ps%d" % yb])
                    I("dve", "tensor_tensor", dict(out=x_res[:, i, hf * 512:(hf + 1) * 512], in0=ps[yb][:, :], in1=x_res[:, i, hf * 512:(hf + 1) * 512], op=ALU.add),
                      reads=["ps%d" % yb, xk], writes=[xk])
            dump("x1", x_res[:, :, :], [128, 16, 1024], ["x%d" % i for i in range(16)])
            if stop in ("P4", "%d:P4" % l):
                break
            P.barrier()
            AR.off = 0
            W_mkv, K_Wmkv = AR.alloc([8, 1024], BF16, "W_mkv")
            W_mq, K_Wmq = AR.alloc([8, 512], BF16, "W_mq")
            W_mo, K_Wmo = AR.alloc([4, 1024], BF16, "W_mo")
            gmkv, K_gmkv = AR.alloc([1024], F32, "gmkv")
            gmem, K_gmem = AR.alloc([1024], F32, "gmem")
            mqg, K_mqg = AR.alloc([512], F32, "mqg")
            mkg, K_mkg = AR.alloc([512], F32, "mkg")
            hTm, K_hTm = AR.alloc([8, 256], BF16, "hTm")
            KmT, K_KmT = AR.alloc([4, 256], BF16, "KmT")
            Vm, K_Vm = AR.alloc([2, 4, 129], BF16, "Vm")
            xins = Rot([AR.alloc([1024], F32, "xin") for _ in range(2)])
            hs = Rot([AR.alloc([1024], BF16, "h") for _ in range(2)])
            hTs = Rot([AR.alloc([8, 128], BF16, "hT") for _ in range(2)])
            junk, jk = AR.alloc([1024], BF16, "junk")
            sts = Rot([AR.alloc([32], F32, "st") for _ in range(2)])
            t1s = Rot([AR.alloc([512], F32, "t1") for _ in range(2)])
            qns = Rot([AR.alloc([512], BF16, "qn") for _ in range(2)])
            QmTs = Rot([AR.alloc([4, 128], BF16, "QmT") for _ in range(2)])
            PTs = Rot([AR.alloc([512], BF16, "PTm") for _ in range(4)])
            oms = Rot([AR.alloc([512], BF16, "om") for _ in range(2)])
            omTs = Rot([AR.alloc([4, 128], BF16, "omT") for _ in range(2)])
            wload(W_mkv, D["w_mkv"][l], K_Wmkv, nsplit=4)
            wload(W_mq, D["w_mq"][l], K_Wmq, nsplit=2)
            wload(W_mo, D["w_mo"][l], K_Wmo, nsplit=2)
            gload(l, "mkvg", gmkv, K_gmkv); gload(l, "memg", gmem, K_gmem); gload(l, "mqg", mqg, K_mqg); gload(l, "mkg", mkg, K_mkg)
            I("dve", "memset", dict(ap=Vm[:, :, :, 128:129], constant=1.0), writes=[K_Vm + "one"])

            def head_rms128(psb, gt, gk, st, stk, t1, t1k, out_bf, outk):
                I("act", "activation", dict(out=t1, in_=ps[psb][:, :], func=AF.Square), reads=["ps%d" % psb], writes=[t1k])
                I("dve", "tensor_reduce", dict(out=st[:, 8:12], in_=t1.rearrange("p (a b) -> p a b", b=128), axis=AX.X, op=ALU.add), reads=[t1k], writes=[stk + "q"])
                I("act", "activation", dict(out=st[:, 12:16], in_=st[:, 8:12], func=AF.Sqrt, scale=1.0 / 128, bias=EPS), reads=[stk + "q"], writes=[stk + "q"])
                I("dve", "reciprocal", dict(out=st[:, 16:20], in_=st[:, 12:16]), reads=[stk + "q"], writes=[stk + "q"])
                I("dve", "tensor_tensor", dict(out=t1.rearrange("p (a b) -> p a b", b=128), in0=ps[psb][:, :].rearrange("p (a b) -> p a b", b=128),
                                               in1=st[:, 16:20].unsqueeze(2).to_broadcast([128, 4, 128]), op=ALU.mult), reads=["ps%d" % psb, stk + "q"], writes=[t1k])
                I("dve", "tensor_tensor", dict(out=out_bf, in0=t1, in1=gt, op=ALU.mult), reads=[t1k, gk], writes=[outk])

            for mt in range(2):
                xin, xk_ = xins.next(); h, hk = hs.next(); hT, hTk = hTs.next(); st, stk = sts.next(); t1, t1k = t1s.next(); qn, qnk = qns.next()
                dma("sp", xin, D["mem"][mt * 128:(mt + 1) * 128, :], writes=[xk_], stream="xin%d" % mt)
                norm_to_hT(xin, xk_, gmkv, K_gmkv, h, hk, hT, hTk, junk, jk, st, stk, 6)
                I("dve", "tensor_copy", dict(out=hTm[:, :, mt * 128:(mt + 1) * 128], in_=hT), reads=[hTk], writes=[K_hTm])
                for hf in range(2):
                    for kc in range(8):
                        I("pe", "matmul", dict(out=ps[hf][:, :], lhsT=hTm[:, kc, mt * 128:(mt + 1) * 128], rhs=W_mkv[:, kc, hf * 512:(hf + 1) * 512],
                                               start=(kc == 0), stop=(kc == 7)), reads=[K_hTm, K_Wmkv], writes=["ps%d" % hf])
                head_rms128(0, mkg, K_mkg, st, stk, t1, t1k, qn, qnk)
                pst = ps[7][:, :].bitcast(BF16)
                for c in range(4):
                    I("pe", "transpose", dict(out=pst[:, c * 128:(c + 1) * 128], in_=qn[:, c * 128:(c + 1) * 128], identity=identb[:]), reads=[qnk, "identf"], writes=["ps7"])
                I("dve", "tensor_copy", dict(out=KmT[:, :, mt * 128:(mt + 1) * 128], in_=pst[:, 0:512].rearrange("p (a b) -> p a b", b=128)), reads=["ps7"], writes=[K_KmT])
                I("act", "activation", dict(out=Vm[:, mt, :, 0:128], in_=ps[1][:, :].rearrange("p (a b) -> p a b", b=128), func=AF.Copy), reads=["ps1"], writes=[K_Vm])
            sc_m = 128.0 ** -0.5
            for i in range(16):
                xk = "x%d" % i
                h, hk = hs.next(); hT, hTk = hTs.next(); st, stk = sts.next(); t1, t1k = t1s.next(); qn, qnk = qns.next()
                QmT, QmTk = QmTs.next(); om, omk = oms.next(); omT, omTk = omTs.next()
                norm_to_hT(x_res[:, i, :], xk, gmem, K_gmem, h, hk, hT, hTk, junk, jk, st, stk, 6)
                for kc in range(8):
                    I("pe", "matmul", dict(out=ps[0][:, :], lhsT=hT[:, kc, :], rhs=W_mq[:, kc, :], start=(kc == 0), stop=(kc == 7)), reads=[hTk, K_Wmq], writes=["ps0"])
                head_rms128(0, mqg, K_mqg, st, stk, t1, t1k, qn, qnk)
                pst = ps[7][:, :].bitcast(BF16)
                for c in range(4):
                    I("pe", "transpose", dict(out=pst[:, c * 128:(c + 1) * 128], in_=qn[:, c * 128:(c + 1) * 128], identity=identb[:]), reads=[qnk, "identf"], writes=["ps7"])
                I("dve", "tensor_copy", dict(out=QmT, in_=pst[:, 0:512].rearrange("p (a b) -> p a b", b=128)), reads=["ps7"], writes=[QmTk])
                ptm = []
                for mt in range(2):
                    sbk = 1 + mt
                    for hd in range(4):
                        I("pe", "matmul", dict(out=ps[sbk][:, hd * 128:(hd + 1) * 128], lhsT=KmT[:, hd, mt * 128:(mt + 1) * 128], rhs=QmT[:, hd, :], start=True, stop=True),
                          reads=[K_KmT, QmTk], writes=["ps%d" % sbk])
                    pt, ptk = PTs.next()
                    I("act", "activation", dict(out=pt, in_=ps[sbk][:, :], func=AF.Exp, scale=sc_m), reads=["ps%d" % sbk], writes=[ptk])
                    ptm.append((pt, ptk))
                for hd in range(4):
                    ab = 3 + hd // 2
                    c0 = (hd % 2) * 129
                    for mt in range(2):
                        pt, ptk = ptm[mt]
                        I("pe", "matmul", dict(out=ps[ab][:, c0:c0 + 129], lhsT=pt[:, hd * 128:(hd + 1) * 128], rhs=Vm[:, mt, hd, :], start=(mt == 0), stop=(mt == 1)),
                          reads=[ptk, K_Vm, K_Vm + "one"], writes=["ps%d" % ab])
                for ab in (3, 4):
                    accv = ps[ab][:, 0:258].rearrange("p (a c) -> p a c", c=129)
                    so = 20 + (ab - 3) * 2
                    I("dve", "reciprocal", dict(out=st[:, so:so + 2], in_=accv[:, :, 128]), reads=["ps%d" % ab], writes=[stk + "m%d" % ab])
                    I("dve", "tensor_tensor", dict(out=om[:, (ab - 3) * 256:(ab - 2) * 256].rearrange("p (a c) -> p a c", c=128), in0=accv[:, :, 0:128],
                                                   in1=st[:, so:so + 2].unsqueeze(2).to_broadcast([128, 2, 128]), op=ALU.mult),
                      reads=["ps%d" % ab, stk + "m%d" % ab], writes=[omk])
                pst5 = ps[5][:, :].bitcast(BF16)
                for c in range(4):
                    I("pe", "transpose", dict(out=pst5[:, c * 128:(c + 1) * 128], in_=om[:, c * 128:(c + 1) * 128], identity=identb[:]), reads=[omk, "identf"], writes=["ps5"])
                I("dve", "tensor_copy", dict(out=omT, in_=pst5[:, 0:512].rearrange("p (a b) -> p a b", b=128)), reads=["ps5"], writes=[omTk])
                for hf in range(2):
                    yb = 0 if hf == 0 else 7
                    for kc in range(4):
                        I("pe", "matmul", dict(out=ps[yb][:, :], lhsT=omT[:, kc, :], rhs=W_mo[:, kc, hf * 512:(hf + 1) * 512], start=(kc == 0), stop=(kc == 3)),
                          reads=[omTk, K_Wmo], writes=["ps%d" % yb])
                    I("dve", "tensor_tensor", dict(out=x_res[:, i, hf * 512:(hf + 1) * 512], in0=ps[yb][:, :], in1=x_res[:, i, hf * 512:(hf + 1) * 512], op=ALU.add),
                      reads=["ps%d" % yb, xk], writes=[xk])
            dump("x2", x_res[:, :, :], [128, 16, 1024], ["x%d" % i for i in range(16)])
            if stop in ("P5", "%d:P5" % l):
                break
            P.barrier()
            AR.off = 0
            hTa, K_hTa = AR.alloc([8, 2048], BF16, "hTa")
            hidT, K_hidT = AR.alloc([4, 2048], BF16, "hidT")
            W1gs = Rot([AR.alloc([8, 512], BF16, "W1g") for _ in range(2)])
            W2gs = Rot([AR.alloc([4, 1024], BF16, "W2g") for _ in range(2)])
            rts = Rot([AR.alloc([512], F32, "rt") for _ in range(3)])
            gffn, K_gffn = AR.alloc([1024], F32, "gffn")
            hs = Rot([AR.alloc([1024], BF16, "h") for _ in range(2)])
            junk, jk = AR.alloc([1024], BF16, "junk")
            sts = Rot([AR.alloc([8], F32, "st") for _ in range(2)])
            gload(l, "ffng", gffn, K_gffn)
            w1v = D["w_ff1"][l].rearrange("(kc p) n -> p kc n", p=128)
            w2v = D["w_ff2"][l].rearrange("(jc j) n -> j jc n", j=128)
            wg = {}

            def load_group(grp):
                W1g, W1gk = W1gs.next(); W2g, W2gk = W2gs.next()
                for hf2 in range(2):
                    dma("pool", W1g[:, hf2 * 4:(hf2 + 1) * 4, :], w1v[:, hf2 * 4:(hf2 + 1) * 4, grp * 512:(grp + 1) * 512], writes=[W1gk + "_%d" % hf2],
                        stream="w1g%d_%d" % (grp % 2, hf2))
                    dma("pool", W2g[:, hf2 * 2:(hf2 + 1) * 2, :], w2v[:, grp * 4 + hf2 * 2:grp * 4 + (hf2 + 1) * 2, :], writes=[W2gk + "_%d" % hf2],
                        stream="w2g%d_%d" % (grp % 2, hf2))
                wg[grp] = (W1g, W1gk, W2g, W2gk)
            load_group(0)
            for i in range(16):
                h, hk = hs.next(); st, stk = sts.next()
                pT = 0 if i % 2 == 0 else 7
                rstd = rms_rstd(x_res[:, i, :], 1024, "x%d" % i, junk, jk, st, stk, 0)
                I("dve", "scalar_tensor_tensor", dict(out=h, in0=x_res[:, i, :], scalar=rstd, in1=gffn, op0=ALU.mult, op1=ALU.mult),
                  reads=["x%d" % i, stk, K_gffn], writes=[hk])
                transposes(h, hk, 8, hTa[:, :, i * 128:(i + 1) * 128], K_hTa + "_%d" % i, pT, evac="dve")
            hbank = Rot([1, 2, 3, 4])
            ybank = Rot([5, 6])
            for grp in range(8):
                if grp + 1 < 8:
                    load_group(grp + 1)
                W1g, W1gk, W2g, W2gk = wg[grp]
                for jc in range(4):
                    for tb in range(4):
                        hb = hbank.next()
                        for kc in range(8):
                            I("pe", "matmul", dict(out=ps[hb][:, :], lhsT=W1g[:, kc, jc * 128:(jc + 1) * 128], rhs=hTa[:, kc, tb * 512:(tb + 1) * 512],
                                                   start=(kc == 0), stop=(kc == 7)),
                              reads=[W1gk + "_%d" % (kc // 4)] + [K_hTa + "_%d" % t_ for t_ in range(tb * 4, tb * 4 + 4)], writes=["ps%d" % hb])
                        rt, rtk = rts.next()
                        I("act", "activation", dict(out=rt, in_=ps[hb][:, :], func=AF.Relu), reads=["ps%d" % hb], writes=[rtk])
                        I("pool", "tensor_tensor", dict(out=hidT[:, jc, tb * 512:(tb + 1) * 512], in0=rt, in1=rt, op=ALU.mult), reads=[rtk],
                          writes=[K_hidT + "_%d_%d" % (jc, tb)])
                for i in range(16):
                    for hf in range(2):
                        yb = ybank.next()
                        for jc in range(4):
                            I("pe", "matmul", dict(out=ps[yb][:, :], lhsT=hidT[:, jc, i * 128:(i + 1) * 128], rhs=W2g[:, jc, hf * 512:(hf + 1) * 512],
                                                   start=(jc == 0), stop=(jc == 3)),
                              reads=[K_hidT + "_%d_%d" % (jc, i // 4), W2gk + "_%d" % (jc // 2)], writes=["ps%d" % yb])
                        I("dve", "tensor_tensor", dict(out=x_res[:, i, hf * 512:(hf + 1) * 512], in0=ps[yb][:, :], in1=x_res[:, i, hf * 512:(hf + 1) * 512], op=ALU.add),
                          reads=["ps%d" % yb, "x%d" % i], writes=["x%d" % i])
            P.barrier()

        yo = D["y"].rearrange("(i p) d -> p i d", p=128)
        for i4 in range(4):
            dma("sp", yo[:, i4 * 4:(i4 + 1) * 4, :], x_res[:, i4 * 4:(i4 + 1) * 4, :],
                reads=["x%d" % i for i in range(i4 * 4, i4 * 4 + 4)], stream="yo%d" % i4)
        P.emit()
    return nc, dbg_out


_NC_CACHE = {}


def _assemble(ys):
    out = np.empty((4, 32, 128, 1024), np.float32)
    for c in range(8):
        out[c // 2, (c % 2)::2] = ys[c].reshape(16, 128, 1024)
    return out.reshape(4, 4096, 1024)


def kernel(**inputs):
    inp = {k: np.asarray(v) for k, v in inputs.items()}
    x = np.ascontiguousarray(inp["x"], dtype=np.float32)
    mem = np.ascontiguousarray(inp["mem"], dtype=np.float32)
    consts = [host_consts(h) for h in (0, 1)]
    NL = 2
    if "nc" not in _NC_CACHE:
        _NC_CACHE["nc"] = build(nl=NL)[0]
    nc = _NC_CACHE["nc"]
    ws = [host_layer_weights(inp, l) for l in range(NL)]
    wst = {k: np.stack([w[k] for w in ws]) for k in ws[0]}
    in_maps = []
    for c in range(8):
        b, half = c // 2, c % 2
        m = {}
        xt = x[b].reshape(32, 128, 1024)
        m["x_own"] = np.ascontiguousarray(xt[half::2].reshape(2048, 1024))
        m["mem"] = mem[b]
        m.update(wst)
        m.update(consts[half])
        in_maps.append(m)
    res = run_bass_kernel_spmd(nc, in_maps, core_ids=list(range(8)))
    return _assemble([res.results[c]["y"] for c in range(8)]).astype(np.float32)
```

```python
import contextlib
import os
import numpy as np
import concourse.bass as bass
import concourse.mybir as mybir
from concourse.bass_utils import run_bass_kernel_spmd

F32 = mybir.dt.float32
BF16 = mybir.dt.bfloat16
ALU = mybir.AluOpType
AF = mybir.ActivationFunctionType
AX = mybir.AxisListType

ENGS = ("pe", "act", "dve", "pool", "sp")
EPS = 1e-6
NEG = -30000.0


class Prog:
    def __init__(self, nc):
        self.nc = nc
        self.ops = []
        self.last_writer = {}
        self.readers = {}
        self.barrier_deps = set()
        self.last_eng_op = {}
        self.last_dma_op = {}

    def op(self, eng, fn, reads=(), writes=(), dma=None, inc=None):
        idx = len(self.ops)
        deps = set(self.barrier_deps)
        for r in reads:
            w = self.last_writer.get(r)
            if w is not None:
                deps.add(w)
        for w_ in writes:
            w = self.last_writer.get(w_)
            if w is not None:
                deps.add(w)
            for rd in self.readers.get(w_, ()):
                deps.add(rd)
        if dma is not None and dma in self.last_dma_op:
            deps.add(self.last_dma_op[dma])
        self.ops.append(dict(eng=eng, fn=fn, deps=deps, dma=dma, sig=False, inc=(inc if inc is not None else (16 if dma is not None else 1))))
        for r in reads:
            self.readers.setdefault(r, []).append(idx)
        for w_ in writes:
            self.last_writer[w_] = idx
            self.readers[w_] = []
        if dma is None:
            self.last_eng_op[eng] = idx
        else:
            self.last_dma_op[dma] = idx
        return idx

    def barrier(self):
        self.barrier_deps = set(self.last_eng_op.values()) | set(self.last_dma_op.values())

    def _skip(self, p, o):
        return p["dma"] is None and o["dma"] is None and p["eng"] == o["eng"] and p["eng"] == "pe"

    def emit(self, final_wait_eng="sp"):
        nc = self.nc
        ops = self.ops
        for o in ops:
            for d in o["deps"]:
                p = ops[d]
                if not self._skip(p, o):
                    p["sig"] = True
            if o["dma"] is not None:
                o["sig"] = True
        counters = {}
        for o in ops:
            if not o["sig"]:
                continue
            sname = ("dma:" + o["dma"]) if o["dma"] is not None else ("eng:" + o["eng"])
            counters[sname] = counters.get(sname, 0) + o["inc"]
            o["signal"] = (sname, counters[sname])
        with contextlib.ExitStack() as es:
            sems = {}
            for sn in sorted(counters):
                sems[sn] = es.enter_context(nc.semaphore(sn.replace(":", "_")))
            block = es.enter_context(nc.Block())
            per_eng = {e: [] for e in ENGS}
            for i, o in enumerate(ops):
                per_eng[o["eng"]].append(i)

            def body(ename, engine):
                waited = {}
                for i in per_eng[ename]:
                    o = ops[i]
                    need = {}
                    for d in o["deps"]:
                        p = ops[d]
                        if not p["sig"] or self._skip(p, o):
                            continue
                        sn, val = p["signal"]
                        if need.get(sn, 0) < val:
                            need[sn] = val
                    for sn, val in need.items():
                        if waited.get(sn, 0) < val:
                            engine.wait_ge(sems[sn], val)
                            waited[sn] = val
                    ins = o["fn"](engine)
                    if o["sig"]:
                        sn, val = o["signal"]
                        ins.then_inc(sems[sn], o["inc"])
                if ename == final_wait_eng:
                    for sn, val in counters.items():
                        if sn.startswith("dma:") and waited.get(sn, 0) < val:
                            engine.wait_ge(sems[sn], val)

            regs = dict(pe=block.tensor, act=block.scalar, dve=block.vector, pool=block.gpsimd, sp=block.sync)
            for ename in ENGS:
                if not per_eng[ename] and ename != final_wait_eng:
                    continue

                def mk(en):
                    def f(engine):
                        body(en, engine)
                    return f
                regs[ename](mk(ename))
        return counters


class Rot:
    def __init__(self, items):
        self.items = items
        self.i = 0

    def next(self):
        it = self.items[self.i % len(self.items)]
        self.i += 1
        return it


class Arena:
    def __init__(self, t, nbytes):
        self.t = t
        self.nbytes = nbytes
        self.off = 0
        self.cnt = 0

    def alloc(self, free_shape, dt, name):
        n = int(np.prod(free_shape))
        esz = 4 if dt == F32 else 2
        nb = (n * esz + 63) // 64 * 64
        assert self.off + nb <= self.nbytes, (name, self.off, nb, self.nbytes)
        a = self.off // 2
        v = self.t[:, a:a + (n * esz) // 2]
        if dt == F32:
            v = v.bitcast(F32)
        self.off += nb
        self.cnt += 1
        key = "%s@%d" % (name, self.cnt)
        if len(free_shape) == 2:
            v = v.rearrange("p (a b) -> p a b", b=free_shape[1])
        elif len(free_shape) == 3:
            v = v.rearrange("p (a b c) -> p a b c", b=free_shape[1], c=free_shape[2])
        return v, key


GV = {}
_o = 0
for _n, _s in [("mixg", 1024), ("memg", 1024), ("ffng", 1024), ("mkvg", 1024), ("lng", 512), ("lnb", 512),
               ("mog0", 512), ("mog1", 512), ("qg", 512), ("kg", 256), ("kcg", 64), ("b2k", 64), ("b2v", 64),
               ("mqg", 512), ("mkg", 512)]:
    GV[_n] = (_o, _s)
    _o += _s
NGV = _o


def host_consts(half):
    c = {}
    c["identf"] = np.eye(128, dtype=np.float32)
    k = np.arange(4096)
    c["Econst"] = (k[None, :] // 64 == np.arange(64)[:, None]).astype(np.float32)
    OFF = 240
    cc = np.arange(512) - OFF - 8 * half
    A = np.zeros((64, 512), np.float32)
    C = np.zeros((64, 128), np.float32)
    tl = np.arange(128)
    for j in range(8):
        m = j - 1
        A[j] = (cc == m)
        C[j] = np.where(16 * m + 31 > tl, NEG, 0.0)
    A[8] = (cc >= 7)
    C[8] = NEG
    c["Astat"] = A
    c["Cst"] = np.tile(C, (1, 4)).astype(np.float32)
    kl = np.arange(128)[:, None]
    ql = np.arange(128)[None, :]
    tri = (kl <= ql).astype(np.float32)
    left = (kl > ql).astype(np.float32)
    ones = np.ones((128, 128), np.float32)
    zeros = np.zeros((128, 128), np.float32)
    if half == 0:
        mD = [tri, zeros]
        mW = [left, ones, tri, zeros]
    else:
        mD = [ones, tri]
        mW = [zeros, left, ones, tri]
    c["maskD"] = np.stack([np.tile(m, (1, 4)) for m in mD], 1).astype(np.float32)
    c["maskW"] = np.stack([np.tile(m, (1, 4)) for m in mW], 1).astype(np.float32)
    Mv = np.zeros((128, 16, 64), np.float32)
    Ma = np.zeros((128, 16, 64), np.float32)
    j = np.arange(64)[None, :]
    for i in range(16):
        G = 2 * i + half
        t = 128 * G + np.arange(128)[:, None]
        tb = t // 64
        forced = (j == 0) | (j == tb) | (j == tb - 1)
        invalid = (j > tb)
        Mv[:, i, :] = (~forced & ~invalid)
        Ma[:, i, :] = np.where(forced, 1e4, np.where(invalid, -1e4, 0.0))
    c["Mvalid"] = Mv
    c["Madd"] = Ma
    n = np.arange(256)
    cs = n * 16
    ss = np.arange(64) * 64
    ov = np.clip(np.minimum(cs[:, None] + 32, ss[None, :] + 64) - np.maximum(cs[:, None], ss[None, :]), 0, None) / 32.0
    ov[255] = 0.0
    c["ovc"] = ov.reshape(2, 128, 64).transpose(1, 0, 2).astype(np.float32).copy()
    c["trisg"] = tri.copy()
    return c


def host_layer_weights(inp, l):
    w = {}
    wi = inp["w_in"][l]
    kv = wi[:, 1536:2304]
    kc, vc, ks, vs, kw, vw = [kv[:, i * 128:(i + 1) * 128] for i in range(6)]
    w["w_in_p"] = np.concatenate([wi[:, 0:1536], wi[:, 2304:2328], ks, kw, vs, vw, kc, vc], axis=1)
    gv = np.zeros((1, NGV), np.float32)

    def put(name, v):
        o, s = GV[name]
        gv[0, o:o + s] = np.asarray(v).reshape(-1)
    put("mixg", inp["norm_mix_g"][l]); put("memg", inp["norm_mem_g"][l]); put("ffng", inp["norm_ffn_g"][l])
    put("mkvg", inp["mem_kv_norm_g"][l]); put("lng", inp["sg_ln_g"][l]); put("lnb", inp["sg_ln_b"][l])
    put("mog0", inp["mix_out_g"][l, 0]); put("mog1", inp["mix_out_g"][l, 1])
    put("qg", np.tile(inp["q_norm_g"][l], 8))
    put("kg", np.concatenate([np.tile(inp["k_norm_g"][l, 1], 2), np.tile(inp["k_norm_g"][l, 2], 2)]))
    put("kcg", inp["k_norm_g"][l, 0]); put("b2k", inp["cmp_b2"][l, 0]); put("b2v", inp["cmp_b2"][l, 1])
    put("mqg", np.tile(inp["mem_q_norm_g"][l], 4)); put("mkg", np.tile(inp["mem_k_norm_g"][l], 4))
    w["gvec"] = gv
    w["sg_wT"] = np.ascontiguousarray(inp["sg_w"][l].transpose(2, 0, 1))
    w["sg_bT"] = np.ascontiguousarray(inp["sg_b"][l].T)
    w["cmp_w1"] = np.ascontiguousarray(inp["cmp_w1"][l])
    w["cmp_b1T"] = np.ascontiguousarray(inp["cmp_b1"][l].reshape(2, 2, 128).transpose(0, 2, 1))
    w["cmp_posT"] = np.ascontiguousarray(inp["cmp_pos"][l].transpose(0, 2, 1))
    w["cmp_w2"] = np.ascontiguousarray(inp["cmp_w2"][l])
    for k in ("w_out", "w_mq", "w_mkv", "w_mo", "w_ff1", "w_ff2"):
        w[k] = np.ascontiguousarray(inp[k][l])
    return w


WSHAPES = dict(w_in_p=[1024, 2328], gvec=[1, NGV], sg_wT=[128, 8, 128], sg_bT=[128, 8], cmp_w1=[2, 2048, 256],
               cmp_b1T=[2, 128, 2], cmp_posT=[2, 64, 32], cmp_w2=[2, 256, 64], w_out=[1024, 1024],
               w_mq=[1024, 512], w_mkv=[1024, 1024], w_mo=[512, 1024], w_ff1=[1024, 4096], w_ff2=[4096, 1024])
CSHAPES = dict(identf=[128, 128], Econst=[64, 4096], Astat=[64, 512], Cst=[64, 512], maskD=[128, 2, 512],
               maskW=[128, 4, 512], Mvalid=[128, 16, 64], Madd=[128, 16, 64], ovc=[128, 2, 64], trisg=[128, 128])


def build(nl=1, stop=None, dbg=None, ncores=8):
    nc = bass.Bass("TRN2", target_bir_lowering=False)
    D = {}
    D["x_own"] = nc.dram_tensor("x_own", [2048, 1024], F32, kind="ExternalInput").ap()
    hx_in = [nc.dram_tensor("hxin%d" % g, [512, 1024], BF16) for g in range(4)]
    hx_out = [nc.dram_tensor("hxout%d" % g, [1024, 1024], BF16) for g in range(4)]
    RG = [[2 * p, 2 * p + 1] for p in range(ncores // 2)]
    D["mem"] = nc.dram_tensor("mem", [256, 1024], F32, kind="ExternalInput").ap()
    for k, s in WSHAPES.items():
        D[k] = nc.dram_tensor(k, [nl] + s, F32, kind="ExternalInput").ap()
    for k, s in CSHAPES.items():
        D[k] = nc.dram_tensor(k, s, F32, kind="ExternalInput").ap()
    D["y"] = nc.dram_tensor("y", [2048, 1024], F32, kind="ExternalOutput").ap()
    dbg_out = {}

    with contextlib.ExitStack() as es:
        def sb(name, shape, dt=F32):
            return es.enter_context(nc.sbuf_tensor("s_" + name, shape, dt))

        x_res = sb("x_res", [128, 16, 1024], F32)
        ARB = 128 * 1024
        arena_t = sb("arena", [128, ARB // 2], BF16)
        identb = sb("identb", [128, 128], BF16)
        Astat = sb("Astat", [64, 512], BF16)
        Cst = sb("Cst", [64, 512], BF16)
        maskD = sb("maskD", [128, 2, 512], BF16)
        maskW = sb("maskW", [128, 4, 512], BF16)
        Mvalid = sb("Mvalid", [128, 16, 64], BF16)
        Madd = sb("Madd", [128, 16, 64], BF16)
        trisg = sb("trisg", [128, 128], BF16)
        zerob = sb("zerob", [128, 512], BF16)
        ps = [es.enter_context(nc.psum_tensor("ps%d" % b, [128, 512], F32)) for b in range(8)]

        P = Prog(nc)
        AR = Arena(arena_t, ARB)
        dmac = [0]

        def dma(eng, out, in_, reads=(), writes=(), stream=None):
            if stream is None:
                dmac[0] += 1
                stream = "q%s%d" % (eng, dmac[0] % 12)
            P.op(eng, lambda e, out=out, in_=in_: e.dma_start(out=out, in_=in_), reads=reads, writes=writes, dma=stream)

        def I(eng, meth, kw, reads=(), writes=()):
            P.op(eng, lambda e, kw=kw, meth=meth: getattr(e, meth)(**kw), reads=reads, writes=writes)

        def dump(name, ap, shape, reads):
            if dbg is None or name not in dbg:
                return
            d = nc.dram_tensor("dbg_" + name, shape, F32 if ap.dtype == F32 else BF16, kind="ExternalOutput").ap()
            dbg_out[name] = d
            dma("sp", d, ap, reads=reads, stream="dbg")

        for t, nm in ((identb, "identf"), (Astat, "Astat"), (Cst, "Cst"), (maskD, "maskD"), (maskW, "maskW"),
                      (Mvalid, "Mvalid"), (Madd, "Madd"), (trisg, "trisg")):
            dma("pool", t[:], D[nm], writes=[nm])
        I("dve", "memset", dict(ap=zerob[:], constant=0.0), writes=["zerob"])
        xo = D["x_own"].rearrange("(i p) d -> p i d", p=128)
        for i4 in range(4):
            dma("sp", x_res[:, i4 * 4:(i4 + 1) * 4, :], xo[:, i4 * 4:(i4 + 1) * 4, :],
                writes=["x%d" % i for i in range(i4 * 4, i4 * 4 + 4)], stream="xr%d" % i4)

        def gload(l, name, dst, key):
            o, s = GV[name]
            dma("sp", dst, D["gvec"][l, 0:1, o:o + s].partition_broadcast(128), writes=[key])

        small = {}

        def rms_rstd(x_ap, n, xkey, junk, junk_key, st, st_key, col):
            I("act", "activation", dict(out=junk, in_=x_ap, func=AF.Square, accum_out=st[:, col:col + 1]),
                 reads=[xkey], writes=[junk_key, st_key])
            I("act", "activation", dict(out=st[:, col + 1:col + 2], in_=st[:, col:col + 1], func=AF.Sqrt,
                                               scale=1.0 / n, bias=EPS), reads=[st_key], writes=[st_key])
            I("dve", "reciprocal", dict(out=st[:, col + 2:col + 3], in_=st[:, col + 1:col + 2]),
                 reads=[st_key], writes=[st_key])
            return st[:, col + 2:col + 3]

        def transposes(src, src_key, n, dst, dst_key, psb, evac="act"):
            pst = ps[psb][:, :].bitcast(BF16)
            for c in range(n):
                I("pe", "transpose", dict(out=pst[:, c * 128:(c + 1) * 128], in_=src[:, c * 128:(c + 1) * 128], identity=identb[:]),
                     reads=[src_key, "identf"], writes=["ps%d" % psb])
            v = pst[:, 0:n * 128].rearrange("p (a b) -> p a b", b=128)
            if evac == "act":
                I("act", "activation", dict(out=dst, in_=v, func=AF.Copy), reads=["ps%d" % psb], writes=[dst_key])
            else:
                I("dve", "tensor_copy", dict(out=dst, in_=v), reads=["ps%d" % psb], writes=[dst_key])

        def norm_to_hT(x_ap, xkey, g_tile, g_key, h, h_key, hT, hT_key, junk, junk_key, st, st_key, psb):
            rstd = rms_rstd(x_ap, 1024, xkey, junk, junk_key, st, st_key, 0)
            I("dve", "scalar_tensor_tensor", dict(out=h, in0=x_ap, scalar=rstd, in1=g_tile, op0=ALU.mult, op1=ALU.mult),
                 reads=[xkey, st_key, g_key], writes=[h_key])
            transposes(h, h_key, 8, hT, hT_key, psb)

        def wload(dst, src, key, nsplit=1):
            kc = dst.shape[1]
            v = src.rearrange("(kc p) n -> p kc n", p=128)
            step = (kc + nsplit - 1) // nsplit
            for a in range(0, kc, step):
                b = min(kc, a + step)
                dma("pool", dst[:, a:b, :], v[:, a:b, :], writes=[key])

        for l in range(nl):
            AR.off = 0
            q_tok, K_qtok = AR.alloc([16, 512], BF16, "q_tok")
            a_tok, K_atok = AR.alloc([16, 512], BF16, "a_tok")
            gates, K_gates = AR.alloc([16, 24], F32, "gates")
            mark_kv = AR.off
            W_own, K_Wown = AR.alloc([8, 1560], BF16, "W_own")
            sgw, K_sgw = AR.alloc([8, 128], BF16, "sgw")
            sgwf, K_sgwf = AR.alloc([8, 128], BF16, "sgwf")
            sgb, K_sgb = AR.alloc([8], F32, "sgb")
            gmix, K_gmix = AR.alloc([1024], F32, "gmix")
            lng, K_lng = AR.alloc([512], F32, "lng")
            lnb, K_lnb = AR.alloc([512], F32, "lnb")
            mog0, K_mog0 = AR.alloc([512], F32, "mog0")
            qg, K_qg = AR.alloc([512], F32, "qg")
            hs = Rot([AR.alloc([1024], BF16, "h") for _ in range(2)])
            hTs = Rot([AR.alloc([8, 128], BF16, "hT") for _ in range(2)])
            junks = Rot([AR.alloc([1024], F32, "junk") for _ in range(2)])
            sts = Rot([AR.alloc([32], F32, "st") for _ in range(2)])
            ugs = Rot([AR.alloc([512], BF16, "ug") for _ in range(2)])
            vgs = Rot([AR.alloc([512], F32, "vg") for _ in range(2)])
            vns = Rot([AR.alloc([512], BF16, "vn") for _ in range(2)])
            t1s = Rot([AR.alloc([512], F32, "t1") for _ in range(2)])
            t2s = Rot([AR.alloc([512], F32, "t2") for _ in range(2)])
            bns = Rot([AR.alloc([16], F32, "bn") for _ in range(2)])

            wload(W_own, D["w_in_p"][l, :, 0:1560], K_Wown, nsplit=4)
            dma("pool", sgwf, D["sg_wT"][l], writes=[K_sgwf])
            dma("sp", sgb, D["sg_bT"][l], writes=[K_sgb])
            gload(l, "mixg", gmix, K_gmix); gload(l, "lng", lng, K_lng); gload(l, "lnb", lnb, K_lnb)
            gload(l, "mog0", mog0, K_mog0); gload(l, "qg", qg, K_qg)
            I("dve", "tensor_tensor", dict(out=sgw, in0=sgwf, in1=trisg[:].unsqueeze(1).to_broadcast([128, 8, 128]), op=ALU.mult),
                 reads=[K_sgwf, "trisg"], writes=[K_sgw])

            for i in range(16):
                xk = "x%d" % i
                h, hk = hs.next(); hT, hTk = hTs.next(); junk, jk = junks.next(); st, stk = sts.next()
                pT = 0 if i % 2 == 0 else 7
                norm_to_hT(x_res[:, i, :], xk, gmix, K_gmix, h, hk, hT, hTk, junk, jk, st, stk, pT)
                gq = i // 4
                dma("sp", hx_in[gq][(i % 4) * 128:(i % 4 + 1) * 128, :], h, reads=[hk], writes=["hxin%d" % gq], stream="hxs%d" % gq)
                if i % 4 == 3:
                    P.op("pool", lambda e, gq=gq: e.collective_compute("AllGather", ALU.bypass, replica_groups=RG,
                                                                         ins=[hx_in[gq].ap().opt()], outs=[hx_out[gq].ap().opt()]),
                         reads=["hxin%d" % gq], writes=["hxout%d" % gq], dma="cc%d" % gq, inc=1)
                pb = [1, 2, 3] if i % 2 == 0 else [4, 5, 6]
                for ci, (c0, c1) in enumerate(((0, 512), (512, 1024), (1024, 1536))):
                    for kc in range(8):
                        I("pe", "matmul", dict(out=ps[pb[ci]][:, :], lhsT=hT[:, kc, :], rhs=W_own[:, kc, c0:c1],
                                                                                  start=(kc == 0), stop=(kc == 7)),
                             reads=[hTk, K_Wown], writes=["ps%d" % pb[ci]])
                ug, ugk = ugs.next(); vg, vgk = vgs.next(); vn, vnk = vns.next()
                t1, t1k = t1s.next(); t2, t2k = t2s.next(); bn, bnk = bns.next()
                if i == 0:
                    dump("h0", h, [128, 1024], [hk]); dump("hT0", hT, [128, 8, 128], [hTk]); dump("st0", st, [128, 32], [stk])
                    dump("gmix", gmix, [128, 1024], [K_gmix]); dump("W0", W_own[:, 0, :], [128, 1560], [K_Wown])
                I("act", "activation", dict(out=ug, in_=ps[pb[0]][:, :], func=AF.Gelu_apprx_tanh), reads=["ps%d" % pb[0]], writes=[ugk])
                I("act", "activation", dict(out=vg, in_=ps[pb[1]][:, :], func=AF.Gelu_apprx_tanh), reads=["ps%d" % pb[1]], writes=[vgk])
                if i == 0:
                    dump("ug0", ug, [128, 512], [ugk]); dump("vg0", vg, [128, 512], [vgk])
                I("act", "activation", dict(out=t1, in_=ps[pb[2]][:, :], func=AF.Square), reads=["ps%d" % pb[2]], writes=[t1k])
                I("dve", "tensor_reduce", dict(out=st[:, 8:16], in_=t1.rearrange("p (a b) -> p a b", b=64), axis=AX.X, op=ALU.add),
                     reads=[t1k], writes=[stk + "q"])
                I("act", "activation", dict(out=st[:, 16:24], in_=st[:, 8:16], func=AF.Sqrt, scale=1.0 / 64, bias=EPS),
                     reads=[stk + "q"], writes=[stk + "q"])
                I("dve", "reciprocal", dict(out=st[:, 24:32], in_=st[:, 16:24]), reads=[stk + "q"], writes=[stk + "q"])
                I("dve", "tensor_tensor", dict(out=t2.rearrange("p (a b) -> p a b", b=64),
                                                              in0=ps[pb[2]][:, :].rearrange("p (a b) -> p a b", b=64),
                                                              in1=st[:, 24:32].unsqueeze(2).to_broadcast([128, 8, 64]), op=ALU.mult),
                     reads=["ps%d" % pb[2], stk + "q"], writes=[t2k])
                I("dve", "tensor_tensor", dict(out=q_tok[:, i, :], in0=t2, in1=qg, op=ALU.mult),
                     reads=[t2k, K_qg], writes=[K_qtok + "_%d" % i])
                for kc in range(8):
                    I("pe", "matmul", dict(out=ps[pb[2]][:, 0:24], lhsT=hT[:, kc, :], rhs=W_own[:, kc, 1536:1560],
                                                                  start=(kc == 0), stop=(kc == 7)),
                         reads=[hTk, K_Wown], writes=["ps%d" % pb[2]])
                I("act", "activation", dict(out=gates[:, i, :], in_=ps[pb[2]][:, 0:24], func=AF.Sigmoid),
                     reads=["ps%d" % pb[2]], writes=[K_gates + "_%d" % i])
                I("dve", "bn_stats", dict(out=bn[:, 0:6], in_=vg), reads=[vgk], writes=[bnk])
                I("dve", "bn_aggr", dict(out=bn[:, 8:10], in_=bn[:, 0:6]), reads=[bnk], writes=[bnk])
                I("act", "activation", dict(out=bn[:, 10:11], in_=bn[:, 9:10], func=AF.Sqrt, scale=1.0, bias=EPS), reads=[bnk], writes=[bnk])
                I("dve", "reciprocal", dict(out=bn[:, 11:12], in_=bn[:, 10:11]), reads=[bnk], writes=[bnk])
                I("dve", "tensor_scalar", dict(out=t1, in0=vg, scalar1=bn[:, 8:9], scalar2=bn[:, 11:12], op0=ALU.subtract, op1=ALU.mult),
                     reads=[vgk, bnk], writes=[t1k])
                I("dve", "tensor_tensor", dict(out=t1, in0=t1, in1=lng, op=ALU.mult), reads=[t1k, K_lng], writes=[t1k])
                I("dve", "tensor_tensor", dict(out=vn, in0=t1, in1=lnb, op=ALU.add), reads=[t1k, K_lnb], writes=[vnk])
                for g in range(8):
                    I("pe", "matmul", dict(out=ps[pb[1]][:, g * 64:(g + 1) * 64], lhsT=sgw[:, g, :], rhs=vn[:, g * 64:(g + 1) * 64],
                                                               start=True, stop=True),
                         reads=[vnk, K_sgw], writes=["ps%d" % pb[1]])
                I("dve", "tensor_tensor", dict(out=t2.rearrange("p (a b) -> p a b", b=64),
                                                              in0=ps[pb[1]][:, :].rearrange("p (a b) -> p a b", b=64),
                                                              in1=sgb.unsqueeze(2).to_broadcast([128, 8, 64]), op=ALU.add),
                     reads=["ps%d" % pb[1], K_sgb], writes=[t2k])
                I("dve", "tensor_tensor", dict(out=t2, in0=t2, in1=ug, op=ALU.mult), reads=[t2k, ugk], writes=[t2k])
                I("act", "activation", dict(out=t1, in_=t2, func=AF.Square, accum_out=bn[:, 12:13]), reads=[t2k], writes=[t1k, bnk])
                I("act", "activation", dict(out=bn[:, 13:14], in_=bn[:, 12:13], func=AF.Sqrt, scale=1.0 / 512, bias=EPS), reads=[bnk], writes=[bnk])
                I("dve", "reciprocal", dict(out=bn[:, 14:15], in_=bn[:, 13:14]), reads=[bnk], writes=[bnk])
                I("dve", "scalar_tensor_tensor", dict(out=a_tok[:, i, :], in0=t2, scalar=bn[:, 14:15], in1=mog0, op0=ALU.mult, op1=ALU.mult),
                     reads=[t2k, bnk, K_mog0], writes=[K_atok + "_%d" % i])
            dump("q_tok", q_tok, [128, 16, 512], [K_qtok + "_%d" % i for i in range(16)])
            dump("a_tok", a_tok, [128, 16, 512], [K_atok + "_%d" % i for i in range(16)])
            dump("gates", gates, [128, 16, 24], [K_gates + "_%d" % i for i in range(16)])
            if stop in ("P1", "%d:P1" % l):
                break
            P.barrier()
            AR.off = mark_kv
            KE, K_KE = AR.alloc([2, 4096], BF16, "KE")
            KW, K_KW = AR.alloc([4096], BF16, "KW")
            VS, K_VS = AR.alloc([32, 2, 65], BF16, "VS")
            VW, K_VW = AR.alloc([32, 2, 65], BF16, "VW")
            kcT, K_kcT = AR.alloc([2, 256], BF16, "kcT")
            vcov, K_vcov = AR.alloc([2, 2, 128], BF16, "vcov")
            mark_tr = AR.off
            cmpTk, K_cTk = AR.alloc([4096], BF16, "cmpTk")
            cmpTv, K_cTv = AR.alloc([4096], BF16, "cmpTv")
            mark_p3 = AR.off
            W_kv, K_Wkv = AR.alloc([8, 768], BF16, "W_kv")
            kg, K_kg = AR.alloc([256], F32, "kg")
            hs = Rot([AR.alloc([1024], BF16, "h") for _ in range(3)])
            hTs = Rot([AR.alloc([8, 128], BF16, "hT") for _ in range(2)])
            sts = Rot([AR.alloc([32], F32, "st") for _ in range(2)])
            kns = Rot([AR.alloc([256], BF16, "kn") for _ in range(2)])
            cbs = Rot([AR.alloc([256], BF16, "cb") for _ in range(2)])
            t1s = Rot([AR.alloc([256], F32, "t1") for _ in range(1)])
            t2s = t1s

            SK = os.environ.get("SKIP", "") if l == 0 else os.environ.get("SKIP1", "")
            wload(W_kv, D["w_in_p"][l, :, 1560:2328], K_Wkv, nsplit=2)
            gload(l, "kg", kg, K_kg)
            for hh in range(2):
                if "d" not in SK:
                    dma("pool", KE[64:128, hh, :], D["Econst"], writes=[K_KE + "E%d" % hh])
                if "e" not in SK:
                    dma("pool", vcov[:, :, hh, 64:128], D["ovc"], writes=[K_vcov + "ov%d" % hh])
            if "a" not in SK:
                I("dve", "memset", dict(ap=VS[:, :, :, 64:65], constant=1.0), writes=[K_VS + "one"])
                I("dve", "memset", dict(ap=VW[:, :, :, 64:65], constant=1.0), writes=[K_VW + "one"])
            if "f" not in SK:
                I("dve", "memset", dict(ap=kcT[0:64, :, :], constant=0.0), writes=[K_kcT])
                I("dve", "memset", dict(ap=vcov[:, :, :, 0:64], constant=0.0), writes=[K_vcov])

            for G in range(int(os.environ.get("NG", "32")) if l == 0 else int(os.environ.get("NG1", "32"))):
                h, hk = hs.next(); hT, hTk = hTs.next(); st, stk = sts.next()
                kn, knk = kns.next(); cb, cbk = cbs.next(); t1, t1k = t1s.next(); t2, t2k = t2s.next()
                io, rk = G // 2, G % 2
                dma("sp", h, hx_out[io // 4][rk * 512 + (io % 4) * 128: rk * 512 + (io % 4 + 1) * 128, :], reads=["hxout%d" % (io // 4)], writes=[hk],
                    stream="hin%d" % (G % 3))
                pT, pA, pB, pC = (0, 1, 2, 3) if G % 2 == 0 else (7, 4, 5, 6)
                transposes(h, hk, 8, hT, hTk, pT)
                if "g" in SK:
                    continue
                for kc in range(8):
                    I("pe", "matmul", dict(out=ps[pA][:, :], lhsT=hT[:, kc, :], rhs=W_kv[:, kc, 0:512], start=(kc == 0), stop=(kc == 7)),
                      reads=[hTk, K_Wkv], writes=["ps%d" % pA])
                for kc in range(8):
                    I("pe", "matmul", dict(out=ps[pB][:, 0:256], lhsT=hT[:, kc, :], rhs=W_kv[:, kc, 512:768], start=(kc == 0), stop=(kc == 7)),
                      reads=[hTk, K_Wkv], writes=["ps%d" % pB])
                I("act", "activation", dict(out=t1, in_=ps[pA][:, 0:256], func=AF.Square), reads=["ps%d" % pA], writes=[t1k])
                I("dve", "tensor_reduce", dict(out=st[:, 8:12], in_=t1.rearrange("p (a b) -> p a b", b=64), axis=AX.X, op=ALU.add),
                  reads=[t1k], writes=[stk + "q"])
                I("act", "activation", dict(out=st[:, 12:16], in_=st[:, 8:12], func=AF.Sqrt, scale=1.0 / 64, bias=EPS), reads=[stk + "q"], writes=[stk + "q"])
                I("dve", "reciprocal", dict(out=st[:, 16:20], in_=st[:, 12:16]), reads=[stk + "q"], writes=[stk + "q"])
                I("dve", "tensor_tensor", dict(out=t2.rearrange("p (a b) -> p a b", b=64), in0=ps[pA][:, 0:256].rearrange("p (a b) -> p a b", b=64),
                                               in1=st[:, 16:20].unsqueeze(2).to_broadcast([128, 4, 64]), op=ALU.mult),
                  reads=["ps%d" % pA, stk + "q"], writes=[t2k])
                I("dve", "tensor_tensor", dict(out=kn, in0=t2, in1=kg, op=ALU.mult), reads=[t2k, K_kg], writes=[knk])
                if "b" not in SK:
                    I("act", "activation", dict(out=VS[:, G, :, 0:64], in_=ps[pA][:, 256:384].rearrange("p (a b) -> p a b", b=64), func=AF.Copy),
                      reads=["ps%d" % pA], writes=[K_VS + "_%d" % G])
                    I("act", "activation", dict(out=VW[:, G, :, 0:64], in_=ps[pA][:, 384:512].rearrange("p (a b) -> p a b", b=64), func=AF.Copy),
                      reads=["ps%d" % pA], writes=[K_VW + "_%d" % G])
                I("act", "activation", dict(out=cb, in_=ps[pB][:, 0:256], func=AF.Copy), reads=["ps%d" % pB], writes=[cbk])
                if "h" in SK:
                    continue
                pst = ps[pC][:, :].bitcast(BF16)
                for c in range(2):
                    I("pe", "transpose", dict(out=pst[:, c * 128:(c + 1) * 128], in_=kn[:, c * 128:(c + 1) * 128], identity=identb[:]),
                      reads=[knk, "identf"], writes=["ps%d" % pC])
                for c in range(2):
                    I("pe", "transpose", dict(out=pst[:, (2 + c) * 128:(3 + c) * 128], in_=cb[:, c * 128:(c + 1) * 128], identity=identb[:]),
                      reads=[cbk, "identf"], writes=["ps%d" % pC])
                tl = slice(G * 128, (G + 1) * 128)
                if "i" in SK:
                    continue
                I("dve", "tensor_copy", dict(out=KE[0:64, 0, tl], in_=pst[0:64, 0:128]), reads=["ps%d" % pC], writes=[K_KE + "_%d" % G])
                if "c" not in SK:
                    I("dve", "tensor_copy", dict(out=KE[0:64, 1, tl], in_=pst[64:128, 0:128]), reads=["ps%d" % pC], writes=[K_KE + "_%d" % G])
                if "j" in SK:
                    continue
                if "k" not in SK:
                    I("dve", "tensor_copy", dict(out=KW[:, tl], in_=pst[:, 128:256]), reads=["ps%d" % pC], writes=[K_KW + "_%d" % G])
                if "m" not in SK:
                    I("dve", "tensor_copy", dict(out=cmpTk[:, tl], in_=pst[:, 256:384]), reads=["ps%d" % pC], writes=[K_cTk])
                if "n" not in SK:
                    I("dve", "tensor_copy", dict(out=cmpTv[:, tl], in_=pst[:, 384:512]), reads=["ps%d" % pC], writes=[K_cTv])
            dump("KE", KE, [128, 2, 4096], [K_KE + "_%d" % G for G in range(32)] + [K_KE + "E0", K_KE + "E1"])
            dump("KW", KW, [128, 4096], [K_KW + "_%d" % G for G in range(32)])
            dump("VS", VS, [128, 32, 2, 65], [K_VS + "_%d" % G for G in range(32)] + [K_VS + "one"])
            dump("VW", VW, [128, 32, 2, 65], [K_VW + "_%d" % G for G in range(32)] + [K_VW + "one"])
            if stop in ("P2", "%d:P2" % l):
                break
            P.barrier()
            AR.off = mark_p3
            w1, K_w1 = AR.alloc([32, 256], BF16, "w1")
            posT, K_posT = AR.alloc([32], BF16, "posT")
            b1T, K_b1T = AR.alloc([2], F32, "b1T")
            c1b, K_c1b = AR.alloc([2], F32, "c1b")
            w2, K_w2 = AR.alloc([2, 64], BF16, "w2")
            h1T, K_h1T = AR.alloc([2, 2, 256], BF16, "h1T")
            b2t, K_b2t = AR.alloc([64], F32, "b2t")
            kcgt, K_kcgt = AR.alloc([64], F32, "kcgt")
            t1s = Rot([AR.alloc([64], F32, "t1") for _ in range(2)])
            t2s = Rot([AR.alloc([64], BF16, "t2") for _ in range(2)])
            sts = Rot([AR.alloc([8], F32, "st") for _ in range(2)])
            junk, jk = AR.alloc([64], F32, "junk")
            gload(l, "kcg", kcgt, K_kcgt)
            I("dve", "memset", dict(ap=h1T, constant=0.0), writes=[K_h1T])
            bank = Rot([0, 1, 2, 3, 4, 5, 6, 7])
            for kvi in range(2):
                cT, cTk = (cmpTk, K_cTk) if kvi == 0 else (cmpTv, K_cTv)
                w1src = D["cmp_w1"][l, kvi].rearrange("(s d) j -> d s j", d=64)
                for cp in range(2):
                    for s4 in range(4):
                        dma("pool", w1[cp * 64:(cp + 1) * 64, s4 * 8:(s4 + 1) * 8, :], w1src[:, s4 * 8:(s4 + 1) * 8, :], writes=[K_w1])
                    dma("pool", posT[cp * 64:(cp + 1) * 64, :], D["cmp_posT"][l, kvi], writes=[K_posT])
                dma("sp", b1T, D["cmp_b1T"][l, kvi], writes=[K_b1T])
                dma("pool", w2, D["cmp_w2"][l, kvi].rearrange("(jc j) d -> j jc d", j=128), writes=[K_w2])
                gload(l, "b2k" if kvi == 0 else "b2v", b2t, K_b2t)
                for jc in range(2):
                    b = bank.next()
                    for s in range(32):
                        I("pe", "matmul", dict(out=ps[b][:, 0:1], lhsT=w1[0:64, s, jc * 128:(jc + 1) * 128], rhs=posT[0:64, s:s + 1],
                                               start=(s == 0), stop=(s == 31)), reads=[K_w1, K_posT], writes=["ps%d" % b])
                    I("dve", "tensor_tensor", dict(out=c1b[:, jc:jc + 1], in0=ps[b][:, 0:1], in1=b1T[:, jc:jc + 1], op=ALU.add),
                      reads=["ps%d" % b, K_b1T], writes=[K_c1b])
                for hh in range(2):
                    for jc in range(2):
                        b = bank.next()
                        for s in range(32):
                            I("pe", "matmul", dict(out=ps[b][:, 0:255], lhsT=w1[hh * 64:(hh + 1) * 64, s, jc * 128:(jc + 1) * 128],
                                                   rhs=cT[hh * 64:(hh + 1) * 64, s:s + 16 * 254 + 1:16], start=(s == 0), stop=(s == 31)),
                              reads=[K_w1, cTk], writes=["ps%d" % b])
                        I("act", "activation", dict(out=h1T[:, hh, jc, 0:255], in_=ps[b][:, 0:255], func=AF.Gelu_apprx_tanh, bias=c1b[:, jc:jc + 1]),
                          reads=["ps%d" % b, K_c1b], writes=[K_h1T])
                for hh in range(2):
                    for nt in range(2):
                        b = bank.next()
                        t1, t1k = t1s.next(); t2, t2k = t2s.next(); st, stk = sts.next()
                        for jc in range(2):
                            I("pe", "matmul", dict(out=ps[b][:, 0:64], lhsT=h1T[:, hh, jc, nt * 128:(nt + 1) * 128], rhs=w2[:, jc, :],
                                                   start=(jc == 0), stop=(jc == 1)), reads=[K_h1T, K_w2], writes=["ps%d" % b])
                        if kvi == 1:
                            I("dve", "tensor_tensor", dict(out=vcov[:, nt, hh, 0:64], in0=ps[b][:, 0:64], in1=b2t, op=ALU.add),
                              reads=["ps%d" % b, K_b2t], writes=[K_vcov])
                        else:
                            I("dve", "tensor_tensor", dict(out=t1, in0=ps[b][:, 0:64], in1=b2t, op=ALU.add), reads=["ps%d" % b, K_b2t], writes=[t1k])
                            rstd = rms_rstd(t1, 64, t1k, junk, jk, st, stk, 0)
                            I("dve", "scalar_tensor_tensor", dict(out=t2, in0=t1, scalar=rstd, in1=kcgt, op0=ALU.mult, op1=ALU.mult),
                              reads=[t1k, stk, K_kcgt], writes=[t2k])
                            b2_ = bank.next()
                            pst = ps[b2_][:, :].bitcast(BF16)
                            I("pe", "transpose", dict(out=pst[0:64, 0:128], in_=t2, identity=identb[:]), reads=[t2k, "identf"], writes=["ps%d" % b2_])
                            I("act", "activation", dict(out=kcT[0:64, hh, nt * 128:(nt + 1) * 128], in_=pst[0:64, 0:128], func=AF.Copy),
                              reads=["ps%d" % b2_], writes=[K_kcT])
            dump("kcT", kcT[0:64, :, :], [64, 2, 256], [K_kcT])
            dump("vcov", vcov, [128, 2, 2, 128], [K_vcov, K_vcov + "ov0", K_vcov + "ov1"])
            if stop in ("P3", "%d:P3" % l):
                break
            P.barrier()
            AR.off = mark_tr
            W_out, K_Wout = AR.alloc([8, 1024], BF16, "W_out")
            mog1, K_mog1 = AR.alloc([512], F32, "mog1")
            QBs = [Rot([AR.alloc([512], BF16, "QB%d" % hh) for _ in range(2)]) for hh in range(2)]
            QW1s = Rot([AR.alloc([512], BF16, "QW1") for _ in range(2)])
            PTs = Rot([AR.alloc([512], BF16, "PT") for _ in range(7)])
            oaccs = Rot([AR.alloc([512], F32, "oacc") for _ in range(2)])
            tmps = Rot([AR.alloc([256], F32, "tmp") for _ in range(2)])
            imp3s = Rot([AR.alloc([256], F32, "imp3") for _ in range(2)])
            smalls = Rot([AR.alloc([256], F32, "small") for _ in range(2)])
            bnegs = Rot([AR.alloc([64], BF16, "bneg") for _ in range(2)])
            bns = Rot([AR.alloc([512], BF16, "b_n") for _ in range(2)])
            mixTs = Rot([AR.alloc([8, 128], BF16, "mixT") for _ in range(2)])
            junk, jk = AR.alloc([512], BF16, "junk")
            sts = Rot([AR.alloc([8], F32, "st") for _ in range(2)])
            wload(W_out, D["w_out"][l], K_Wout, nsplit=4)
            gload(l, "mog1", mog1, K_mog1)
            sbank = Rot([0, 1, 2, 7])
            LOOKAHEAD = int(os.environ.get("LOOKAHEAD", "3"))
            pst6 = ps[6][:, :].bitcast(BF16)
            NT4 = int(os.environ.get("NT4", "16"))
            for i in range(NT4):
                xk = "x%d" % i
                oacc, oak = oaccs.next()
                QB = []
                for hh in range(2):
                    QB.append(QBs[hh].next())
                QW1, QW1k = QW1s.next()
                for c in range(4):
                    I("pe", "transpose", dict(out=pst6[:, c * 128:(c + 1) * 128], in_=q_tok[:, i, c * 128:(c + 1) * 128], identity=identb[:]),
                      reads=[K_qtok + "_%d" % i, "identf"], writes=["ps6q"])
                for hh in range(2):
                    qb, qbk = QB[hh]
                    qv = qb[0:64, :].rearrange("p (c e t) -> p c e t", c=2, e=2)
                    src_e = pst6[0:64, hh * 256:(hh + 1) * 256].rearrange("p (c t) -> p c t", c=2)
                    src_o = pst6[64:128, hh * 256:(hh + 1) * 256].rearrange("p (c t) -> p c t", c=2)
                    I("dve", "tensor_copy", dict(out=qv[:, :, 0, :], in_=src_e), reads=["ps6q"], writes=[qbk + "Q"])
                    I("dve", "tensor_copy", dict(out=qv[:, :, 1, :], in_=src_o), reads=["ps6q"], writes=[qbk + "Q"])
                    if hh == 1:
                        qv1 = QW1[64:128, :].rearrange("p (c e t) -> p c e t", c=2, e=2)
                        I("dve", "tensor_copy", dict(out=qv1[:, :, 0, :], in_=src_e), reads=["ps6q"], writes=[QW1k])
                        I("dve", "tensor_copy", dict(out=qv1[:, :, 1, :], in_=src_o), reads=["ps6q"], writes=[QW1k])
                gview = gates[:, i, :].rearrange("p (hd r) -> p hd r", r=3)

                def finish_branch(acc_view64, den_ap, r, hh, first, sm, smk, tmp, tmpk, deps):
                    I("dve", "tensor_scalar", dict(out=sm[:, 8:12], in0=den_ap, scalar1=1e-30, scalar2=None, op0=ALU.max),
                      reads=deps, writes=[smk + "f"])
                    I("dve", "reciprocal", dict(out=sm[:, 12:16], in_=sm[:, 8:12]), reads=[smk + "f"], writes=[smk + "f"])
                    I("dve", "tensor_tensor", dict(out=sm[:, 16:20], in0=sm[:, 12:16], in1=gview[:, 4 * hh:4 * hh + 4, r], op=ALU.mult),
                      reads=[smk + "f", K_gates + "_%d" % i], writes=[smk + "f"])
                    ov_ = oacc[:, hh * 256:(hh + 1) * 256].rearrange("p (g d) -> p g d", d=64)
                    cf = sm[:, 16:20].unsqueeze(2).to_broadcast([128, 4, 64])
                    if first:
                        I("dve", "tensor_tensor", dict(out=ov_, in0=acc_view64, in1=cf, op=ALU.mult), reads=deps + [smk + "f"], writes=[oak + "_%d" % hh])
                    else:
                        tv = tmp.rearrange("p (g d) -> p g d", d=64)
                        I("dve", "tensor_tensor", dict(out=tv, in0=acc_view64, in1=cf, op=ALU.mult), reads=deps + [smk + "f"], writes=[tmpk])
                        I("dve", "tensor_tensor", dict(out=ov_, in0=ov_, in1=tv, op=ALU.add), reads=[tmpk, oak + "_%d" % hh], writes=[oak + "_%d" % hh])

                for hh in range(2):
                    qb, qbk = QB[hh]
                    sm, smk = smalls.next(); tmp, tmpk = tmps.next(); imp3, imp3k = imp3s.next(); bneg, bnegk = bnegs.next()
                    ptc = []
                    for nt in range(2):
                        sbk = sbank.next()
                        w0 = nt * 128 - 16 * i + 240
                        I("pe", "matmul", dict(out=ps[sbk][:, :], lhsT=kcT[0:64, hh, nt * 128:(nt + 1) * 128], rhs=qb[0:64, :], start=True, stop=False),
                          reads=[K_kcT, qbk + "Q"], writes=["ps%d" % sbk])
                        I("pe", "matmul", dict(out=ps[sbk][:, :], lhsT=Astat[0:64, w0:w0 + 128], rhs=Cst[0:64, :], start=False, stop=True),
                          reads=["Astat", "Cst"], writes=["ps%d" % sbk])
                        pt, ptk = PTs.next()
                        I("act", "activation", dict(out=pt, in_=ps[sbk][:, :], func=AF.Exp, scale=0.125), reads=["ps%d" % sbk], writes=[ptk])
                        ptc.append((pt, ptk))
                    for g in range(4):
                        for nt in range(2):
                            pt, ptk = ptc[nt]
                            I("pe", "matmul", dict(out=ps[3][:, g * 128:(g + 1) * 128], lhsT=pt[:, g * 128:(g + 1) * 128], rhs=vcov[:, nt, hh, :],
                                                   start=(nt == 0), stop=(nt == 1)), reads=[ptk, K_vcov, K_vcov + "ov%d" % hh], writes=["ps3"])
                    acc3 = ps[3][:, :].rearrange("p (g c) -> p g c", c=128)
                    I("dve", "tensor_reduce", dict(out=sm[:, 0:4], in_=acc3[:, :, 64:128], axis=AX.X, op=ALU.add), reads=["ps3"], writes=[smk + "d"])
                    I("dve", "tensor_scalar", dict(out=sm[:, 4:8], in0=sm[:, 0:4], scalar1=1e-30, scalar2=None, op0=ALU.max), reads=[smk + "d"], writes=[smk + "d"])
                    I("dve", "reciprocal", dict(out=sm[:, 0:4], in_=sm[:, 4:8]), reads=[smk + "d"], writes=[smk + "d"])
                    i3v = imp3.rearrange("p (g j) -> p g j", j=64)
                    I("dve", "tensor_tensor", dict(out=i3v, in0=acc3[:, :, 64:128], in1=sm[:, 0:4].unsqueeze(2).to_broadcast([128, 4, 64]), op=ALU.mult),
                      reads=["ps3", smk + "d"], writes=[imp3k])
                    I("dve", "tensor_reduce", dict(out=sm[:, 64:128], in_=imp3.rearrange("p (g j) -> p j g", j=64), axis=AX.X, op=ALU.add),
                      reads=[imp3k], writes=[smk + "i"])
                    I("dve", "tensor_tensor", dict(out=sm[:, 64:128], in0=sm[:, 64:128], in1=Mvalid[:, i, :], op=ALU.mult), reads=[smk + "i", "Mvalid"], writes=[smk + "i"])
                    I("dve", "tensor_tensor", dict(out=sm[:, 64:128], in0=sm[:, 64:128], in1=Madd[:, i, :], op=ALU.add), reads=[smk + "i", "Madd"], writes=[smk + "i"])
                    I("dve", "max", dict(out=sm[:, 32:40], in_=sm[:, 64:128]), reads=[smk + "i"], writes=[smk + "m"])
                    I("dve", "match_replace", dict(out=sm[:, 128:192], in_to_replace=sm[:, 32:40], in_values=sm[:, 64:128], imm_value=-1e30),
                      reads=[smk + "i", smk + "m"], writes=[smk + "w"])
                    I("dve", "max", dict(out=sm[:, 40:48], in_=sm[:, 128:192]), reads=[smk + "w"], writes=[smk + "m"])
                    I("dve", "tensor_scalar", dict(out=bneg, in0=sm[:, 64:128], scalar1=sm[:, 47:48], scalar2=NEG, op0=ALU.is_lt, op1=ALU.mult),
                      reads=[smk + "i", smk + "m"], writes=[bnegk])
                    if i == 0:
                        dump("impm%d" % hh, sm[:, 64:128], [128, 64], [smk + "i"])
                        dump("bneg%d" % hh, bneg, [128, 64], [bnegk])
                    blk = 4 + hh
                    I("pe", "transpose", dict(out=pst6[0:64, blk * 128:(blk + 1) * 128], in_=bneg, identity=identb[:]), reads=[bnegk, "identf"], writes=["ps6b%d" % hh])
                    I("dve", "tensor_copy", dict(out=qb[64:128, :].rearrange("p (g t) -> p g t", g=4),
                                                 in_=pst6[0:64, blk * 128:(blk + 1) * 128].unsqueeze(1).to_broadcast([64, 4, 128])),
                      reads=["ps6b%d" % hh], writes=[qbk + "B"])
                    finish_branch(acc3[:, :, 0:64], sm[:, 4:8], 0, hh, True, sm, smk, tmp, tmpk, ["ps3", smk + "d"])
                    def run_branch(accb, tiles, qk, mask_of, vsrc, vkeys):
                        I("pe", "matmul", dict(out=ps[accb][:, 0:260], lhsT=zerob[:, 0:128], rhs=zerob[:, 0:260], start=True, stop=False),
                          reads=["zerob"], writes=["ps%d" % accb])
                        pend = []
                        for n_, kt in enumerate(tiles):
                            sbk = sbank.next()
                            lhsT, lreads, rhs, rreads = qk(kt)
                            I("pe", "matmul", dict(out=ps[sbk][:, :], lhsT=lhsT, rhs=rhs, start=True, stop=True), reads=lreads + rreads, writes=["ps%d" % sbk])
                            pt, ptk = PTs.next()
                            I("act", "activation", dict(out=pt, in_=ps[sbk][:, :], func=AF.Exp, scale=0.125), reads=["ps%d" % sbk], writes=[ptk])
                            mk = mask_of(n_, kt)
                            if mk is not None:
                                I("pool", "tensor_tensor", dict(out=pt, in0=pt, in1=mk[0], op=ALU.mult), reads=[ptk, mk[1]], writes=[ptk])
                            pend.append((pt, ptk, kt))
                            if len(pend) > LOOKAHEAD:
                                pv(accb, pend.pop(0), vsrc, vkeys, False)
                        while pend:
                            pv(accb, pend.pop(0), vsrc, vkeys, len(pend) == 0)

                    def pv(accb, pend, vsrc, vkeys, last):
                        pt, ptk, kt = pend
                        for g in range(4):
                            I("pe", "matmul", dict(out=ps[accb][:, g * 65:(g + 1) * 65], lhsT=pt[:, g * 128:(g + 1) * 128], rhs=vsrc[:, kt, hh, :],
                                                   start=False, stop=(last and g == 3)),
                              reads=[ptk, vkeys[0] + "_%d" % kt, vkeys[0] + "one"], writes=["ps%d" % accb])

                    kts = [(o, 2 * i - 4 + o) for o in range(6) if 2 * i - 4 + o >= 0]
                    o_of = {kt: o for (o, kt) in kts}
                    rq = qb[0:64, :] if hh == 0 else QW1[64:128, :]
                    rqk = (qbk + "Q") if hh == 0 else QW1k

                    def qk_w(kt):
                        return KW[hh * 64:(hh + 1) * 64, kt * 128:(kt + 1) * 128], [K_KW + "_%d" % kt], rq, [rqk]

                    def mask_w(n_, kt):
                        o = o_of[kt]
                        if o in (0, 1, 4, 5):
                            return maskW[:, {0: 0, 1: 1, 4: 2, 5: 3}[o], :], "maskW"
                        return None
                    run_branch(5, [kt for (_, kt) in kts], qk_w, mask_w, VW, [K_VW])
                    acc5 = ps[5][:, 0:260].rearrange("p (g c) -> p g c", c=65)
                    finish_branch(acc5[:, :, 0:64], acc5[:, :, 64], 2, hh, False, sm, smk, tmp, tmpk, ["ps5"])
                    nkt = 2 * i + 2

                    def qk_s(kt):
                        return KE[:, hh, kt * 128:(kt + 1) * 128], [K_KE + "_%d" % kt, K_KE + "E%d" % hh], qb[:, :], [qbk + "Q", qbk + "B"]

                    def mask_s(n_, kt):
                        if kt >= 2 * i:
                            return maskD[:, kt - 2 * i, :], "maskD"
                        return None
                    run_branch(4, list(range(nkt)), qk_s, mask_s, VS, [K_VS])
                    acc4 = ps[4][:, 0:260].rearrange("p (g c) -> p g c", c=65)
                    finish_branch(acc4[:, :, 0:64], acc4[:, :, 64], 1, hh, False, sm, smk, tmp, tmpk, ["ps4"])
                if i == 0 or dbg is not None:
                    pass
                b_n, bnk = bns.next(); mixT, mixTk = mixTs.next(); st, stk = sts.next()
                okeys = [oak + "_0", oak + "_1"]
                I("act", "activation", dict(out=junk, in_=oacc, func=AF.Square, accum_out=st[:, 0:1]), reads=okeys, writes=[jk, stk])
                I("act", "activation", dict(out=st[:, 1:2], in_=st[:, 0:1], func=AF.Sqrt, scale=1.0 / 512, bias=EPS), reads=[stk], writes=[stk])
                I("dve", "reciprocal", dict(out=st[:, 2:3], in_=st[:, 1:2]), reads=[stk], writes=[stk])
                I("dve", "scalar_tensor_tensor", dict(out=b_n, in0=oacc, scalar=st[:, 2:3], in1=mog1, op0=ALU.mult, op1=ALU.mult),
                  reads=okeys + [stk, K_mog1], writes=[bnk])
                if dbg is not None and "b_tok" in dbg:
                    if i == 0:
                        dbg_b = nc.dram_tensor("dbg_b_tok", [128, 16, 512], F32, kind="ExternalOutput").ap()
                    dma("sp", dbg_b[:, i, :], oacc, reads=okeys, stream="dbg")
                k6 = ["ps6q", "ps6b0", "ps6b1"]
                for c in range(4):
                    I("pe", "transpose", dict(out=pst6[:, c * 128:(c + 1) * 128], in_=a_tok[:, i, c * 128:(c + 1) * 128], identity=identb[:]),
                      reads=[K_atok + "_%d" % i, "identf"], writes=k6)
                for c in range(4):
                    I("pe", "transpose", dict(out=pst6[:, (4 + c) * 128:(5 + c) * 128], in_=b_n[:, c * 128:(c + 1) * 128], identity=identb[:]),
                      reads=[bnk, "identf"], writes=k6)
                I("dve", "tensor_copy", dict(out=mixT, in_=pst6[:, :].rearrange("p (a b) -> p a b", b=128)), reads=k6, writes=[mixTk])
                for hf in range(2):
                    yb = sbank.next()
                    for kc in range(8):
                        I("pe", "matmul", dict(out=ps[yb][:, :], lhsT=mixT[:, kc, :], rhs=W_out[:, kc, hf * 512:(hf + 1) * 512], start=(kc == 0), stop=(kc == 7)),
                          reads=[mixTk, K_Wout], writes=["ps%d" % yb])
                    I("dve", "tensor_tensor", dict(out=x_res[:, i, hf * 512:(hf + 1) * 512], in0=ps[yb][:, :], in1=x_res[:, i, hf * 512:(hf + 1) * 512], op=ALU.add),
                      reads=["ps%d" % yb, xk], writes=[xk])
            dump("x1", x_res[:, :, :], [128, 16, 1024], ["x%d" % i for i in range(16)])
            if stop in ("P4", "%d:P4" % l):
                break
            P.barrier()
            AR.off = 0
            W_mkv, K_Wmkv = AR.alloc([8, 1024], BF16, "W_mkv")
            W_mq, K_Wmq = AR.alloc([8, 512], BF16, "W_mq")
            W_mo, K_Wmo = AR.alloc([4, 1024], BF16, "W_mo")
            gmkv, K_gmkv = AR.alloc([1024], F32, "gmkv")
            gmem, K_gmem = AR.alloc([1024], F32, "gmem")
            mqg, K_mqg = AR.alloc([512], F32, "mqg")
            mkg, K_mkg = AR.alloc([512], F32, "mkg")
            hTm, K_hTm = AR.alloc([8, 256], BF16, "hTm")
            KmT, K_KmT = AR.alloc([4, 256], BF16, "KmT")
            Vm, K_Vm = AR.alloc([2, 4, 129], BF16, "Vm")
            xins = Rot([AR.alloc([1024], F32, "xin") for _ in range(2)])
            hs = Rot([AR.alloc([1024], BF16, "h") for _ in range(2)])
            hTs = Rot([AR.alloc([8, 128], BF16, "hT") for _ in range(2)])
            junk, jk = AR.alloc([1024], BF16, "junk")
            sts = Rot([AR.alloc([32], F32, "st") for _ in range(2)])
            t1s = Rot([AR.alloc([512], F32, "t1") for _ in range(2)])
            qns = Rot([AR.alloc([512], BF16, "qn") for _ in range(2)])
            QmTs = Rot([AR.alloc([4, 128], BF16, "QmT") for _ in range(2)])
            PTs = Rot([AR.alloc([512], BF16, "PTm") for _ in range(4)])
            oms = Rot([AR.alloc([512], BF16, "om") for _ in range(2)])
            omTs = Rot([AR.alloc([4, 128], BF16, "omT") for _ in range(2)])
            wload(W_mkv, D["w_mkv"][l], K_Wmkv, nsplit=4)
            wload(W_mq, D["w_mq"][l], K_Wmq, nsplit=2)
            wload(W_mo, D["w_mo"][l], K_Wmo, nsplit=2)
            gload(l, "mkvg", gmkv, K_gmkv); gload(l, "memg", gmem, K_gmem); gload(l, "mqg", mqg, K_mqg); gload(l, "mkg", mkg, K_mkg)
            I("dve", "memset", dict(ap=Vm[:, :, :, 128:129], constant=1.0), writes=[K_Vm + "one"])

            def head_rms128(psb, gt, gk, st, stk, t1, t1k, out_bf, outk):
                I("act", "activation", dict(out=t1, in_=ps[psb][:, :], func=AF.Square), reads=["ps%d" % psb], writes=[t1k])
                I("dve", "tensor_reduce", dict(out=st[:, 8:12], in_=t1.rearrange("p (a b) -> p a b", b=128), axis=AX.X, op=ALU.add), reads=[t1k], writes=[stk + "q"])
                I("act", "activation", dict(out=st[:, 12:16], in_=st[:, 8:12], func=AF.Sqrt, scale=1.0 / 128, bias=EPS), reads=[stk + "q"], writes=[stk + "q"])
                I("dve", "reciprocal", dict(out=st[:, 16:20], in_=st[:, 12:16]), reads=[stk + "q"], writes=[stk + "q"])
                I("dve", "tensor_tensor", dict(out=t1.rearrange("p (a b) -> p a b", b=128), in0=ps[psb][:, :].rearrange("p (a b) -> p a b", b=128),
                                               in1=st[:, 16:20].unsqueeze(2).to_broadcast([128, 4, 128]), op=ALU.mult), reads=["ps%d" % psb, stk + "q"], writes=[t1k])
                I("dve", "tensor_tensor", dict(out=out_bf, in0=t1, in1=gt, op=ALU.mult), reads=[t1k, gk], writes=[outk])

            for mt in range(2):
                xin, xk_ = xins.next(); h, hk = hs.next(); hT, hTk = hTs.next(); st, stk = sts.next(); t1, t1k = t1s.next(); qn, qnk = qns.next()
                dma("sp", xin, D["mem"][mt * 128:(mt + 1) * 128, :], writes=[xk_], stream="xin%d" % mt)
                norm_to_hT(xin, xk_, gmkv, K_gmkv, h, hk, hT, hTk, junk, jk, st, stk, 6)
                I("dve", "tensor_copy", dict(out=hTm[:, :, mt * 128:(mt + 1) * 128], in_=hT), reads=[hTk], writes=[K_hTm])
                for hf in range(2):
                    for kc in range(8):
                        I("pe", "matmul", dict(out=ps[hf][:, :], lhsT=hTm[:, kc, mt * 128:(mt + 1) * 128], rhs=W_mkv[:, kc, hf * 512:(hf + 1) * 512],
                                               start=(kc == 0), stop=(kc == 7)), reads=[K_hTm, K_Wmkv], writes=["ps%d" % hf])
                head_rms128(0, mkg, K_mkg, st, stk, t1, t1k, qn, qnk)
                pst = ps[7][:, :].bitcast(BF16)
                for c in range(4):
                    I("pe", "transpose", dict(out=pst[:, c * 128:(c + 1) * 128], in_=qn[:, c * 128:(c + 1) * 128], identity=identb[:]), reads=[qnk, "identf"], writes=["ps7"])
                I("dve", "tensor_copy", dict(out=KmT[:, :, mt * 128:(mt + 1) * 128], in_=pst[:, 0:512].rearrange("p (a b) -> p a b", b=128)), reads=["ps7"], writes=[K_KmT])
                I("act", "activation", dict(out=Vm[:, mt, :, 0:128], in_=ps[1][:, :].rearrange("p (a b) -> p a b", b=128), func=AF.Copy), reads=["ps1"], writes=[K_Vm])
            sc_m = 128.0 ** -0.5
            for i in range(16):
                xk = "x%d" % i
                h, hk = hs.next(); hT, hTk = hTs.next(); st, stk = sts.next(); t1, t1k = t1s.next(); qn, qnk = qns.next()
                QmT, QmTk = QmTs.next(); om, omk = oms.next(); omT, omTk = omTs.next()
                norm_to_hT(x_res[:, i, :], xk, gmem, K_gmem, h, hk, hT, hTk, junk, jk, st, stk, 6)
                for kc in range(8):
                    I("pe", "matmul", dict(out=ps[0][:, :], lhsT=hT[:, kc, :], rhs=W_mq[:, kc, :], start=(kc == 0), stop=(kc == 7)), reads=[hTk, K_Wmq], writes=["ps0"])
                head_rms128(0, mqg, K_mqg, st, stk, t1, t1k, qn, qnk)
                pst = ps[7][:, :].bitcast(BF16)
                for c in range(4):
                    I("pe", "transpose", dict(out=pst[:, c * 128:(c + 1) * 128], in_=qn[:, c * 128:(c + 1) * 128], identity=identb[:]), reads=[qnk, "identf"], writes=["ps7"])
                I("dve", "tensor_copy", dict(out=QmT, in_=pst[:, 0:512].rearrange("p (a b) -> p a b", b=128)), reads=["ps7"], writes=[QmTk])
                ptm = []
                for mt in range(2):
                    sbk = 1 + mt
                    for hd in range(4):
                        I("pe", "matmul", dict(out=ps[sbk][:, hd * 128:(hd + 1) * 128], lhsT=KmT[:, hd, mt * 128:(mt + 1) * 128], rhs=QmT[:, hd, :], start=True, stop=True),
                          reads=[K_KmT, QmTk], writes=["ps%d" % sbk])
                    pt, ptk = PTs.next()
                    I("act", "activation", dict(out=pt, in_=ps[sbk][:, :], func=AF.Exp, scale=sc_m), reads=["ps%d" % sbk], writes=[ptk])
                    ptm.append((pt, ptk))
                for hd in range(4):
                    ab = 3 + hd // 2
                    c0 = (hd % 2) * 129
                    for mt in range(2):
                        pt, ptk = ptm[mt]
                        I("pe", "matmul", dict(out=ps[ab][:, c0:c0 + 129], lhsT=pt[:, hd * 128:(hd + 1) * 128], rhs=Vm[:, mt, hd, :], start=(mt == 0), stop=(mt == 1)),
                          reads=[ptk, K_Vm, K_Vm + "one"], writes=["ps%d" % ab])
                for ab in (3, 4):
                    accv = ps[ab][:, 0:258].rearrange("p (a c) -> p a c", c=129)
                    so = 20 + (ab - 3) * 2
                    I("dve", "reciprocal", dict(out=st[:, so:so + 2], in_=accv[:, :, 128]), reads=["ps%d" % ab], writes=[stk + "m%d" % ab])
                    I("dve", "tensor_tensor", dict(out=om[:, (ab - 3) * 256:(ab - 2) * 256].rearrange("p (a c) -> p a c", c=128), in0=accv[:, :, 0:128],
                                                   in1=st[:, so:so + 2].unsqueeze(2).to_broadcast([128, 2, 128]), op=ALU.mult),
                      reads=["ps%d" % ab, stk + "m%d" % ab], writes=[omk])
                pst5 = ps[5][:, :].bitcast(BF16)
                for c in range(4):
                    I("pe", "transpose", dict(out=pst5[:, c * 128:(c + 1) * 128], in_=om[:, c * 128:(c + 1) * 128], identity=identb[:]), reads=[omk, "identf"], writes=["ps5"])
                I("dve", "tensor_copy", dict(out=omT, in_=pst5[:, 0:512].rearrange("p (a b) -> p a b", b=128)), reads=["ps5"], writes=[omTk])
                for hf in range(2):
                    yb = 0 if hf == 0 else 7
                    for kc in range(4):
                        I("pe", "matmul", dict(out=ps[yb][:, :], lhsT=omT[:, kc, :], rhs=W_mo[:, kc, hf * 512:(hf + 1) * 512], start=(kc == 0), stop=(kc == 3)),
                          reads=[omTk, K_Wmo], writes=["ps%d" % yb])
                    I("dve", "tensor_tensor", dict(out=x_res[:, i, hf * 512:(hf + 1) * 512], in0=ps[yb][:, :], in1=x_res[:, i, hf * 512:(hf + 1) * 512], op=ALU.add),
                      reads=["ps%d" % yb, xk], writes=[xk])
            dump("x2", x_res[:, :, :], [128, 16, 1024], ["x%d" % i for i in range(16)])
            if stop in ("P5", "%d:P5" % l):
                break
            P.barrier()
            AR.off = 0
            hTa, K_hTa = AR.alloc([8, 2048], BF16, "hTa")
            hidT, K_hidT = AR.alloc([4, 2048], BF16, "hidT")
            W1gs = Rot([AR.alloc([8, 512], BF16, "W1g") for _ in range(2)])
            W2gs = Rot([AR.alloc([4, 1024], BF16, "W2g") for _ in range(2)])
            rts = Rot([AR.alloc([512], F32, "rt") for _ in range(3)])
            gffn, K_gffn = AR.alloc([1024], F32, "gffn")
            hs = Rot([AR.alloc([1024], BF16, "h") for _ in range(2)])
            junk, jk = AR.alloc([1024], BF16, "junk")
            sts = Rot([AR.alloc([8], F32, "st") for _ in range(2)])
            gload(l, "ffng", gffn, K_gffn)
            w1v = D["w_ff1"][l].rearrange("(kc p) n -> p kc n", p=128)
            w2v = D["w_ff2"][l].rearrange("(jc j) n -> j jc n", j=128)
            wg = {}

            def load_group(grp):
                W1g, W1gk = W1gs.next(); W2g, W2gk = W2gs.next()
                for hf2 in range(2):
                    dma("pool", W1g[:, hf2 * 4:(hf2 + 1) * 4, :], w1v[:, hf2 * 4:(hf2 + 1) * 4, grp * 512:(grp + 1) * 512], writes=[W1gk + "_%d" % hf2],
                        stream="w1g%d_%d" % (grp % 2, hf2))
                    dma("pool", W2g[:, hf2 * 2:(hf2 + 1) * 2, :], w2v[:, grp * 4 + hf2 * 2:grp * 4 + (hf2 + 1) * 2, :], writes=[W2gk + "_%d" % hf2],
                        stream="w2g%d_%d" % (grp % 2, hf2))
                wg[grp] = (W1g, W1gk, W2g, W2gk)
            load_group(0)
            for i in range(16):
                h, hk = hs.next(); st, stk = sts.next()
                pT = 0 if i % 2 == 0 else 7
                rstd = rms_rstd(x_res[:, i, :], 1024, "x%d" % i, junk, jk, st, stk, 0)
                I("dve", "scalar_tensor_tensor", dict(out=h, in0=x_res[:, i, :], scalar=rstd, in1=gffn, op0=ALU.mult, op1=ALU.mult),
                  reads=["x%d" % i, stk, K_gffn], writes=[hk])
                transposes(h, hk, 8, hTa[:, :, i * 128:(i + 1) * 128], K_hTa + "_%d" % i, pT, evac="dve")
            hbank = Rot([1, 2, 3, 4])
            ybank = Rot([5, 6])
            for grp in range(8):
                if grp + 1 < 8:
                    load_group(grp + 1)
                W1g, W1gk, W2g, W2gk = wg[grp]
                for jc in range(4):
                    for tb in range(4):
                        hb = hbank.next()
                        for kc in range(8):
                            I("pe", "matmul", dict(out=ps[hb][:, :], lhsT=W1g[:, kc, jc * 128:(jc + 1) * 128], rhs=hTa[:, kc, tb * 512:(tb + 1) * 512],
                                                   start=(kc == 0), stop=(kc == 7)),
                              reads=[W1gk + "_%d" % (kc // 4)] + [K_hTa + "_%d" % t_ for t_ in range(tb * 4, tb * 4 + 4)], writes=["ps%d" % hb])
                        rt, rtk = rts.next()
                        I("act", "activation", dict(out=rt, in_=ps[hb][:, :], func=AF.Relu), reads=["ps%d" % hb], writes=[rtk])
                        I("pool", "tensor_tensor", dict(out=hidT[:, jc, tb * 512:(tb + 1) * 512], in0=rt, in1=rt, op=ALU.mult), reads=[rtk],
                          writes=[K_hidT + "_%d_%d" % (jc, tb)])
                for i in range(16):
                    for hf in range(2):
                        yb = ybank.next()
                        for jc in range(4):
                            I("pe", "matmul", dict(out=ps[yb][:, :], lhsT=hidT[:, jc, i * 128:(i + 1) * 128], rhs=W2g[:, jc, hf * 512:(hf + 1) * 512],
                                                   start=(jc == 0), stop=(jc == 3)),
                              reads=[K_hidT + "_%d_%d" % (jc, i // 4), W2gk + "_%d" % (jc // 2)], writes=["ps%d" % yb])
                        I("dve", "tensor_tensor", dict(out=x_res[:, i, hf * 512:(hf + 1) * 512], in0=ps[yb][:, :], in1=x_res[:, i, hf * 512:(hf + 1) * 512], op=ALU.add),
                          reads=["ps%d" % yb, "x%d" % i], writes=["x%d" % i])
            P.barrier()

        yo = D["y"].rearrange("(i p) d -> p i d", p=128)
        for i4 in range(4):
            dma("sp", yo[:, i4 * 4:(i4 + 1) * 4, :], x_res[:, i4 * 4:(i4 + 1) * 4, :],
                reads=["x%d" % i for i in range(i4 * 4, i4 * 4 + 4)], stream="yo%d" % i4)
        P.emit()
    return nc, dbg_out


_NC_CACHE = {}


def _assemble(ys):
    out = np.empty((4, 32, 128, 1024), np.float32)
    for c in range(8):
        out[c // 2, (c % 2)::2] = ys[c].reshape(16, 128, 1024)
    return out.reshape(4, 4096, 1024)


def kernel(**inputs):
    inp = {k: np.asarray(v) for k, v in inputs.items()}
    x = np.ascontiguousarray(inp["x"], dtype=np.float32)
    mem = np.ascontiguousarray(inp["mem"], dtype=np.float32)
    consts = [host_consts(h) for h in (0, 1)]
    NL = 2
    if "nc" not in _NC_CACHE:
        _NC_CACHE["nc"] = build(nl=NL)[0]
    nc = _NC_CACHE["nc"]
    ws = [host_layer_weights(inp, l) for l in range(NL)]
    wst = {k: np.stack([w[k] for w in ws]) for k in ws[0]}
    in_maps = []
    for c in range(8):
        b, half = c // 2, c % 2
        m = {}
        xt = x[b].reshape(32, 128, 1024)
        m["x_own"] = np.ascontiguousarray(xt[half::2].reshape(2048, 1024))
        m["mem"] = mem[b]
        m.update(wst)
        m.update(consts[half])
        in_maps.append(m)
    res = run_bass_kernel_spmd(nc, in_maps, core_ids=list(range(8)))
    return _assemble([res.results[c]["y"] for c in range(8)]).astype(np.float32)
```

```python
import contextlib
import os
import numpy as np
import concourse.bass as bass
import concourse.mybir as mybir
from concourse.bass_utils import run_bass_kernel_spmd

F32 = mybir.dt.float32
BF16 = mybir.dt.bfloat16
ALU = mybir.AluOpType
AF = mybir.ActivationFunctionType
AX = mybir.AxisListType

ENGS = ("pe", "act", "dve", "pool", "sp")
SKIP_BIG = os.environ.get("SKIP_BIG", "1") == "1"
BIGN = 128
EPS = 1e-6
NEG = -30000.0


class Prog:
    def __init__(self, nc):
        self.nc = nc
        self.ops = []
        self.last_writer = {}
        self.readers = {}
        self.barrier_deps = set()
        self.last_eng_op = {}
        self.last_dma_op = {}
        self.groups = {}

    def _expand(self, keys):
        out = []
        for k in keys:
            out.extend(self.groups.get(k, (k,)))
        return out

    def op(self, eng, fn, reads=(), writes=(), dma=None, inc=None, big=False):
        idx = len(self.ops)
        reads = self._expand(reads)
        writes = self._expand(writes)
        deps = set(self.barrier_deps)
        for r in reads:
            w = self.last_writer.get(r)
            if w is not None:
                deps.add(w)
        for w_ in writes:
            w = self.last_writer.get(w_)
            if w is not None:
                deps.add(w)
            for rd in self.readers.get(w_, ()):
                deps.add(rd)
        if dma is not None and dma in self.last_dma_op:
            deps.add(self.last_dma_op[dma])
        self.ops.append(dict(eng=eng, fn=fn, deps=deps, dma=dma, big=big, sig=False, inc=(inc if inc is not None else (16 if dma is not None else 1))))
        for r in reads:
            self.readers.setdefault(r, []).append(idx)
        for w_ in writes:
            self.last_writer[w_] = idx
            self.readers[w_] = []
        if dma is None:
            self.last_eng_op[eng] = idx
        else:
            self.last_dma_op[dma] = idx
        return idx

    def barrier(self):
        self.barrier_deps = set(self.last_eng_op.values()) | set(self.last_dma_op.values())

    def _skip(self, p, o):
        if p["dma"] is not None or o["dma"] is not None or p["eng"] != o["eng"]:
            return False
        return p["eng"] == "pe" or (p["big"] and SKIP_BIG)

    def emit(self, final_wait_eng="sp"):
        nc = self.nc
        ops = self.ops
        for o in ops:
            for d in o["deps"]:
                p = ops[d]
                if not self._skip(p, o):
                    p["sig"] = True
            if o["dma"] is not None:
                o["sig"] = True
        counters = {}
        for o in ops:
            if not o["sig"]:
                continue
            sname = ("dma:" + o["dma"]) if o["dma"] is not None else ("eng:" + o["eng"])
            counters[sname] = counters.get(sname, 0) + o["inc"]
            o["signal"] = (sname, counters[sname])
        with contextlib.ExitStack() as es:
            sems = {}
            for sn in sorted(counters):
                sems[sn] = es.enter_context(nc.semaphore(sn.replace(":", "_")))
            block = es.enter_context(nc.Block())
            per_eng = {e: [] for e in ENGS}
            for i, o in enumerate(ops):
                per_eng[o["eng"]].append(i)

            def body(ename, engine):
                waited = {}
                for i in per_eng[ename]:
                    o = ops[i]
                    need = {}
                    for d in o["deps"]:
                        p = ops[d]
                        if not p["sig"] or self._skip(p, o):
                            continue
                        sn, val = p["signal"]
                        if need.get(sn, 0) < val:
                            need[sn] = val
                    for sn, val in need.items():
                        if waited.get(sn, 0) < val:
                            engine.wait_ge(sems[sn], val)
                            waited[sn] = val
                    ins = o["fn"](engine)
                    if o["sig"]:
                        sn, val = o["signal"]
                        ins.then_inc(sems[sn], o["inc"])
                if ename == final_wait_eng:
                    for sn, val in counters.items():
                        if sn.startswith("dma:") and waited.get(sn, 0) < val:
                            engine.wait_ge(sems[sn], val)

            regs = dict(pe=block.tensor, act=block.scalar, dve=block.vector, pool=block.gpsimd, sp=block.sync)
            for ename in ENGS:
                if not per_eng[ename] and ename != final_wait_eng:
                    continue

                def mk(en):
                    def f(engine):
                        body(en, engine)
                    return f
                regs[ename](mk(ename))
        return counters


class Rot:
    def __init__(self, items):
        self.items = items
        self.i = 0

    def next(self):
        it = self.items[self.i % len(self.items)]
        self.i += 1
        return it


class Arena:
    def __init__(self, t, nbytes):
        self.t = t
        self.nbytes = nbytes
        self.off = 0
        self.cnt = 0

    def alloc(self, free_shape, dt, name):
        n = int(np.prod(free_shape))
        esz = 4 if dt == F32 else 2
        nb = (n * esz + 63) // 64 * 64
        assert self.off + nb <= self.nbytes, (name, self.off, nb, self.nbytes)
        a = self.off // 2
        v = self.t[:, a:a + (n * esz) // 2]
        if dt == F32:
            v = v.bitcast(F32)
        self.off += nb
        self.cnt += 1
        key = "%s@%d" % (name, self.cnt)
        if len(free_shape) == 2:
            v = v.rearrange("p (a b) -> p a b", b=free_shape[1])
        elif len(free_shape) == 3:
            v = v.rearrange("p (a b c) -> p a b c", b=free_shape[1], c=free_shape[2])
        return v, key


GV = {}
_o = 0
for _n, _s in [("mixg", 1024), ("memg", 1024), ("ffng", 1024), ("mkvg", 1024), ("lng", 512), ("lnb", 512),
               ("mog0", 512), ("mog1", 512), ("qg", 512), ("kg", 256), ("kcg", 64), ("b2k", 64), ("b2v", 64),
               ("mqg", 512), ("mkg", 512)]:
    GV[_n] = (_o, _s)
    _o += _s
NGV = _o


def host_consts(half):
    c = {}
    c["identf"] = np.eye(128, dtype=np.float32)
    k = np.arange(4096)
    c["Econst"] = (k[None, :] // 64 == np.arange(64)[:, None]).astype(np.float32)
    OFF = 240
    cc = np.arange(512) - OFF - 8 * half
    A = np.zeros((64, 512), np.float32)
    C = np.zeros((64, 128), np.float32)
    tl = np.arange(128)
    for j in range(8):
        m = j - 1
        A[j] = (cc == m)
        C[j] = np.where(16 * m + 31 > tl, NEG, 0.0)
    A[8] = (cc >= 7)
    C[8] = NEG
    c["Astat"] = A
    c["Cst"] = np.tile(C, (1, 4)).astype(np.float32)
    kl = np.arange(128)[:, None]
    ql = np.arange(128)[None, :]
    tri = (kl <= ql).astype(np.float32)
    left = (kl > ql).astype(np.float32)
    ones = np.ones((128, 128), np.float32)
    zeros = np.zeros((128, 128), np.float32)
    if half == 0:
        mD = [tri, zeros]
        mW = [left, ones, tri, zeros]
    else:
        mD = [ones, tri]
        mW = [zeros, left, ones, tri]
    c["maskD"] = np.stack([np.tile(m, (1, 4)) for m in mD], 1).astype(np.float32)
    c["maskW"] = np.stack([np.tile(m, (1, 4)) for m in mW], 1).astype(np.float32)
    Mv = np.zeros((128, 16, 64), np.float32)
    Ma = np.zeros((128, 16, 64), np.float32)
    j = np.arange(64)[None, :]
    for i in range(16):
        G = 2 * i + half
        t = 128 * G + np.arange(128)[:, None]
        tb = t // 64
        forced = (j == 0) | (j == tb) | (j == tb - 1)
        invalid = (j > tb)
        Mv[:, i, :] = (~forced & ~invalid)
        Ma[:, i, :] = np.where(forced, 1e4, np.where(invalid, -1e4, 0.0))
    c["Mvalid"] = Mv
    c["Madd"] = Ma
    n = np.arange(256)
    cs = n * 16
    ss = np.arange(64) * 64
    ov = np.clip(np.minimum(cs[:, None] + 32, ss[None, :] + 64) - np.maximum(cs[:, None], ss[None, :]), 0, None) / 32.0
    ov[255] = 0.0
    c["ovc"] = ov.reshape(2, 128, 64).transpose(1, 0, 2).astype(np.float32).copy()
    c["trisg"] = tri.copy()
    return c


def host_layer_weights(inp, l):
    w = {}
    wi = inp["w_in"][l]
    kv = wi[:, 1536:2304]
    kc, vc, ks, vs, kw, vw = [kv[:, i * 128:(i + 1) * 128] for i in range(6)]
    w["w_in_p"] = np.concatenate([wi[:, 0:1536], wi[:, 2304:2328], ks, kw, vs, vw, kc, vc], axis=1)
    gv = np.zeros((1, NGV), np.float32)

    def put(name, v):
        o, s = GV[name]
        gv[0, o:o + s] = np.asarray(v).reshape(-1)
    put("mixg", inp["norm_mix_g"][l]); put("memg", inp["norm_mem_g"][l]); put("ffng", inp["norm_ffn_g"][l])
    put("mkvg", inp["mem_kv_norm_g"][l]); put("lng", inp["sg_ln_g"][l]); put("lnb", inp["sg_ln_b"][l])
    put("mog0", inp["mix_out_g"][l, 0]); put("mog1", inp["mix_out_g"][l, 1])
    put("qg", np.tile(inp["q_norm_g"][l], 8))
    put("kg", np.concatenate([np.tile(inp["k_norm_g"][l, 1], 2), np.tile(inp["k_norm_g"][l, 2], 2)]))
    put("kcg", inp["k_norm_g"][l, 0]); put("b2k", inp["cmp_b2"][l, 0]); put("b2v", inp["cmp_b2"][l, 1])
    put("mqg", np.tile(inp["mem_q_norm_g"][l], 4)); put("mkg", np.tile(inp["mem_k_norm_g"][l], 4))
    w["gvec"] = gv
    w["sg_wT"] = np.ascontiguousarray(inp["sg_w"][l].transpose(2, 0, 1))
    w["sg_bT"] = np.ascontiguousarray(inp["sg_b"][l].T)
    w["cmp_w1"] = np.ascontiguousarray(inp["cmp_w1"][l])
    w["cmp_b1T"] = np.ascontiguousarray(inp["cmp_b1"][l].reshape(2, 2, 128).transpose(0, 2, 1))
    w["cmp_posT"] = np.ascontiguousarray(inp["cmp_pos"][l].transpose(0, 2, 1))
    w["cmp_w2"] = np.ascontiguousarray(inp["cmp_w2"][l])
    for k in ("w_out", "w_mq", "w_mkv", "w_mo", "w_ff1", "w_ff2"):
        w[k] = np.ascontiguousarray(inp[k][l])
    return w


WSHAPES = dict(w_in_p=[1024, 2328], gvec=[1, NGV], sg_wT=[128, 8, 128], sg_bT=[128, 8], cmp_w1=[2, 2048, 256],
               cmp_b1T=[2, 128, 2], cmp_posT=[2, 64, 32], cmp_w2=[2, 256, 64], w_out=[1024, 1024],
               w_mq=[1024, 512], w_mkv=[1024, 1024], w_mo=[512, 1024], w_ff1=[1024, 4096], w_ff2=[4096, 1024])
CSHAPES = dict(identf=[128, 128], Econst=[64, 4096], Astat=[64, 512], Cst=[64, 512], maskD=[128, 2, 512],
               maskW=[128, 4, 512], Mvalid=[128, 16, 64], Madd=[128, 16, 64], ovc=[128, 2, 64], trisg=[128, 128])


def build(nl=1, stop=None, dbg=None, ncores=8):
    nc = bass.Bass("TRN2", target_bir_lowering=False)
    D = {}
    D["x_own"] = nc.dram_tensor("x_own", [2048, 1024], F32, kind="ExternalInput").ap()
    hx_in = [nc.dram_tensor("hxin%d" % g, [512, 1024], BF16) for g in range(4)]
    hx_out = [nc.dram_tensor("hxout%d" % g, [1024, 1024], BF16) for g in range(4)]
    RG = [[2 * p, 2 * p + 1] for p in range(ncores // 2)]
    D["mem"] = nc.dram_tensor("mem", [256, 1024], F32, kind="ExternalInput").ap()
    for k, s in WSHAPES.items():
        D[k] = nc.dram_tensor(k, [nl] + s, F32, kind="ExternalInput").ap()
    for k, s in CSHAPES.items():
        D[k] = nc.dram_tensor(k, s, F32, kind="ExternalInput").ap()
    D["y"] = nc.dram_tensor("y", [2048, 1024], F32, kind="ExternalOutput").ap()
    dbg_out = {}

    with contextlib.ExitStack() as es:
        def sb(name, shape, dt=F32):
            return es.enter_context(nc.sbuf_tensor("s_" + name, shape, dt))

        x_res = sb("x_res", [128, 16, 1024], F32)
        ARB = 128 * 1024
        arena_t = sb("arena", [128, ARB // 2], BF16)
        identb = sb("identb", [128, 128], BF16)
        Astat = sb("Astat", [64, 512], BF16)
        Cst = sb("Cst", [64, 512], BF16)
        maskD = sb("maskD", [128, 2, 512], BF16)
        maskW = sb("maskW", [128, 4, 512], BF16)
        Mvalid = sb("Mvalid", [128, 16, 64], BF16)
        Madd = sb("Madd", [128, 16, 64], BF16)
        trisg = sb("trisg", [128, 128], BF16)
        zerob = sb("zerob", [128, 512], BF16)
        ps = [es.enter_context(nc.psum_tensor("ps%d" % b, [128, 512], F32)) for b in range(8)]

        P = Prog(nc)
        AR = Arena(arena_t, ARB)
        dmac = [0]

        def dma(eng, out, in_, reads=(), writes=(), stream=None):
            if stream is None:
                dmac[0] += 1
                stream = "q%s%d" % (eng, dmac[0] % 12)
            P.op(eng, lambda e, out=out, in_=in_: e.dma_start(out=out, in_=in_), reads=reads, writes=writes, dma=stream)

        def I(eng, meth, kw, reads=(), writes=()):
            big = False
            o_ = kw.get("out", None)
            if o_ is not None and "accum_out" not in kw and meth in ("tensor_tensor", "tensor_scalar", "scalar_tensor_tensor", "tensor_copy", "activation"):
                try:
                    big = int(np.prod(o_.shape[1:])) >= BIGN
                except Exception:
                    big = False
            P.op(eng, lambda e, kw=kw, meth=meth: getattr(e, meth)(**kw), reads=reads, writes=writes, big=big)

        def dump(name, ap, shape, reads):
            if dbg is None or name not in dbg:
                return
            d = nc.dram_tensor("dbg_" + name, shape, F32 if ap.dtype == F32 else BF16, kind="ExternalOutput").ap()
            dbg_out[name] = d
            dma("sp", d, ap, reads=reads, stream="dbg")

        for t, nm in ((identb, "identf"), (Astat, "Astat"), (Cst, "Cst"), (maskD, "maskD"), (maskW, "maskW"),
                      (Mvalid, "Mvalid"), (Madd, "Madd"), (trisg, "trisg")):
            dma("pool", t[:], D[nm], writes=[nm])
        I("dve", "memset", dict(ap=zerob[:], constant=0.0), writes=["zerob"])
        xo = D["x_own"].rearrange("(i p) d -> p i d", p=128)
        for i4 in range(4):
            dma("sp", x_res[:, i4 * 4:(i4 + 1) * 4, :], xo[:, i4 * 4:(i4 + 1) * 4, :],
                writes=["x%d" % i for i in range(i4 * 4, i4 * 4 + 4)], stream="xr%d" % i4)

        def gload(l, name, dst, key):
            o, s = GV[name]
            dma("sp", dst, D["gvec"][l, 0:1, o:o + s].partition_broadcast(128), writes=[key])

        small = {}

        def rms_rstd(x_ap, n, xkey, junk, junk_key, st, st_key, col):
            I("act", "activation", dict(out=junk, in_=x_ap, func=AF.Square, accum_out=st[:, col:col + 1]),
                 reads=[xkey], writes=[junk_key, st_key])
            I("act", "activation", dict(out=st[:, col + 1:col + 2], in_=st[:, col:col + 1], func=AF.Sqrt,
                                               scale=1.0 / n, bias=EPS), reads=[st_key], writes=[st_key])
            I("dve", "reciprocal", dict(out=st[:, col + 2:col + 3], in_=st[:, col + 1:col + 2]),
                 reads=[st_key], writes=[st_key])
            return st[:, col + 2:col + 3]

        def transposes(src, src_key, n, dst, dst_key, psb, evac="act"):
            pst = ps[psb][:, :].bitcast(BF16)
            for c in range(n):
                I("pe", "transpose", dict(out=pst[:, c * 128:(c + 1) * 128], in_=src[:, c * 128:(c + 1) * 128], identity=identb[:]),
                     reads=[src_key, "identf"], writes=["ps%d" % psb])
            v = pst[:, 0:n * 128].rearrange("p (a b) -> p a b", b=128)
            if evac == "act":
                I("act", "activation", dict(out=dst, in_=v, func=AF.Copy), reads=["ps%d" % psb], writes=[dst_key])
            else:
                I("dve", "tensor_copy", dict(out=dst, in_=v), reads=["ps%d" % psb], writes=[dst_key])

        def norm_to_hT(x_ap, xkey, g_tile, g_key, h, h_key, hT, hT_key, junk, junk_key, st, st_key, psb):
            rstd = rms_rstd(x_ap, 1024, xkey, junk, junk_key, st, st_key, 0)
            I("dve", "scalar_tensor_tensor", dict(out=h, in0=x_ap, scalar=rstd, in1=g_tile, op0=ALU.mult, op1=ALU.mult),
                 reads=[xkey, st_key, g_key], writes=[h_key])
            transposes(h, h_key, 8, hT, hT_key, psb)

        def wload(dst, src, key, nsplit=1):
            kc = dst.shape[1]
            v = src.rearrange("(kc p) n -> p kc n", p=128)
            step = (kc + nsplit - 1) // nsplit
            parts = []
            for a in range(0, kc, step):
                b = min(kc, a + step)
                parts.append("%s#%d" % (key, a))
                P.groups.pop(key, None)
                dma("pool", dst[:, a:b, :], v[:, a:b, :], writes=[parts[-1]])
            P.groups[key] = parts

        for l in range(nl):
            AR.off = 0
            q_tok, K_qtok = AR.alloc([16, 512], BF16, "q_tok")
            a_tok, K_atok = AR.alloc([16, 512], BF16, "a_tok")
            gates, K_gates = AR.alloc([16, 24], F32, "gates")
            mark_kv = AR.off
            W_own, K_Wown = AR.alloc([8, 1560], BF16, "W_own")
            sgw, K_sgw = AR.alloc([8, 128], BF16, "sgw")
            sgwf, K_sgwf = AR.alloc([8, 128], BF16, "sgwf")
            sgb, K_sgb = AR.alloc([8], F32, "sgb")
            gmix, K_gmix = AR.alloc([1024], F32, "gmix")
            lng, K_lng = AR.alloc([512], F32, "lng")
            lnb, K_lnb = AR.alloc([512], F32, "lnb")
            mog0, K_mog0 = AR.alloc([512], F32, "mog0")
            qg, K_qg = AR.alloc([512], F32, "qg")
            hs = Rot([AR.alloc([1024], BF16, "h") for _ in range(3)])
            hTs = Rot([AR.alloc([8, 128], BF16, "hT") for _ in range(3)])
            junks = Rot([AR.alloc([1024], F32, "junk") for _ in range(2)])
            sts = Rot([AR.alloc([32], F32, "st") for _ in range(3)])
            ugs = Rot([AR.alloc([512], BF16, "ug") for _ in range(3)])
            vgs = Rot([AR.alloc([512], F32, "vg") for _ in range(3)])
            vns = Rot([AR.alloc([512], BF16, "vn") for _ in range(2)])
            t1s = Rot([AR.alloc([512], F32, "t1") for _ in range(2)])
            t2s = Rot([AR.alloc([512], F32, "t2") for _ in range(2)])
            bns = Rot([AR.alloc([16], F32, "bn") for _ in range(2)])

            wload(W_own, D["w_in_p"][l, :, 0:1560], K_Wown, nsplit=4)
            dma("pool", sgwf, D["sg_wT"][l], writes=[K_sgwf])
            dma("sp", sgb, D["sg_bT"][l], writes=[K_sgb])
            gload(l, "mixg", gmix, K_gmix); gload(l, "lng", lng, K_lng); gload(l, "lnb", lnb, K_lnb)
            gload(l, "mog0", mog0, K_mog0); gload(l, "qg", qg, K_qg)
            I("dve", "tensor_tensor", dict(out=sgw, in0=sgwf, in1=trisg[:].unsqueeze(1).to_broadcast([128, 8, 128]), op=ALU.mult),
                 reads=[K_sgwf, "trisg"], writes=[K_sgw])

            ctx = {}
            tq1s = Rot([AR.alloc([512], F32, "tq1") for _ in range(2)])
            tq2s = Rot([AR.alloc([512], F32, "tq2") for _ in range(2)])
            for i in range(16):
                h, hk = hs.next(); hT, hTk = hTs.next(); junk, jk = junks.next(); st, stk = sts.next()
                ug, ugk = ugs.next(); vg, vgk = vgs.next(); vn, vnk = vns.next()
                t1, t1k = t1s.next(); t2, t2k = t2s.next(); bn, bnk = bns.next(); tq1, tq1k = tq1s.next(); tq2, tq2k = tq2s.next()
                ctx[i] = ("x%d" % i, h, hk, hT, hTk, junk, jk, st, stk, (0 if i % 2 == 0 else 7), ([1, 2, 3] if i % 2 == 0 else [4, 5, 6]),
                          ug, ugk, vg, vgk, vn, vnk, t1, t1k, t2, t2k, bn, bnk, tq1, tq1k, tq2, tq2k)

            def stA(i):
                (xk, h, hk, hT, hTk, junk, jk, st, stk, pT, pb, ug, ugk, vg, vgk, vn, vnk, t1, t1k, t2, t2k, bn, bnk, tq1, tq1k, tq2, tq2k) = ctx[i]
                norm_to_hT(x_res[:, i, :], xk, gmix, K_gmix, h, hk, hT, hTk, junk, jk, st, stk, pT)
                gq = i // 4
                dma("sp", hx_in[gq][(i % 4) * 128:(i % 4 + 1) * 128, :], h, reads=[hk], writes=["hxin%d" % gq], stream="hxs%d" % gq)
                if i % 4 == 3:
                    P.op("pool", lambda e, gq=gq: e.collective_compute("AllGather", ALU.bypass, replica_groups=RG,
                                                                         ins=[hx_in[gq].ap().opt()], outs=[hx_out[gq].ap().opt()]),
                         reads=["hxin%d" % gq], writes=["hxout%d" % gq], dma="cc%d" % gq, inc=1)

            def stB(i):
                (xk, h, hk, hT, hTk, junk, jk, st, stk, pT, pb, ug, ugk, vg, vgk, vn, vnk, t1, t1k, t2, t2k, bn, bnk, tq1, tq1k, tq2, tq2k) = ctx[i]
                for ci, (c0, c1) in enumerate(((0, 512), (512, 1024), (1024, 1536))):
                    for kc in range(8):
                        I("pe", "matmul", dict(out=ps[pb[ci]][:, :], lhsT=hT[:, kc, :], rhs=W_own[:, kc, c0:c1],
                                                                                  start=(kc == 0), stop=(kc == 7)),
                             reads=[hTk, K_Wown], writes=["ps%d" % pb[ci]])
                if i == 0:
                    dump("h0", h, [128, 1024], [hk]); dump("hT0", hT, [128, 8, 128], [hTk]); dump("st0", st, [128, 32], [stk])
                    dump("gmix", gmix, [128, 1024], [K_gmix]); dump("W0", W_own[:, 0, :], [128, 1560], [K_Wown])
                I("act", "activation", dict(out=ug, in_=ps[pb[0]][:, :], func=AF.Gelu_apprx_tanh), reads=["ps%d" % pb[0]], writes=[ugk])
                I("act", "activation", dict(out=vg, in_=ps[pb[1]][:, :], func=AF.Gelu_apprx_tanh), reads=["ps%d" % pb[1]], writes=[vgk])
                if i == 0:
                    dump("ug0", ug, [128, 512], [ugk]); dump("vg0", vg, [128, 512], [vgk])
                I("act", "activation", dict(out=tq1, in_=ps[pb[2]][:, :], func=AF.Square), reads=["ps%d" % pb[2]], writes=[tq1k])
                I("dve", "tensor_reduce", dict(out=st[:, 8:16], in_=tq1.rearrange("p (a b) -> p a b", b=64), axis=AX.X, op=ALU.add),
                     reads=[tq1k], writes=[stk + "q"])
                I("act", "activation", dict(out=st[:, 16:24], in_=st[:, 8:16], func=AF.Sqrt, scale=1.0 / 64, bias=EPS),
                     reads=[stk + "q"], writes=[stk + "q"])
                I("dve", "reciprocal", dict(out=st[:, 24:32], in_=st[:, 16:24]), reads=[stk + "q"], writes=[stk + "q"])
                I("dve", "tensor_tensor", dict(out=tq2.rearrange("p (a b) -> p a b", b=64),
                                                              in0=ps[pb[2]][:, :].rearrange("p (a b) -> p a b", b=64),
                                                              in1=st[:, 24:32].unsqueeze(2).to_broadcast([128, 8, 64]), op=ALU.mult),
                     reads=["ps%d" % pb[2], stk + "q"], writes=[tq2k])
                I("dve", "tensor_tensor", dict(out=q_tok[:, i, :], in0=tq2, in1=qg, op=ALU.mult),
                     reads=[tq2k, K_qg], writes=[K_qtok + "_%d" % i])
                for kc in range(8):
                    I("pe", "matmul", dict(out=ps[pb[2]][:, 0:24], lhsT=hT[:, kc, :], rhs=W_own[:, kc, 1536:1560],
                                                                  start=(kc == 0), stop=(kc == 7)),
                         reads=[hTk, K_Wown], writes=["ps%d" % pb[2]])
                I("act", "activation", dict(out=gates[:, i, :], in_=ps[pb[2]][:, 0:24], func=AF.Sigmoid),
                     reads=["ps%d" % pb[2]], writes=[K_gates + "_%d" % i])

            def stC(i):
                (xk, h, hk, hT, hTk, junk, jk, st, stk, pT, pb, ug, ugk, vg, vgk, vn, vnk, t1, t1k, t2, t2k, bn, bnk, tq1, tq1k, tq2, tq2k) = ctx[i]
                I("dve", "bn_stats", dict(out=bn[:, 0:6], in_=vg), reads=[vgk], writes=[bnk])
                I("dve", "bn_aggr", dict(out=bn[:, 8:10], in_=bn[:, 0:6]), reads=[bnk], writes=[bnk])
                I("act", "activation", dict(out=bn[:, 10:11], in_=bn[:, 9:10], func=AF.Sqrt, scale=1.0, bias=EPS), reads=[bnk], writes=[bnk])
                I("dve", "reciprocal", dict(out=bn[:, 11:12], in_=bn[:, 10:11]), reads=[bnk], writes=[bnk])
                I("dve", "tensor_scalar", dict(out=t1, in0=vg, scalar1=bn[:, 8:9], scalar2=bn[:, 11:12], op0=ALU.subtract, op1=ALU.mult),
                     reads=[vgk, bnk], writes=[t1k])
                I("dve", "tensor_tensor", dict(out=t1, in0=t1, in1=lng, op=ALU.mult), reads=[t1k, K_lng], writes=[t1k])
                I("dve", "tensor_tensor", dict(out=vn, in0=t1, in1=lnb, op=ALU.add), reads=[t1k, K_lnb], writes=[vnk])
                for g in range(8):
                    I("pe", "matmul", dict(out=ps[pb[1]][:, g * 64:(g + 1) * 64], lhsT=sgw[:, g, :], rhs=vn[:, g * 64:(g + 1) * 64],
                                                               start=True, stop=True),
                         reads=[vnk, K_sgw], writes=["ps%d" % pb[1]])
                I("dve", "tensor_tensor", dict(out=t2.rearrange("p (a b) -> p a b", b=64),
                                                              in0=ps[pb[1]][:, :].rearrange("p (a b) -> p a b", b=64),
                                                              in1=sgb.unsqueeze(2).to_broadcast([128, 8, 64]), op=ALU.add),
                     reads=["ps%d" % pb[1], K_sgb], writes=[t2k])
                I("dve", "tensor_tensor", dict(out=t2, in0=t2, in1=ug, op=ALU.mult), reads=[t2k, ugk], writes=[t2k])
                I("act", "activation", dict(out=t1, in_=t2, func=AF.Square, accum_out=bn[:, 12:13]), reads=[t2k], writes=[t1k, bnk])
                I("act", "activation", dict(out=bn[:, 13:14], in_=bn[:, 12:13], func=AF.Sqrt, scale=1.0 / 512, bias=EPS), reads=[bnk], writes=[bnk])
                I("dve", "reciprocal", dict(out=bn[:, 14:15], in_=bn[:, 13:14]), reads=[bnk], writes=[bnk])
                I("dve", "scalar_tensor_tensor", dict(out=a_tok[:, i, :], in0=t2, scalar=bn[:, 14:15], in1=mog0, op0=ALU.mult, op1=ALU.mult),
                     reads=[t2k, bnk, K_mog0], writes=[K_atok + "_%d" % i])


            for s_ in range(18):
                if s_ < 16:
                    stA(s_)
                if 0 <= s_ - 1 < 16:
                    stB(s_ - 1)
                if 0 <= s_ - 2 < 16:
                    stC(s_ - 2)

            dump("q_tok", q_tok, [128, 16, 512], [K_qtok + "_%d" % i for i in range(16)])
            dump("a_tok", a_tok, [128, 16, 512], [K_atok + "_%d" % i for i in range(16)])
            dump("gates", gates, [128, 16, 24], [K_gates + "_%d" % i for i in range(16)])
            if stop in ("P1", "%d:P1" % l):
                break
            P.barrier()
            AR.off = mark_kv
            KE, K_KE = AR.alloc([2, 4096], BF16, "KE")
            KW, K_KW = AR.alloc([4096], BF16, "KW")
            VS, K_VS = AR.alloc([32, 2, 65], BF16, "VS")
            VW, K_VW = AR.alloc([32, 2, 65], BF16, "VW")
            kcT, K_kcT = AR.alloc([2, 256], BF16, "kcT")
            vcov, K_vcov = AR.alloc([2, 2, 128], BF16, "vcov")
            mark_tr = AR.off
            cmpTk, K_cTk = AR.alloc([4096], BF16, "cmpTk")
            cmpTv, K_cTv = AR.alloc([4096], BF16, "cmpTv")
            mark_p3 = AR.off
            W_kv, K_Wkv = AR.alloc([8, 768], BF16, "W_kv")
            kg, K_kg = AR.alloc([256], F32, "kg")
            hs = Rot([AR.alloc([1024], BF16, "h") for _ in range(3)])
            hTs = Rot([AR.alloc([8, 128], BF16, "hT") for _ in range(2)])
            sts = Rot([AR.alloc([32], F32, "st") for _ in range(2)])
            kns = Rot([AR.alloc([256], BF16, "kn") for _ in range(2)])
            cbs = Rot([AR.alloc([256], BF16, "cb") for _ in range(2)])
            t1s = Rot([AR.alloc([256], F32, "t1") for _ in range(1)])
            t2s = t1s

            SK = os.environ.get("SKIP", "") if l == 0 else os.environ.get("SKIP1", "")
            wload(W_kv, D["w_in_p"][l, :, 1560:2328], K_Wkv, nsplit=2)
            gload(l, "kg", kg, K_kg)
            for hh in range(2):
                if "d" not in SK:
                    dma("pool", KE[64:128, hh, :], D["Econst"], writes=[K_KE + "E%d" % hh])
                if "e" not in SK:
                    dma("pool", vcov[:, :, hh, 64:128], D["ovc"], writes=[K_vcov + "ov%d" % hh])
            if "a" not in SK:
                I("dve", "memset", dict(ap=VS[:, :, :, 64:65], constant=1.0), writes=[K_VS + "one"])
                I("dve", "memset", dict(ap=VW[:, :, :, 64:65], constant=1.0), writes=[K_VW + "one"])
            if "f" not in SK:
                I("dve", "memset", dict(ap=kcT[0:64, :, :], constant=0.0), writes=[K_kcT])
                I("dve", "memset", dict(ap=vcov[:, :, :, 0:64], constant=0.0), writes=[K_vcov])

            for G in range(int(os.environ.get("NG", "32")) if l == 0 else int(os.environ.get("NG1", "32"))):
                h, hk = hs.next(); hT, hTk = hTs.next(); st, stk = sts.next()
                kn, knk = kns.next(); cb, cbk = cbs.next(); t1, t1k = t1s.next(); t2, t2k = t2s.next()
                io, rk = G // 2, G % 2
                dma("sp", h, hx_out[io // 4][rk * 512 + (io % 4) * 128: rk * 512 + (io % 4 + 1) * 128, :], reads=["hxout%d" % (io // 4)], writes=[hk],
                    stream="hin%d" % (G % 3))
                pT, pA, pB, pC = (0, 1, 2, 3) if G % 2 == 0 else (7, 4, 5, 6)
                transposes(h, hk, 8, hT, hTk, pT)
                if "g" in SK:
                    continue
                for kc in range(8):
                    I("pe", "matmul", dict(out=ps[pA][:, :], lhsT=hT[:, kc, :], rhs=W_kv[:, kc, 0:512], start=(kc == 0), stop=(kc == 7)),
                      reads=[hTk, K_Wkv], writes=["ps%d" % pA])
                for kc in range(8):
                    I("pe", "matmul", dict(out=ps[pB][:, 0:256], lhsT=hT[:, kc, :], rhs=W_kv[:, kc, 512:768], start=(kc == 0), stop=(kc == 7)),
                      reads=[hTk, K_Wkv], writes=["ps%d" % pB])
                I("act", "activation", dict(out=t1, in_=ps[pA][:, 0:256], func=AF.Square), reads=["ps%d" % pA], writes=[t1k])
                I("dve", "tensor_reduce", dict(out=st[:, 8:12], in_=t1.rearrange("p (a b) -> p a b", b=64), axis=AX.X, op=ALU.add),
                  reads=[t1k], writes=[stk + "q"])
                I("act", "activation", dict(out=st[:, 12:16], in_=st[:, 8:12], func=AF.Sqrt, scale=1.0 / 64, bias=EPS), reads=[stk + "q"], writes=[stk + "q"])
                I("dve", "reciprocal", dict(out=st[:, 16:20], in_=st[:, 12:16]), reads=[stk + "q"], writes=[stk + "q"])
                I("dve", "tensor_tensor", dict(out=t2.rearrange("p (a b) -> p a b", b=64), in0=ps[pA][:, 0:256].rearrange("p (a b) -> p a b", b=64),
                                               in1=st[:, 16:20].unsqueeze(2).to_broadcast([128, 4, 64]), op=ALU.mult),
                  reads=["ps%d" % pA, stk + "q"], writes=[t2k])
                I("dve", "tensor_tensor", dict(out=kn, in0=t2, in1=kg, op=ALU.mult), reads=[t2k, K_kg], writes=[knk])
                if "b" not in SK:
                    I("act", "activation", dict(out=VS[:, G, :, 0:64], in_=ps[pA][:, 256:384].rearrange("p (a b) -> p a b", b=64), func=AF.Copy),
                      reads=["ps%d" % pA], writes=[K_VS + "_%d" % G])
                    I("act", "activation", dict(out=VW[:, G, :, 0:64], in_=ps[pA][:, 384:512].rearrange("p (a b) -> p a b", b=64), func=AF.Copy),
                      reads=["ps%d" % pA], writes=[K_VW + "_%d" % G])
                I("act", "activation", dict(out=cb, in_=ps[pB][:, 0:256], func=AF.Copy), reads=["ps%d" % pB], writes=[cbk])
                if "h" in SK:
                    continue
                pst = ps[pC][:, :].bitcast(BF16)
                for c in range(2):
                    I("pe", "transpose", dict(out=pst[:, c * 128:(c + 1) * 128], in_=kn[:, c * 128:(c + 1) * 128], identity=identb[:]),
                      reads=[knk, "identf"], writes=["ps%d" % pC])
                for c in range(2):
                    I("pe", "transpose", dict(out=pst[:, (2 + c) * 128:(3 + c) * 128], in_=cb[:, c * 128:(c + 1) * 128], identity=identb[:]),
                      reads=[cbk, "identf"], writes=["ps%d" % pC])
                tl = slice(G * 128, (G + 1) * 128)
                if "i" in SK:
                    continue
                I("dve", "tensor_copy", dict(out=KE[0:64, 0, tl], in_=pst[0:64, 0:128]), reads=["ps%d" % pC], writes=[K_KE + "_%d" % G])
                if "c" not in SK:
                    I("dve", "tensor_copy", dict(out=KE[0:64, 1, tl], in_=pst[64:128, 0:128]), reads=["ps%d" % pC], writes=[K_KE + "_%d" % G])
                if "j" in SK:
                    continue
                if "k" not in SK:
                    I("dve", "tensor_copy", dict(out=KW[:, tl], in_=pst[:, 128:256]), reads=["ps%d" % pC], writes=[K_KW + "_%d" % G])
                if "m" not in SK:
                    I("dve", "tensor_copy", dict(out=cmpTk[:, tl], in_=pst[:, 256:384]), reads=["ps%d" % pC], writes=[K_cTk])
                if "n" not in SK:
                    I("dve", "tensor_copy", dict(out=cmpTv[:, tl], in_=pst[:, 384:512]), reads=["ps%d" % pC], writes=[K_cTv])
            dump("KE", KE, [128, 2, 4096], [K_KE + "_%d" % G for G in range(32)] + [K_KE + "E0", K_KE + "E1"])
            dump("KW", KW, [128, 4096], [K_KW + "_%d" % G for G in range(32)])
            dump("VS", VS, [128, 32, 2, 65], [K_VS + "_%d" % G for G in range(32)] + [K_VS + "one"])
            dump("VW", VW, [128, 32, 2, 65], [K_VW + "_%d" % G for G in range(32)] + [K_VW + "one"])
            if stop in ("P2", "%d:P2" % l):
                break
            P.barrier()
            AR.off = mark_p3
            w1, K_w1 = AR.alloc([32, 256], BF16, "w1")
            posT, K_posT = AR.alloc([32], BF16, "posT")
            b1T, K_b1T = AR.alloc([2], F32, "b1T")
            c1b, K_c1b = AR.alloc([2], F32, "c1b")
            w2, K_w2 = AR.alloc([2, 64], BF16, "w2")
            h1T, K_h1T = AR.alloc([2, 2, 256], BF16, "h1T")
            b2t, K_b2t = AR.alloc([64], F32, "b2t")
            kcgt, K_kcgt = AR.alloc([64], F32, "kcgt")
            t1s = Rot([AR.alloc([64], F32, "t1") for _ in range(2)])
            t2s = Rot([AR.alloc([64], BF16, "t2") for _ in range(2)])
            sts = Rot([AR.alloc([8], F32, "st") for _ in range(2)])
            junk, jk = AR.alloc([64], F32, "junk")
            gload(l, "kcg", kcgt, K_kcgt)
            I("dve", "memset", dict(ap=h1T, constant=0.0), writes=[K_h1T])
            bank = Rot([0, 1, 2, 3, 4, 5, 6, 7])
            P.groups[K_w1] = [K_w1 + "#%d_%d" % (cp, s4) for cp in range(2) for s4 in range(4)]
            for kvi in range(2):
                cT, cTk = (cmpTk, K_cTk) if kvi == 0 else (cmpTv, K_cTv)
                w1src = D["cmp_w1"][l, kvi].rearrange("(s d) j -> d s j", d=64)
                for cp in range(2):
                    for s4 in range(4):
                        dma("pool", w1[cp * 64:(cp + 1) * 64, s4 * 8:(s4 + 1) * 8, :], w1src[:, s4 * 8:(s4 + 1) * 8, :], writes=[K_w1 + "#%d_%d" % (cp, s4)])
                    dma("pool", posT[cp * 64:(cp + 1) * 64, :], D["cmp_posT"][l, kvi], writes=[K_posT])
                dma("sp", b1T, D["cmp_b1T"][l, kvi], writes=[K_b1T])
                dma("pool", w2, D["cmp_w2"][l, kvi].rearrange("(jc j) d -> j jc d", j=128), writes=[K_w2])
                gload(l, "b2k" if kvi == 0 else "b2v", b2t, K_b2t)
                for jc in range(2):
                    b = bank.next()
                    for s in range(32):
                        I("pe", "matmul", dict(out=ps[b][:, 0:1], lhsT=w1[0:64, s, jc * 128:(jc + 1) * 128], rhs=posT[0:64, s:s + 1],
                                               start=(s == 0), stop=(s == 31)), reads=[K_w1, K_posT], writes=["ps%d" % b])
                    I("dve", "tensor_tensor", dict(out=c1b[:, jc:jc + 1], in0=ps[b][:, 0:1], in1=b1T[:, jc:jc + 1], op=ALU.add),
                      reads=["ps%d" % b, K_b1T], writes=[K_c1b])
                for hh in range(2):
                    for jc in range(2):
                        b = bank.next()
                        for s in range(32):
                            I("pe", "matmul", dict(out=ps[b][:, 0:255], lhsT=w1[hh * 64:(hh + 1) * 64, s, jc * 128:(jc + 1) * 128],
                                                   rhs=cT[hh * 64:(hh + 1) * 64, s:s + 16 * 254 + 1:16], start=(s == 0), stop=(s == 31)),
                              reads=[K_w1, cTk], writes=["ps%d" % b])
                        I("act", "activation", dict(out=h1T[:, hh, jc, 0:255], in_=ps[b][:, 0:255], func=AF.Gelu_apprx_tanh, bias=c1b[:, jc:jc + 1]),
                          reads=["ps%d" % b, K_c1b], writes=[K_h1T])
                for hh in range(2):
                    for nt in range(2):
                        b = bank.next()
                        t1, t1k = t1s.next(); t2, t2k = t2s.next(); st, stk = sts.next()
                        for jc in range(2):
                            I("pe", "matmul", dict(out=ps[b][:, 0:64], lhsT=h1T[:, hh, jc, nt * 128:(nt + 1) * 128], rhs=w2[:, jc, :],
                                                   start=(jc == 0), stop=(jc == 1)), reads=[K_h1T, K_w2], writes=["ps%d" % b])
                        if kvi == 1:
                            I("dve", "tensor_tensor", dict(out=vcov[:, nt, hh, 0:64], in0=ps[b][:, 0:64], in1=b2t, op=ALU.add),
                              reads=["ps%d" % b, K_b2t], writes=[K_vcov])
                        else:
                            I("dve", "tensor_tensor", dict(out=t1, in0=ps[b][:, 0:64], in1=b2t, op=ALU.add), reads=["ps%d" % b, K_b2t], writes=[t1k])
                            rstd = rms_rstd(t1, 64, t1k, junk, jk, st, stk, 0)
                            I("dve", "scalar_tensor_tensor", dict(out=t2, in0=t1, scalar=rstd, in1=kcgt, op0=ALU.mult, op1=ALU.mult),
                              reads=[t1k, stk, K_kcgt], writes=[t2k])
                            b2_ = bank.next()
                            pst = ps[b2_][:, :].bitcast(BF16)
                            I("pe", "transpose", dict(out=pst[0:64, 0:128], in_=t2, identity=identb[:]), reads=[t2k, "identf"], writes=["ps%d" % b2_])
                            I("act", "activation", dict(out=kcT[0:64, hh, nt * 128:(nt + 1) * 128], in_=pst[0:64, 0:128], func=AF.Copy),
                              reads=["ps%d" % b2_], writes=[K_kcT])
            dump("kcT", kcT[0:64, :, :], [64, 2, 256], [K_kcT])
            dump("vcov", vcov, [128, 2, 2, 128], [K_vcov, K_vcov + "ov0", K_vcov + "ov1"])
            if stop in ("P3", "%d:P3" % l):
                break
            P.barrier()
            AR.off = mark_tr
            W_out, K_Wout = AR.alloc([8, 1024], BF16, "W_out")
            mog1, K_mog1 = AR.alloc([512], F32, "mog1")
            QBs = [Rot([AR.alloc([512], BF16, "QB%d" % hh) for _ in range(2)]) for hh in range(2)]
            QW1s = Rot([AR.alloc([512], BF16, "QW1") for _ in range(2)])
            PTs = Rot([AR.alloc([512], BF16, "PT") for _ in range(7)])
            oaccs = Rot([AR.alloc([512], F32, "oacc") for _ in range(2)])
            tmps = Rot([AR.alloc([256], F32, "tmp") for _ in range(2)])
            imp3s = Rot([AR.alloc([256], F32, "imp3") for _ in range(2)])
            smalls = Rot([AR.alloc([256], F32, "small") for _ in range(2)])
            bnegs = Rot([AR.alloc([64], BF16, "bneg") for _ in range(2)])
            bns = Rot([AR.alloc([512], BF16, "b_n") for _ in range(2)])
            mixTs = Rot([AR.alloc([8, 128], BF16, "mixT") for _ in range(2)])
            junk, jk = AR.alloc([512], BF16, "junk")
            sts = Rot([AR.alloc([8], F32, "st") for _ in range(2)])
            wload(W_out, D["w_out"][l], K_Wout, nsplit=4)
            gload(l, "mog1", mog1, K_mog1)
            sbank = Rot([0, 1, 2, 7])
            LOOKAHEAD = int(os.environ.get("LOOKAHEAD", "3"))
            pst6 = ps[6][:, :].bitcast(BF16)
            NT4 = int(os.environ.get("NT4", "16"))
            for i in range(NT4):
                xk = "x%d" % i
                oacc, oak = oaccs.next()
                QB = []
                for hh in range(2):
                    QB.append(QBs[hh].next())
                QW1, QW1k = QW1s.next()
                for c in range(4):
                    I("pe", "transpose", dict(out=pst6[:, c * 128:(c + 1) * 128], in_=q_tok[:, i, c * 128:(c + 1) * 128], identity=identb[:]),
                      reads=[K_qtok + "_%d" % i, "identf"], writes=["ps6q"])
                for hh in range(2):
                    qb, qbk = QB[hh]
                    qv = qb[0:64, :].rearrange("p (c e t) -> p c e t", c=2, e=2)
                    src_e = pst6[0:64, hh * 256:(hh + 1) * 256].rearrange("p (c t) -> p c t", c=2)
                    src_o = pst6[64:128, hh * 256:(hh + 1) * 256].rearrange("p (c t) -> p c t", c=2)
                    I("dve", "tensor_copy", dict(out=qv[:, :, 0, :], in_=src_e), reads=["ps6q"], writes=[qbk + "Q"])
                    I("dve", "tensor_copy", dict(out=qv[:, :, 1, :], in_=src_o), reads=["ps6q"], writes=[qbk + "Q"])
                    if hh == 1:
                        qv1 = QW1[64:128, :].rearrange("p (c e t) -> p c e t", c=2, e=2)
                        I("dve", "tensor_copy", dict(out=qv1[:, :, 0, :], in_=src_e), reads=["ps6q"], writes=[QW1k])
                        I("dve", "tensor_copy", dict(out=qv1[:, :, 1, :], in_=src_o), reads=["ps6q"], writes=[QW1k])
                gview = gates[:, i, :].rearrange("p (hd r) -> p hd r", r=3)

                def finish_branch(acc_view64, den_ap, r, hh, first, sm, smk, tmp, tmpk, deps):
                    I("dve", "tensor_scalar", dict(out=sm[:, 8:12], in0=den_ap, scalar1=1e-30, scalar2=None, op0=ALU.max),
                      reads=deps, writes=[smk + "f"])
                    I("dve", "reciprocal", dict(out=sm[:, 12:16], in_=sm[:, 8:12]), reads=[smk + "f"], writes=[smk + "f"])
                    I("dve", "tensor_tensor", dict(out=sm[:, 16:20], in0=sm[:, 12:16], in1=gview[:, 4 * hh:4 * hh + 4, r], op=ALU.mult),
                      reads=[smk + "f", K_gates + "_%d" % i], writes=[smk + "f"])
                    ov_ = oacc[:, hh * 256:(hh + 1) * 256].rearrange("p (g d) -> p g d", d=64)
                    cf = sm[:, 16:20].unsqueeze(2).to_broadcast([128, 4, 64])
                    if first:
                        I("dve", "tensor_tensor", dict(out=ov_, in0=acc_view64, in1=cf, op=ALU.mult), reads=deps + [smk + "f"], writes=[oak + "_%d" % hh])
                    else:
                        tv = tmp.rearrange("p (g d) -> p g d", d=64)
                        I("dve", "tensor_tensor", dict(out=tv, in0=acc_view64, in1=cf, op=ALU.mult), reads=deps + [smk + "f"], writes=[tmpk])
                        I("dve", "tensor_tensor", dict(out=ov_, in0=ov_, in1=tv, op=ALU.add), reads=[tmpk, oak + "_%d" % hh], writes=[oak + "_%d" % hh])

                for hh in range(2):
                    qb, qbk = QB[hh]
                    sm, smk = smalls.next(); tmp, tmpk = tmps.next(); imp3, imp3k = imp3s.next(); bneg, bnegk = bnegs.next()
                    ptc = []
                    for nt in range(2):
                        sbk = sbank.next()
                        w0 = nt * 128 - 16 * i + 240
                        I("pe", "matmul", dict(out=ps[sbk][:, :], lhsT=kcT[0:64, hh, nt * 128:(nt + 1) * 128], rhs=qb[0:64, :], start=True, stop=False),
                          reads=[K_kcT, qbk + "Q"], writes=["ps%d" % sbk])
                        I("pe", "matmul", dict(out=ps[sbk][:, :], lhsT=Astat[0:64, w0:w0 + 128], rhs=Cst[0:64, :], start=False, stop=True),
                          reads=["Astat", "Cst"], writes=["ps%d" % sbk])
                        pt, ptk = PTs.next()
                        I("act", "activation", dict(out=pt, in_=ps[sbk][:, :], func=AF.Exp, scale=0.125), reads=["ps%d" % sbk], writes=[ptk])
                        ptc.append((pt, ptk))
                    for g in range(4):
                        for nt in range(2):
                            pt, ptk = ptc[nt]
                            I("pe", "matmul", dict(out=ps[3][:, g * 128:(g + 1) * 128], lhsT=pt[:, g * 128:(g + 1) * 128], rhs=vcov[:, nt, hh, :],
                                                   start=(nt == 0), stop=(nt == 1)), reads=[ptk, K_vcov, K_vcov + "ov%d" % hh], writes=["ps3"])
                    acc3 = ps[3][:, :].rearrange("p (g c) -> p g c", c=128)
                    I("dve", "tensor_reduce", dict(out=sm[:, 0:4], in_=acc3[:, :, 64:128], axis=AX.X, op=ALU.add), reads=["ps3"], writes=[smk + "d"])
                    I("dve", "tensor_scalar", dict(out=sm[:, 4:8], in0=sm[:, 0:4], scalar1=1e-30, scalar2=None, op0=ALU.max), reads=[smk + "d"], writes=[smk + "d"])
                    I("dve", "reciprocal", dict(out=sm[:, 0:4], in_=sm[:, 4:8]), reads=[smk + "d"], writes=[smk + "d"])
                    i3v = imp3.rearrange("p (g j) -> p g j", j=64)
                    I("dve", "tensor_tensor", dict(out=i3v, in0=acc3[:, :, 64:128], in1=sm[:, 0:4].unsqueeze(2).to_broadcast([128, 4, 64]), op=ALU.mult),
                      reads=["ps3", smk + "d"], writes=[imp3k])
                    I("dve", "tensor_reduce", dict(out=sm[:, 64:128], in_=imp3.rearrange("p (g j) -> p j g", j=64), axis=AX.X, op=ALU.add),
                      reads=[imp3k], writes=[smk + "i"])
                    I("dve", "tensor_tensor", dict(out=sm[:, 64:128], in0=sm[:, 64:128], in1=Mvalid[:, i, :], op=ALU.mult), reads=[smk + "i", "Mvalid"], writes=[smk + "i"])
                    I("dve", "tensor_tensor", dict(out=sm[:, 64:128], in0=sm[:, 64:128], in1=Madd[:, i, :], op=ALU.add), reads=[smk + "i", "Madd"], writes=[smk + "i"])
                    I("dve", "max", dict(out=sm[:, 32:40], in_=sm[:, 64:128]), reads=[smk + "i"], writes=[smk + "m"])
                    I("dve", "match_replace", dict(out=sm[:, 128:192], in_to_replace=sm[:, 32:40], in_values=sm[:, 64:128], imm_value=-1e30),
                      reads=[smk + "i", smk + "m"], writes=[smk + "w"])
                    I("dve", "max", dict(out=sm[:, 40:48], in_=sm[:, 128:192]), reads=[smk + "w"], writes=[smk + "m"])
                    I("dve", "tensor_scalar", dict(out=bneg, in0=sm[:, 64:128], scalar1=sm[:, 47:48], scalar2=NEG, op0=ALU.is_lt, op1=ALU.mult),
                      reads=[smk + "i", smk + "m"], writes=[bnegk])
                    if i == 0:
                        dump("impm%d" % hh, sm[:, 64:128], [128, 64], [smk + "i"])
                        dump("bneg%d" % hh, bneg, [128, 64], [bnegk])
                    blk = 4 + hh
                    I("pe", "transpose", dict(out=pst6[0:64, blk * 128:(blk + 1) * 128], in_=bneg, identity=identb[:]), reads=[bnegk, "identf"], writes=["ps6b%d" % hh])
                    I("dve", "tensor_copy", dict(out=qb[64:128, :].rearrange("p (g t) -> p g t", g=4),
                                                 in_=pst6[0:64, blk * 128:(blk + 1) * 128].unsqueeze(1).to_broadcast([64, 4, 128])),
                      reads=["ps6b%d" % hh], writes=[qbk + "B"])
                    finish_branch(acc3[:, :, 0:64], sm[:, 4:8], 0, hh, True, sm, smk, tmp, tmpk, ["ps3", smk + "d"])
                    def run_branch(accb, tiles, qk, mask_of, vsrc, vkeys):
                        I("pe", "matmul", dict(out=ps[accb][:, 0:260], lhsT=zerob[:, 0:128], rhs=zerob[:, 0:260], start=True, stop=False),
                          reads=["zerob"], writes=["ps%d" % accb])
                        pend = []
                        for n_, kt in enumerate(tiles):
                            sbk = sbank.next()
                            lhsT, lreads, rhs, rreads = qk(kt)
                            I("pe", "matmul", dict(out=ps[sbk][:, :], lhsT=lhsT, rhs=rhs, start=True, stop=True), reads=lreads + rreads, writes=["ps%d" % sbk])
                            pt, ptk = PTs.next()
                            I("act", "activation", dict(out=pt, in_=ps[sbk][:, :], func=AF.Exp, scale=0.125), reads=["ps%d" % sbk], writes=[ptk])
                            mk = mask_of(n_, kt)
                            if mk is not None:
                                I("pool", "tensor_tensor", dict(out=pt, in0=pt, in1=mk[0], op=ALU.mult), reads=[ptk, mk[1]], writes=[ptk])
                            pend.append((pt, ptk, kt))
                            if len(pend) > LOOKAHEAD:
                                pv(accb, pend.pop(0), vsrc, vkeys, False)
                        while pend:
                            pv(accb, pend.pop(0), vsrc, vkeys, len(pend) == 0)

                    def pv(accb, pend, vsrc, vkeys, last):
                        pt, ptk, kt = pend
                        for g in range(4):
                            I("pe", "matmul", dict(out=ps[accb][:, g * 65:(g + 1) * 65], lhsT=pt[:, g * 128:(g + 1) * 128], rhs=vsrc[:, kt, hh, :],
                                                   start=False, stop=(last and g == 3)),
                              reads=[ptk, vkeys[0] + "_%d" % kt, vkeys[0] + "one"], writes=["ps%d" % accb])

                    kts = [(o, 2 * i - 4 + o) for o in range(6) if 2 * i - 4 + o >= 0]
                    o_of = {kt: o for (o, kt) in kts}
                    rq = qb[0:64, :] if hh == 0 else QW1[64:128, :]
                    rqk = (qbk + "Q") if hh == 0 else QW1k

                    def qk_w(kt):
                        return KW[hh * 64:(hh + 1) * 64, kt * 128:(kt + 1) * 128], [K_KW + "_%d" % kt], rq, [rqk]

                    def mask_w(n_, kt):
                        o = o_of[kt]
                        if o in (0, 1, 4, 5):
                            return maskW[:, {0: 0, 1: 1, 4: 2, 5: 3}[o], :], "maskW"
                        return None
                    run_branch(5, [kt for (_, kt) in kts], qk_w, mask_w, VW, [K_VW])
                    acc5 = ps[5][:, 0:260].rearrange("p (g c) -> p g c", c=65)
                    finish_branch(acc5[:, :, 0:64], acc5[:, :, 64], 2, hh, False, sm, smk, tmp, tmpk, ["ps5"])
                    nkt = 2 * i + 2

                    def qk_s(kt):
                        return KE[:, hh, kt * 128:(kt + 1) * 128], [K_KE + "_%d" % kt, K_KE + "E%d" % hh], qb[:, :], [qbk + "Q", qbk + "B"]

                    def mask_s(n_, kt):
                        if kt >= 2 * i:
                            return maskD[:, kt - 2 * i, :], "maskD"
                        return None
                    run_branch(4, list(range(nkt)), qk_s, mask_s, VS, [K_VS])
                    acc4 = ps[4][:, 0:260].rearrange("p (g c) -> p g c", c=65)
                    finish_branch(acc4[:, :, 0:64], acc4[:, :, 64], 1, hh, False, sm, smk, tmp, tmpk, ["ps4"])
                if i == 0 or dbg is not None:
                    pass
                b_n, bnk = bns.next(); mixT, mixTk = mixTs.next(); st, stk = sts.next()
                okeys = [oak + "_0", oak + "_1"]
                I("act", "activation", dict(out=junk, in_=oacc, func=AF.Square, accum_out=st[:, 0:1]), reads=okeys, writes=[jk, stk])
                I("act", "activation", dict(out=st[:, 1:2], in_=st[:, 0:1], func=AF.Sqrt, scale=1.0 / 512, bias=EPS), reads=[stk], writes=[stk])
                I("dve", "reciprocal", dict(out=st[:, 2:3], in_=st[:, 1:2]), reads=[stk], writes=[stk])
                I("dve", "scalar_tensor_tensor", dict(out=b_n, in0=oacc, scalar=st[:, 2:3], in1=mog1, op0=ALU.mult, op1=ALU.mult),
                  reads=okeys + [stk, K_mog1], writes=[bnk])
                if dbg is not None and "b_tok" in dbg:
                    if i == 0:
                        dbg_b = nc.dram_tensor("dbg_b_tok", [128, 16, 512], F32, kind="ExternalOutput").ap()
                    dma("sp", dbg_b[:, i, :], oacc, reads=okeys, stream="dbg")
                k6 = ["ps6q", "ps6b0", "ps6b1"]
                for c in range(4):
                    I("pe", "transpose", dict(out=pst6[:, c * 128:(c + 1) * 128], in_=a_tok[:, i, c * 128:(c + 1) * 128], identity=identb[:]),
                      reads=[K_atok + "_%d" % i, "identf"], writes=k6)
                for c in range(4):
                    I("pe", "transpose", dict(out=pst6[:, (4 + c) * 128:(5 + c) * 128], in_=b_n[:, c * 128:(c + 1) * 128], identity=identb[:]),
                      reads=[bnk, "identf"], writes=k6)
                I("dve", "tensor_copy", dict(out=mixT, in_=pst6[:, :].rearrange("p (a b) -> p a b", b=128)), reads=k6, writes=[mixTk])
                for hf in range(2):
                    yb = sbank.next()
                    for kc in range(8):
                        I("pe", "matmul", dict(out=ps[yb][:, :], lhsT=mixT[:, kc, :], rhs=W_out[:, kc, hf * 512:(hf + 1) * 512], start=(kc == 0), stop=(kc == 7)),
                          reads=[mixTk, K_Wout], writes=["ps%d" % yb])
                    I("dve", "tensor_tensor", dict(out=x_res[:, i, hf * 512:(hf + 1) * 512], in0=ps[yb][:, :], in1=x_res[:, i, hf * 512:(hf + 1) * 512], op=ALU.add),
                      reads=["ps%d" % yb, xk], writes=[xk])
            dump("x1", x_res[:, :, :], [128, 16, 1024], ["x%d" % i for i in range(16)])
            if stop in ("P4", "%d:P4" % l):
                break
            P.barrier()
            AR.off = 0
            W_mkv, K_Wmkv = AR.alloc([8, 1024], BF16, "W_mkv")
            W_mq, K_Wmq = AR.alloc([8, 512], BF16, "W_mq")
            W_mo, K_Wmo = AR.alloc([4, 1024], BF16, "W_mo")
            gmkv, K_gmkv = AR.alloc([1024], F32, "gmkv")
            gmem, K_gmem = AR.alloc([1024], F32, "gmem")
            mqg, K_mqg = AR.alloc([512], F32, "mqg")
            mkg, K_mkg = AR.alloc([512], F32, "mkg")
            hTm, K_hTm = AR.alloc([8, 256], BF16, "hTm")
            KmT, K_KmT = AR.alloc([4, 256], BF16, "KmT")
            Vm, K_Vm = AR.alloc([2, 4, 129], BF16, "Vm")
            xins = Rot([AR.alloc([1024], F32, "xin") for _ in range(2)])
            hs = Rot([AR.alloc([1024], BF16, "h") for _ in range(2)])
            hTs = Rot([AR.alloc([8, 128], BF16, "hT") for _ in range(3)])
            junk, jk = AR.alloc([1024], BF16, "junk")
            sts = Rot([AR.alloc([32], F32, "st") for _ in range(4)])
            t1s = Rot([AR.alloc([512], F32, "t1") for _ in range(3)])
            qns = Rot([AR.alloc([512], BF16, "qn") for _ in range(3)])
            QmTs = Rot([AR.alloc([4, 128], BF16, "QmT") for _ in range(3)])
            PTs = Rot([AR.alloc([512], BF16, "PTm") for _ in range(4)])
            oms = Rot([AR.alloc([512], BF16, "om") for _ in range(2)])
            omTs = Rot([AR.alloc([4, 128], BF16, "omT") for _ in range(2)])
            wload(W_mkv, D["w_mkv"][l], K_Wmkv, nsplit=4)
            wload(W_mq, D["w_mq"][l], K_Wmq, nsplit=2)
            wload(W_mo, D["w_mo"][l], K_Wmo, nsplit=2)
            gload(l, "mkvg", gmkv, K_gmkv); gload(l, "memg", gmem, K_gmem); gload(l, "mqg", mqg, K_mqg); gload(l, "mkg", mkg, K_mkg)
            I("dve", "memset", dict(ap=Vm[:, :, :, 128:129], constant=1.0), writes=[K_Vm + "one"])

            def head_rms128(psb, gt, gk, st, stk, t1, t1k, out_bf, outk):
                I("act", "activation", dict(out=t1, in_=ps[psb][:, :], func=AF.Square), reads=["ps%d" % psb], writes=[t1k])
                I("dve", "tensor_reduce", dict(out=st[:, 8:12], in_=t1.rearrange("p (a b) -> p a b", b=128), axis=AX.X, op=ALU.add), reads=[t1k], writes=[stk + "q"])
                I("act", "activation", dict(out=st[:, 12:16], in_=st[:, 8:12], func=AF.Sqrt, scale=1.0 / 128, bias=EPS), reads=[stk + "q"], writes=[stk + "q"])
                I("dve", "reciprocal", dict(out=st[:, 16:20], in_=st[:, 12:16]), reads=[stk + "q"], writes=[stk + "q"])
                I("dve", "tensor_tensor", dict(out=t1.rearrange("p (a b) -> p a b", b=128), in0=ps[psb][:, :].rearrange("p (a b) -> p a b", b=128),
                                               in1=st[:, 16:20].unsqueeze(2).to_broadcast([128, 4, 128]), op=ALU.mult), reads=["ps%d" % psb, stk + "q"], writes=[t1k])
                I("dve", "tensor_tensor", dict(out=out_bf, in0=t1, in1=gt, op=ALU.mult), reads=[t1k, gk], writes=[outk])

            for mt in range(2):
                xin, xk_ = xins.next(); h, hk = hs.next(); hT, hTk = hTs.next(); st, stk = sts.next(); t1, t1k = t1s.next(); qn, qnk = qns.next()
                dma("sp", xin, D["mem"][mt * 128:(mt + 1) * 128, :], writes=[xk_], stream="xin%d" % mt)
                norm_to_hT(xin, xk_, gmkv, K_gmkv, h, hk, hT, hTk, junk, jk, st, stk, 6)
                I("dve", "tensor_copy", dict(out=hTm[:, :, mt * 128:(mt + 1) * 128], in_=hT), reads=[hTk], writes=[K_hTm])
                for hf in range(2):
                    for kc in range(8):
                        I("pe", "matmul", dict(out=ps[hf][:, :], lhsT=hTm[:, kc, mt * 128:(mt + 1) * 128], rhs=W_mkv[:, kc, hf * 512:(hf + 1) * 512],
                                               start=(kc == 0), stop=(kc == 7)), reads=[K_hTm, K_Wmkv], writes=["ps%d" % hf])
                head_rms128(0, mkg, K_mkg, st, stk, t1, t1k, qn, qnk)
                pst = ps[7][:, :].bitcast(BF16)
                for c in range(4):
                    I("pe", "transpose", dict(out=pst[:, c * 128:(c + 1) * 128], in_=qn[:, c * 128:(c + 1) * 128], identity=identb[:]), reads=[qnk, "identf"], writes=["ps7"])
                I("dve", "tensor_copy", dict(out=KmT[:, :, mt * 128:(mt + 1) * 128], in_=pst[:, 0:512].rearrange("p (a b) -> p a b", b=128)), reads=["ps7"], writes=[K_KmT])
                I("act", "activation", dict(out=Vm[:, mt, :, 0:128], in_=ps[1][:, :].rearrange("p (a b) -> p a b", b=128), func=AF.Copy), reads=["ps1"], writes=[K_Vm])
            sc_m = 128.0 ** -0.5
            ctx5 = {}
            for i in range(16):
                h, hk = hs.next(); hT, hTk = hTs.next(); st, stk = sts.next(); t1, t1k = t1s.next(); qn, qnk = qns.next()
                QmT, QmTk = QmTs.next(); om, omk = oms.next(); omT, omTk = omTs.next()
                ctx5[i] = ("x%d" % i, h, hk, hT, hTk, st, stk, t1, t1k, qn, qnk, QmT, QmTk, om, omk, omT, omTk)

            def s5A(i):
                (xk, h, hk, hT, hTk, st, stk, t1, t1k, qn, qnk, QmT, QmTk, om, omk, omT, omTk) = ctx5[i]
                norm_to_hT(x_res[:, i, :], xk, gmem, K_gmem, h, hk, hT, hTk, junk, jk, st, stk, 6)

            def s5B(i):
                (xk, h, hk, hT, hTk, st, stk, t1, t1k, qn, qnk, QmT, QmTk, om, omk, omT, omTk) = ctx5[i]
                for kc in range(8):
                    I("pe", "matmul", dict(out=ps[0][:, :], lhsT=hT[:, kc, :], rhs=W_mq[:, kc, :], start=(kc == 0), stop=(kc == 7)), reads=[hTk, K_Wmq], writes=["ps0"])
                head_rms128(0, mqg, K_mqg, st, stk, t1, t1k, qn, qnk)
                pst = ps[7][:, :].bitcast(BF16)
                for c in range(4):
                    I("pe", "transpose", dict(out=pst[:, c * 128:(c + 1) * 128], in_=qn[:, c * 128:(c + 1) * 128], identity=identb[:]), reads=[qnk, "identf"], writes=["ps7"])
                I("dve", "tensor_copy", dict(out=QmT, in_=pst[:, 0:512].rearrange("p (a b) -> p a b", b=128)), reads=["ps7"], writes=[QmTk])

            def s5C(i):
                (xk, h, hk, hT, hTk, st, stk, t1, t1k, qn, qnk, QmT, QmTk, om, omk, omT, omTk) = ctx5[i]
                ptm = []
                for mt in range(2):
                    sbk = 1 + mt
                    for hd in range(4):
                        I("pe", "matmul", dict(out=ps[sbk][:, hd * 128:(hd + 1) * 128], lhsT=KmT[:, hd, mt * 128:(mt + 1) * 128], rhs=QmT[:, hd, :], start=True, stop=True),
                          reads=[K_KmT, QmTk], writes=["ps%d" % sbk])
                    pt, ptk = PTs.next()
                    I("act", "activation", dict(out=pt, in_=ps[sbk][:, :], func=AF.Exp, scale=sc_m), reads=["ps%d" % sbk], writes=[ptk])
                    ptm.append((pt, ptk))
                for hd in range(4):
                    ab = 3 + hd // 2
                    c0 = (hd % 2) * 129
                    for mt in range(2):
                        pt, ptk = ptm[mt]
                        I("pe", "matmul", dict(out=ps[ab][:, c0:c0 + 129], lhsT=pt[:, hd * 128:(hd + 1) * 128], rhs=Vm[:, mt, hd, :], start=(mt == 0), stop=(mt == 1)),
                          reads=[ptk, K_Vm, K_Vm + "one"], writes=["ps%d" % ab])
                for ab in (3, 4):
                    accv = ps[ab][:, 0:258].rearrange("p (a c) -> p a c", c=129)
                    so = 20 + (ab - 3) * 2
                    I("dve", "reciprocal", dict(out=st[:, so:so + 2], in_=accv[:, :, 128]), reads=["ps%d" % ab], writes=[stk + "m%d" % ab])
                    I("dve", "tensor_tensor", dict(out=om[:, (ab - 3) * 256:(ab - 2) * 256].rearrange("p (a c) -> p a c", c=128), in0=accv[:, :, 0:128],
                                                   in1=st[:, so:so + 2].unsqueeze(2).to_broadcast([128, 2, 128]), op=ALU.mult),
                      reads=["ps%d" % ab, stk + "m%d" % ab], writes=[omk])
                pst5 = ps[5][:, :].bitcast(BF16)
                for c in range(4):
                    I("pe", "transpose", dict(out=pst5[:, c * 128:(c + 1) * 128], in_=om[:, c * 128:(c + 1) * 128], identity=identb[:]), reads=[omk, "identf"], writes=["ps5"])
                I("dve", "tensor_copy", dict(out=omT, in_=pst5[:, 0:512].rearrange("p (a b) -> p a b", b=128)), reads=["ps5"], writes=[omTk])
                for hf in range(2):
                    yb = 1 + hf
                    for kc in range(4):
                        I("pe", "matmul", dict(out=ps[yb][:, :], lhsT=omT[:, kc, :], rhs=W_mo[:, kc, hf * 512:(hf + 1) * 512], start=(kc == 0), stop=(kc == 3)),
                          reads=[omTk, K_Wmo], writes=["ps%d" % yb])
                    I("dve", "tensor_tensor", dict(out=x_res[:, i, hf * 512:(hf + 1) * 512], in0=ps[yb][:, :], in1=x_res[:, i, hf * 512:(hf + 1) * 512], op=ALU.add),
                      reads=["ps%d" % yb, xk], writes=[xk])


            for s_ in range(18):
                if s_ < 16:
                    s5A(s_)
                if 0 <= s_ - 1 < 16:
                    s5B(s_ - 1)
                if 0 <= s_ - 2 < 16:
                    s5C(s_ - 2)

            dump("x2", x_res[:, :, :], [128, 16, 1024], ["x%d" % i for i in range(16)])
            if stop in ("P5", "%d:P5" % l):
                break
            P.barrier()
            AR.off = 0
            hTa, K_hTa = AR.alloc([8, 2048], BF16, "hTa")
            hidT, K_hidT = AR.alloc([4, 2048], BF16, "hidT")
            W1gs = Rot([AR.alloc([8, 512], BF16, "W1g") for _ in range(2)])
            W2gs = Rot([AR.alloc([4, 1024], BF16, "W2g") for _ in range(2)])
            rts = Rot([AR.alloc([512], F32, "rt") for _ in range(3)])
            gffn, K_gffn = AR.alloc([1024], F32, "gffn")
            hs = Rot([AR.alloc([1024], BF16, "h") for _ in range(2)])
            junk, jk = AR.alloc([1024], BF16, "junk")
            sts = Rot([AR.alloc([8], F32, "st") for _ in range(2)])
            gload(l, "ffng", gffn, K_gffn)
            w1v = D["w_ff1"][l].rearrange("(kc p) n -> p kc n", p=128)
            w2v = D["w_ff2"][l].rearrange("(jc j) n -> j jc n", j=128)
            wg = {}

            def load_group(grp):
                W1g, W1gk = W1gs.next(); W2g, W2gk = W2gs.next()
                for hf2 in range(2):
                    dma("pool", W1g[:, hf2 * 4:(hf2 + 1) * 4, :], w1v[:, hf2 * 4:(hf2 + 1) * 4, grp * 512:(grp + 1) * 512], writes=[W1gk + "_%d" % hf2],
                        stream="w1g%d_%d" % (grp % 2, hf2))
                    dma("pool", W2g[:, hf2 * 2:(hf2 + 1) * 2, :], w2v[:, grp * 4 + hf2 * 2:grp * 4 + (hf2 + 1) * 2, :], writes=[W2gk + "_%d" % hf2],
                        stream="w2g%d_%d" % (grp % 2, hf2))
                wg[grp] = (W1g, W1gk, W2g, W2gk)
            load_group(0)
            for i in range(16):
                h, hk = hs.next(); st, stk = sts.next()
                pT = 0 if i % 2 == 0 else 7
                rstd = rms_rstd(x_res[:, i, :], 1024, "x%d" % i, junk, jk, st, stk, 0)
                I("dve", "scalar_tensor_tensor", dict(out=h, in0=x_res[:, i, :], scalar=rstd, in1=gffn, op0=ALU.mult, op1=ALU.mult),
                  reads=["x%d" % i, stk, K_gffn], writes=[hk])
                transposes(h, hk, 8, hTa[:, :, i * 128:(i + 1) * 128], K_hTa + "_%d" % i, pT, evac="dve")
            hbank = Rot([1, 2, 3, 4])
            ybank = Rot([5, 6])
            for grp in range(8):
                if grp + 1 < 8:
                    load_group(grp + 1)
                W1g, W1gk, W2g, W2gk = wg[grp]
                for jc in range(4):
                    for tb in range(4):
                        hb = hbank.next()
                        for kc in range(8):
                            I("pe", "matmul", dict(out=ps[hb][:, :], lhsT=W1g[:, kc, jc * 128:(jc + 1) * 128], rhs=hTa[:, kc, tb * 512:(tb + 1) * 512],
                                                   start=(kc == 0), stop=(kc == 7)),
                              reads=[W1gk + "_%d" % (kc // 4)] + [K_hTa + "_%d" % t_ for t_ in range(tb * 4, tb * 4 + 4)], writes=["ps%d" % hb])
                        rt, rtk = rts.next()
                        I("act", "activation", dict(out=rt, in_=ps[hb][:, :], func=AF.Relu), reads=["ps%d" % hb], writes=[rtk])
                        I("pool", "tensor_tensor", dict(out=hidT[:, jc, tb * 512:(tb + 1) * 512], in0=rt, in1=rt, op=ALU.mult), reads=[rtk],
                          writes=[K_hidT + "_%d_%d" % (jc, tb)])
                for i in range(16):
                    for hf in range(2):
                        yb = ybank.next()
                        for jc in range(4):
                            I("pe", "matmul", dict(out=ps[yb][:, :], lhsT=hidT[:, jc, i * 128:(i + 1) * 128], rhs=W2g[:, jc, hf * 512:(hf + 1) * 512],
                                                   start=(jc == 0), stop=(jc == 3)),
                              reads=[K_hidT + "_%d_%d" % (jc, i // 4), W2gk + "_%d" % (jc // 2)], writes=["ps%d" % yb])
                        I("dve", "tensor_tensor", dict(out=x_res[:, i, hf * 512:(hf + 1) * 512], in0=ps[yb][:, :], in1=x_res[:, i, hf * 512:(hf + 1) * 512], op=ALU.add),
                          reads=["ps%d" % yb, "x%d" % i], writes=["x%d" % i])
            P.barrier()

        yo = D["y"].rearrange("(i p) d -> p i d", p=128)
        for i4 in range(4):
            dma("sp", yo[:, i4 * 4:(i4 + 1) * 4, :], x_res[:, i4 * 4:(i4 + 1) * 4, :],
                reads=["x%d" % i for i in range(i4 * 4, i4 * 4 + 4)], stream="yo%d" % i4)
        P.emit()
    return nc, dbg_out


_NC_CACHE = {}


def _assemble(ys):
    out = np.empty((4, 32, 128, 1024), np.float32)
    for c in range(8):
        out[c // 2, (c % 2)::2] = ys[c].reshape(16, 128, 1024)
    return out.reshape(4, 4096, 1024)


def kernel(**inputs):
    inp = {k: np.asarray(v) for k, v in inputs.items()}
    x = np.ascontiguousarray(inp["x"], dtype=np.float32)
    mem = np.ascontiguousarray(inp["mem"], dtype=np.float32)
    consts = [host_consts(h) for h in (0, 1)]
    NL = 2
    if "nc" not in _NC_CACHE:
        _NC_CACHE["nc"] = build(nl=NL)[0]
    nc = _NC_CACHE["nc"]
    ws = [host_layer_weights(inp, l) for l in range(NL)]
    wst = {k: np.stack([w[k] for w in ws]) for k in ws[0]}
    in_maps = []
    for c in range(8):
        b, half = c // 2, c % 2
        m = {}
        xt = x[b].reshape(32, 128, 1024)
        m["x_own"] = np.ascontiguousarray(xt[half::2].reshape(2048, 1024))
        m["mem"] = mem[b]
        m.update(wst)
        m.update(consts[half])
        in_maps.append(m)
    res = run_bass_kernel_spmd(nc, in_maps, core_ids=list(range(8)))
    return _assemble([res.results[c]["y"] for c in range(8)]).astype(np.float32)
```

```python
import contextlib
import os
import numpy as np
import concourse.bass as bass
import concourse.mybir as mybir
from concourse.bass_utils import run_bass_kernel_spmd

F32 = mybir.dt.float32
BF16 = mybir.dt.bfloat16
ALU = mybir.AluOpType
AF = mybir.ActivationFunctionType
AX = mybir.AxisListType

ENGS = ("pe", "act", "dve", "pool", "sp")
SKIP_BIG = os.environ.get("SKIP_BIG", "1") == "1"
BIGN = 128
EPS = 1e-6
NEG = -30000.0


class Prog:
    def __init__(self, nc):
        self.nc = nc
        self.ops = []
        self.last_writer = {}
        self.readers = {}
        self.barrier_deps = set()
        self.last_eng_op = {}
        self.last_dma_op = {}
        self.groups = {}

    def _expand(self, keys):
        out = []
        for k in keys:
            out.extend(self.groups.get(k, (k,)))
        return out

    def op(self, eng, fn, reads=(), writes=(), dma=None, inc=None, big=False):
        idx = len(self.ops)
        reads = self._expand(reads)
        writes = self._expand(writes)
        deps = set(self.barrier_deps)
        for r in reads:
            w = self.last_writer.get(r)
            if w is not None:
                deps.add(w)
        for w_ in writes:
            w = self.last_writer.get(w_)
            if w is not None:
                deps.add(w)
            for rd in self.readers.get(w_, ()):
                deps.add(rd)
        if dma is not None and dma in self.last_dma_op:
            deps.add(self.last_dma_op[dma])
        self.ops.append(dict(eng=eng, fn=fn, deps=deps, dma=dma, big=big, sig=False, inc=(inc if inc is not None else (16 if dma is not None else 1))))
        for r in reads:
            self.readers.setdefault(r, []).append(idx)
        for w_ in writes:
            self.last_writer[w_] = idx
            self.readers[w_] = []
        if dma is None:
            self.last_eng_op[eng] = idx
        else:
            self.last_dma_op[dma] = idx
        return idx

    def barrier(self):
        self.barrier_deps = set(self.last_eng_op.values()) | set(self.last_dma_op.values())

    def _skip(self, p, o):
        if p["dma"] is not None or o["dma"] is not None or p["eng"] != o["eng"]:
            return False
        return p["eng"] == "pe" or (p["big"] and SKIP_BIG)

    def emit(self, final_wait_eng="sp"):
        nc = self.nc
        ops = self.ops
        for o in ops:
            for d in o["deps"]:
                p = ops[d]
                if not self._skip(p, o):
                    p["sig"] = True
            if o["dma"] is not None:
                o["sig"] = True
        counters = {}
        for o in ops:
            if not o["sig"]:
                continue
            sname = ("dma:" + o["dma"]) if o["dma"] is not None else ("eng:" + o["eng"])
            counters[sname] = counters.get(sname, 0) + o["inc"]
            o["signal"] = (sname, counters[sname])
        with contextlib.ExitStack() as es:
            sems = {}
            for sn in sorted(counters):
                sems[sn] = es.enter_context(nc.semaphore(sn.replace(":", "_")))
            block = es.enter_context(nc.Block())
            per_eng = {e: [] for e in ENGS}
            for i, o in enumerate(ops):
                per_eng[o["eng"]].append(i)

            def body(ename, engine):
                waited = {}
                for i in per_eng[ename]:
                    o = ops[i]
                    need = {}
                    for d in o["deps"]:
                        p = ops[d]
                        if not p["sig"] or self._skip(p, o):
                            continue
                        sn, val = p["signal"]
                        if need.get(sn, 0) < val:
                            need[sn] = val
                    for sn, val in need.items():
                        if waited.get(sn, 0) < val:
                            engine.wait_ge(sems[sn], val)
                            waited[sn] = val
                    ins = o["fn"](engine)
                    if o["sig"]:
                        sn, val = o["signal"]
                        ins.then_inc(sems[sn], o["inc"])
                if ename == final_wait_eng:
                    for sn, val in counters.items():
                        if sn.startswith("dma:") and waited.get(sn, 0) < val:
                            engine.wait_ge(sems[sn], val)

            regs = dict(pe=block.tensor, act=block.scalar, dve=block.vector, pool=block.gpsimd, sp=block.sync)
            for ename in ENGS:
                if not per_eng[ename] and ename != final_wait_eng:
                    continue

                def mk(en):
                    def f(engine):
                        body(en, engine)
                    return f
                regs[ename](mk(ename))
        return counters


class Rot:
    def __init__(self, items):
        self.items = items
        self.i = 0

    def next(self):
        it = self.items[self.i % len(self.items)]
        self.i += 1
        return it


class Arena:
    def __init__(self, t, nbytes):
        self.t = t
        self.nbytes = nbytes
        self.off = 0
        self.cnt = 0

    def alloc(self, free_shape, dt, name):
        n = int(np.prod(free_shape))
        esz = 4 if dt == F32 else 2
        nb = (n * esz + 63) // 64 * 64
        assert self.off + nb <= self.nbytes, (name, self.off, nb, self.nbytes)
        a = self.off // 2
        v = self.t[:, a:a + (n * esz) // 2]
        if dt == F32:
            v = v.bitcast(F32)
        self.off += nb
        self.cnt += 1
        key = "%s@%d" % (name, self.cnt)
        if len(free_shape) == 2:
            v = v.rearrange("p (a b) -> p a b", b=free_shape[1])
        elif len(free_shape) == 3:
            v = v.rearrange("p (a b c) -> p a b c", b=free_shape[1], c=free_shape[2])
        return v, key


GV = {}
_o = 0
for _n, _s in [("mixg", 1024), ("memg", 1024), ("ffng", 1024), ("mkvg", 1024), ("lng", 512), ("lnb", 512),
               ("mog0", 512), ("mog1", 512), ("qg", 512), ("kg", 256), ("kcg", 64), ("b2k", 64), ("b2v", 64),
               ("mqg", 512), ("mkg", 512)]:
    GV[_n] = (_o, _s)
    _o += _s
NGV = _o


def host_consts(half):
    c = {}
    c["identf"] = np.eye(128, dtype=np.float32)
    k = np.arange(4096)
    c["Econst"] = (k[None, :] // 64 == np.arange(64)[:, None]).astype(np.float32)
    OFF = 240
    cc = np.arange(512) - OFF - 8 * half
    A = np.zeros((64, 512), np.float32)
    C = np.zeros((64, 128), np.float32)
    tl = np.arange(128)
    for j in range(8):
        m = j - 1
        A[j] = (cc == m)
        C[j] = np.where(16 * m + 31 > tl, NEG, 0.0)
    A[8] = (cc >= 7)
    C[8] = NEG
    c["Astat"] = A
    c["Cst"] = np.tile(C, (1, 4)).astype(np.float32)
    kl = np.arange(128)[:, None]
    ql = np.arange(128)[None, :]
    tri = (kl <= ql).astype(np.float32)
    left = (kl > ql).astype(np.float32)
    ones = np.ones((128, 128), np.float32)
    zeros = np.zeros((128, 128), np.float32)
    if half == 0:
        mD = [tri, zeros]
        mW = [left, ones, tri, zeros]
    else:
        mD = [ones, tri]
        mW = [zeros, left, ones, tri]
    c["maskD"] = np.stack([np.tile(m, (1, 4)) for m in mD], 1).astype(np.float32)
    c["maskW"] = np.stack([np.tile(m, (1, 4)) for m in mW], 1).astype(np.float32)
    Mv = np.zeros((128, 16, 64), np.float32)
    Ma = np.zeros((128, 16, 64), np.float32)
    j = np.arange(64)[None, :]
    for i in range(16):
        G = 2 * i + half
        t = 128 * G + np.arange(128)[:, None]
        tb = t // 64
        forced = (j == 0) | (j == tb) | (j == tb - 1)
        invalid = (j > tb)
        Mv[:, i, :] = (~forced & ~invalid)
        Ma[:, i, :] = np.where(forced, 1e4, np.where(invalid, -1e4, 0.0))
    c["Mvalid"] = Mv
    c["Madd"] = Ma
    n = np.arange(256)
    cs = n * 16
    ss = np.arange(64) * 64
    ov = np.clip(np.minimum(cs[:, None] + 32, ss[None, :] + 64) - np.maximum(cs[:, None], ss[None, :]), 0, None) / 32.0
    ov[255] = 0.0
    c["ovc"] = ov.reshape(2, 128, 64).transpose(1, 0, 2).astype(np.float32).copy()
    c["trisg"] = tri.copy()
    return c


def host_layer_weights(inp, l):
    w = {}
    wi = inp["w_in"][l]
    kv = wi[:, 1536:2304]
    kc, vc, ks, vs, kw, vw = [kv[:, i * 128:(i + 1) * 128] for i in range(6)]
    w["w_in_p"] = np.concatenate([wi[:, 0:1536], wi[:, 2304:2328], ks, kw, vs, vw, kc, vc], axis=1)
    gv = np.zeros((1, NGV), np.float32)

    def put(name, v):
        o, s = GV[name]
        gv[0, o:o + s] = np.asarray(v).reshape(-1)
    put("mixg", inp["norm_mix_g"][l]); put("memg", inp["norm_mem_g"][l]); put("ffng", inp["norm_ffn_g"][l])
    put("mkvg", inp["mem_kv_norm_g"][l]); put("lng", inp["sg_ln_g"][l]); put("lnb", inp["sg_ln_b"][l])
    put("mog0", inp["mix_out_g"][l, 0]); put("mog1", inp["mix_out_g"][l, 1])
    put("qg", np.tile(inp["q_norm_g"][l], 8))
    put("kg", np.concatenate([np.tile(inp["k_norm_g"][l, 1], 2), np.tile(inp["k_norm_g"][l, 2], 2)]))
    put("kcg", inp["k_norm_g"][l, 0]); put("b2k", inp["cmp_b2"][l, 0]); put("b2v", inp["cmp_b2"][l, 1])
    put("mqg", np.tile(inp["mem_q_norm_g"][l], 4)); put("mkg", np.tile(inp["mem_k_norm_g"][l], 4))
    w["gvec"] = gv
    w["sg_wT"] = np.ascontiguousarray(inp["sg_w"][l].transpose(2, 0, 1))
    w["sg_bT"] = np.ascontiguousarray(inp["sg_b"][l].T)
    w["cmp_w1"] = np.ascontiguousarray(inp["cmp_w1"][l])
    w["cmp_b1T"] = np.ascontiguousarray(inp["cmp_b1"][l].reshape(2, 2, 128).transpose(0, 2, 1))
    w["cmp_posT"] = np.ascontiguousarray(inp["cmp_pos"][l].transpose(0, 2, 1))
    w["cmp_w2"] = np.ascontiguousarray(inp["cmp_w2"][l])
    for k in ("w_out", "w_mq", "w_mkv", "w_mo", "w_ff1", "w_ff2"):
        w[k] = np.ascontiguousarray(inp[k][l])
    return w


WSHAPES = dict(w_in_p=[1024, 2328], gvec=[1, NGV], sg_wT=[128, 8, 128], sg_bT=[128, 8], cmp_w1=[2, 2048, 256],
               cmp_b1T=[2, 128, 2], cmp_posT=[2, 64, 32], cmp_w2=[2, 256, 64], w_out=[1024, 1024],
               w_mq=[1024, 512], w_mkv=[1024, 1024], w_mo=[512, 1024], w_ff1=[1024, 4096], w_ff2=[4096, 1024])
CSHAPES = dict(identf=[128, 128], Econst=[64, 4096], Astat=[64, 512], Cst=[64, 512], maskD=[128, 2, 512],
               maskW=[128, 4, 512], Mvalid=[128, 16, 64], Madd=[128, 16, 64], ovc=[128, 2, 64], trisg=[128, 128])


def build(nl=1, stop=None, dbg=None, ncores=8):
    nc = bass.Bass("TRN2", target_bir_lowering=False)
    D = {}
    D["x_own"] = nc.dram_tensor("x_own", [2048, 1024], F32, kind="ExternalInput").ap()
    hx_in = [nc.dram_tensor("hxin%d" % g, [512, 1024], BF16) for g in range(4)]
    hx_out = [nc.dram_tensor("hxout%d" % g, [1024, 1024], BF16) for g in range(4)]
    RG = [[2 * p, 2 * p + 1] for p in range(ncores // 2)]
    D["mem"] = nc.dram_tensor("mem", [256, 1024], F32, kind="ExternalInput").ap()
    for k, s in WSHAPES.items():
        D[k] = nc.dram_tensor(k, [nl] + s, F32, kind="ExternalInput").ap()
    for k, s in CSHAPES.items():
        D[k] = nc.dram_tensor(k, s, F32, kind="ExternalInput").ap()
    D["y"] = nc.dram_tensor("y", [2048, 1024], F32, kind="ExternalOutput").ap()
    dbg_out = {}

    with contextlib.ExitStack() as es:
        def sb(name, shape, dt=F32):
            return es.enter_context(nc.sbuf_tensor("s_" + name, shape, dt))

        x_res = sb("x_res", [128, 16, 1024], F32)
        ARB = 128 * 1024
        arena_t = sb("arena", [128, ARB // 2], BF16)
        identb = sb("identb", [128, 128], BF16)
        Astat = sb("Astat", [64, 512], BF16)
        Cst = sb("Cst", [64, 512], BF16)
        maskD = sb("maskD", [128, 2, 512], BF16)
        maskW = sb("maskW", [128, 4, 512], BF16)
        Mvalid = sb("Mvalid", [128, 16, 64], BF16)
        Madd = sb("Madd", [128, 16, 64], BF16)
        trisg = sb("trisg", [128, 128], BF16)
        zerob = sb("zerob", [128, 512], BF16)
        ps = [es.enter_context(nc.psum_tensor("ps%d" % b, [128, 512], F32)) for b in range(8)]

        P = Prog(nc)
        AR = Arena(arena_t, ARB)
        dmac = [0]

        def dma(eng, out, in_, reads=(), writes=(), stream=None):
            if stream is None:
                dmac[0] += 1
                stream = "q%s%d" % (eng, dmac[0] % 12)
            P.op(eng, lambda e, out=out, in_=in_: e.dma_start(out=out, in_=in_), reads=reads, writes=writes, dma=stream)

        def I(eng, meth, kw, reads=(), writes=()):
            big = False
            o_ = kw.get("out", None)
            if o_ is not None and "accum_out" not in kw and meth in ("tensor_tensor", "tensor_scalar", "scalar_tensor_tensor", "tensor_copy", "activation"):
                try:
                    big = int(np.prod(o_.shape[1:])) >= BIGN
                except Exception:
                    big = False
            P.op(eng, lambda e, kw=kw, meth=meth: getattr(e, meth)(**kw), reads=reads, writes=writes, big=big)

        def dump(name, ap, shape, reads):
            if dbg is None or name not in dbg:
                return
            d = nc.dram_tensor("dbg_" + name, shape, F32 if ap.dtype == F32 else BF16, kind="ExternalOutput").ap()
            dbg_out[name] = d
            dma("sp", d, ap, reads=reads, stream="dbg")

        for t, nm in ((identb, "identf"), (Astat, "Astat"), (Cst, "Cst"), (maskD, "maskD"), (maskW, "maskW"),
                      (Mvalid, "Mvalid"), (Madd, "Madd"), (trisg, "trisg")):
            dma("pool", t[:], D[nm], writes=[nm])
        I("dve", "memset", dict(ap=zerob[:], constant=0.0), writes=["zerob"])
        xo = D["x_own"].rearrange("(i p) d -> p i d", p=128)
        for i4 in range(4):
            dma("sp", x_res[:, i4 * 4:(i4 + 1) * 4, :], xo[:, i4 * 4:(i4 + 1) * 4, :],
                writes=["x%d" % i for i in range(i4 * 4, i4 * 4 + 4)], stream="xr%d" % i4)

        def gload(l, name, dst, key):
            o, s = GV[name]
            dma("sp", dst, D["gvec"][l, 0:1, o:o + s].partition_broadcast(128), writes=[key])

        small = {}

        def rms_rstd(x_ap, n, xkey, junk, junk_key, st, st_key, col):
            I("act", "activation", dict(out=junk, in_=x_ap, func=AF.Square, accum_out=st[:, col:col + 1]),
                 reads=[xkey], writes=[junk_key, st_key])
            I("act", "activation", dict(out=st[:, col + 1:col + 2], in_=st[:, col:col + 1], func=AF.Sqrt,
                                               scale=1.0 / n, bias=EPS), reads=[st_key], writes=[st_key])
            I("dve", "reciprocal", dict(out=st[:, col + 2:col + 3], in_=st[:, col + 1:col + 2]),
                 reads=[st_key], writes=[st_key])
            return st[:, col + 2:col + 3]

        def transposes(src, src_key, n, dst, dst_key, psb, evac="act"):
            pst = ps[psb][:, :].bitcast(BF16)
            for c in range(n):
                I("pe", "transpose", dict(out=pst[:, c * 128:(c + 1) * 128], in_=src[:, c * 128:(c + 1) * 128], identity=identb[:]),
                     reads=[src_key, "identf"], writes=["ps%d" % psb])
            v = pst[:, 0:n * 128].rearrange("p (a b) -> p a b", b=128)
            if evac == "act":
                I("act", "activation", dict(out=dst, in_=v, func=AF.Copy), reads=["ps%d" % psb], writes=[dst_key])
            else:
                I("dve", "tensor_copy", dict(out=dst, in_=v), reads=["ps%d" % psb], writes=[dst_key])

        def norm_to_hT(x_ap, xkey, g_tile, g_key, h, h_key, hT, hT_key, junk, junk_key, st, st_key, psb):
            rstd = rms_rstd(x_ap, 1024, xkey, junk, junk_key, st, st_key, 0)
            I("dve", "scalar_tensor_tensor", dict(out=h, in0=x_ap, scalar=rstd, in1=g_tile, op0=ALU.mult, op1=ALU.mult),
                 reads=[xkey, st_key, g_key], writes=[h_key])
            transposes(h, h_key, 8, hT, hT_key, psb)

        def wload(dst, src, key, nsplit=1):
            kc = dst.shape[1]
            v = src.rearrange("(kc p) n -> p kc n", p=128)
            step = (kc + nsplit - 1) // nsplit
            parts = []
            for a in range(0, kc, step):
                b = min(kc, a + step)
                parts.append("%s#%d" % (key, a))
                P.groups.pop(key, None)
                dma("pool", dst[:, a:b, :], v[:, a:b, :], writes=[parts[-1]])
            P.groups[key] = parts

        for l in range(nl):
            AR.off = 0
            q_tok, K_qtok = AR.alloc([16, 512], BF16, "q_tok")
            a_tok, K_atok = AR.alloc([16, 512], BF16, "a_tok")
            gates, K_gates = AR.alloc([16, 24], F32, "gates")
            mark_kv = AR.off
            W_own, K_Wown = AR.alloc([8, 1560], BF16, "W_own")
            sgw, K_sgw = AR.alloc([8, 128], BF16, "sgw")
            sgwf, K_sgwf = AR.alloc([8, 128], BF16, "sgwf")
            sgb, K_sgb = AR.alloc([8], F32, "sgb")
            gmix, K_gmix = AR.alloc([1024], F32, "gmix")
            lng, K_lng = AR.alloc([512], F32, "lng")
            lnb, K_lnb = AR.alloc([512], F32, "lnb")
            mog0, K_mog0 = AR.alloc([512], F32, "mog0")
            qg, K_qg = AR.alloc([512], F32, "qg")
            hs = Rot([AR.alloc([1024], BF16, "h") for _ in range(3)])
            hTs = Rot([AR.alloc([8, 128], BF16, "hT") for _ in range(3)])
            junks = Rot([AR.alloc([1024], F32, "junk") for _ in range(2)])
            sts = Rot([AR.alloc([32], F32, "st") for _ in range(3)])
            ugs = Rot([AR.alloc([512], BF16, "ug") for _ in range(3)])
            vgs = Rot([AR.alloc([512], F32, "vg") for _ in range(3)])
            vns = Rot([AR.alloc([512], BF16, "vn") for _ in range(2)])
            t1s = Rot([AR.alloc([512], F32, "t1") for _ in range(2)])
            t2s = Rot([AR.alloc([512], F32, "t2") for _ in range(2)])
            bns = Rot([AR.alloc([16], F32, "bn") for _ in range(2)])

            wload(W_own, D["w_in_p"][l, :, 0:1560], K_Wown, nsplit=4)
            dma("pool", sgwf, D["sg_wT"][l], writes=[K_sgwf])
            dma("sp", sgb, D["sg_bT"][l], writes=[K_sgb])
            gload(l, "mixg", gmix, K_gmix); gload(l, "lng", lng, K_lng); gload(l, "lnb", lnb, K_lnb)
            gload(l, "mog0", mog0, K_mog0); gload(l, "qg", qg, K_qg)
            I("dve", "tensor_tensor", dict(out=sgw, in0=sgwf, in1=trisg[:].unsqueeze(1).to_broadcast([128, 8, 128]), op=ALU.mult),
                 reads=[K_sgwf, "trisg"], writes=[K_sgw])

            ctx = {}
            tq1s = Rot([AR.alloc([512], F32, "tq1") for _ in range(2)])
            tq2s = Rot([AR.alloc([512], F32, "tq2") for _ in range(2)])
            for i in range(16):
                h, hk = hs.next(); hT, hTk = hTs.next(); junk, jk = junks.next(); st, stk = sts.next()
                ug, ugk = ugs.next(); vg, vgk = vgs.next(); vn, vnk = vns.next()
                t1, t1k = t1s.next(); t2, t2k = t2s.next(); bn, bnk = bns.next(); tq1, tq1k = tq1s.next(); tq2, tq2k = tq2s.next()
                ctx[i] = ("x%d" % i, h, hk, hT, hTk, junk, jk, st, stk, (0 if i % 2 == 0 else 7), ([1, 2, 3] if i % 2 == 0 else [4, 5, 6]),
                          ug, ugk, vg, vgk, vn, vnk, t1, t1k, t2, t2k, bn, bnk, tq1, tq1k, tq2, tq2k)

            def stA(i):
                (xk, h, hk, hT, hTk, junk, jk, st, stk, pT, pb, ug, ugk, vg, vgk, vn, vnk, t1, t1k, t2, t2k, bn, bnk, tq1, tq1k, tq2, tq2k) = ctx[i]
                norm_to_hT(x_res[:, i, :], xk, gmix, K_gmix, h, hk, hT, hTk, junk, jk, st, stk, pT)
                gq = i // 4
                dma("sp", hx_in[gq][(i % 4) * 128:(i % 4 + 1) * 128, :], h, reads=[hk], writes=["hxin%d" % gq], stream="hxs%d" % gq)
                if i % 4 == 3:
                    P.op("pool", lambda e, gq=gq: e.collective_compute("AllGather", ALU.bypass, replica_groups=RG,
                                                                         ins=[hx_in[gq].ap().opt()], outs=[hx_out[gq].ap().opt()]),
                         reads=["hxin%d" % gq], writes=["hxout%d" % gq], dma="cc%d" % gq, inc=1)

            def stB(i):
                (xk, h, hk, hT, hTk, junk, jk, st, stk, pT, pb, ug, ugk, vg, vgk, vn, vnk, t1, t1k, t2, t2k, bn, bnk, tq1, tq1k, tq2, tq2k) = ctx[i]
                for ci, (c0, c1) in enumerate(((0, 512), (512, 1024), (1024, 1536))):
                    for kc in range(8):
                        I("pe", "matmul", dict(out=ps[pb[ci]][:, :], lhsT=hT[:, kc, :], rhs=W_own[:, kc, c0:c1],
                                                                                  start=(kc == 0), stop=(kc == 7)),
                             reads=[hTk, K_Wown], writes=["ps%d" % pb[ci]])
                if i == 0:
                    dump("h0", h, [128, 1024], [hk]); dump("hT0", hT, [128, 8, 128], [hTk]); dump("st0", st, [128, 32], [stk])
                    dump("gmix", gmix, [128, 1024], [K_gmix]); dump("W0", W_own[:, 0, :], [128, 1560], [K_Wown])
                I("act", "activation", dict(out=ug, in_=ps[pb[0]][:, :], func=AF.Gelu_apprx_tanh), reads=["ps%d" % pb[0]], writes=[ugk])
                I("act", "activation", dict(out=vg, in_=ps[pb[1]][:, :], func=AF.Gelu_apprx_tanh), reads=["ps%d" % pb[1]], writes=[vgk])
                if i == 0:
                    dump("ug0", ug, [128, 512], [ugk]); dump("vg0", vg, [128, 512], [vgk])
                I("act", "activation", dict(out=tq1, in_=ps[pb[2]][:, :], func=AF.Square), reads=["ps%d" % pb[2]], writes=[tq1k])
                I("dve", "tensor_reduce", dict(out=st[:, 8:16], in_=tq1.rearrange("p (a b) -> p a b", b=64), axis=AX.X, op=ALU.add),
                     reads=[tq1k], writes=[stk + "q"])
                I("act", "activation", dict(out=st[:, 16:24], in_=st[:, 8:16], func=AF.Sqrt, scale=1.0 / 64, bias=EPS),
                     reads=[stk + "q"], writes=[stk + "q"])
                I("dve", "reciprocal", dict(out=st[:, 24:32], in_=st[:, 16:24]), reads=[stk + "q"], writes=[stk + "q"])
                I("dve", "tensor_tensor", dict(out=tq2.rearrange("p (a b) -> p a b", b=64),
                                                              in0=ps[pb[2]][:, :].rearrange("p (a b) -> p a b", b=64),
                                                              in1=st[:, 24:32].unsqueeze(2).to_broadcast([128, 8, 64]), op=ALU.mult),
                     reads=["ps%d" % pb[2], stk + "q"], writes=[tq2k])
                I("dve", "tensor_tensor", dict(out=q_tok[:, i, :], in0=tq2, in1=qg, op=ALU.mult),
                     reads=[tq2k, K_qg], writes=[K_qtok + "_%d" % i])
                for kc in range(8):
                    I("pe", "matmul", dict(out=ps[pb[2]][:, 0:24], lhsT=hT[:, kc, :], rhs=W_own[:, kc, 1536:1560],
                                                                  start=(kc == 0), stop=(kc == 7)),
                         reads=[hTk, K_Wown], writes=["ps%d" % pb[2]])
                I("act", "activation", dict(out=gates[:, i, :], in_=ps[pb[2]][:, 0:24], func=AF.Sigmoid),
                     reads=["ps%d" % pb[2]], writes=[K_gates + "_%d" % i])

            def stC(i):
                (xk, h, hk, hT, hTk, junk, jk, st, stk, pT, pb, ug, ugk, vg, vgk, vn, vnk, t1, t1k, t2, t2k, bn, bnk, tq1, tq1k, tq2, tq2k) = ctx[i]
                I("dve", "bn_stats", dict(out=bn[:, 0:6], in_=vg), reads=[vgk], writes=[bnk])
                I("dve", "bn_aggr", dict(out=bn[:, 8:10], in_=bn[:, 0:6]), reads=[bnk], writes=[bnk])
                I("act", "activation", dict(out=bn[:, 10:11], in_=bn[:, 9:10], func=AF.Sqrt, scale=1.0, bias=EPS), reads=[bnk], writes=[bnk])
                I("dve", "reciprocal", dict(out=bn[:, 11:12], in_=bn[:, 10:11]), reads=[bnk], writes=[bnk])
                I("dve", "tensor_scalar", dict(out=t1, in0=vg, scalar1=bn[:, 8:9], scalar2=bn[:, 11:12], op0=ALU.subtract, op1=ALU.mult),
                     reads=[vgk, bnk], writes=[t1k])
                I("dve", "tensor_tensor", dict(out=t1, in0=t1, in1=lng, op=ALU.mult), reads=[t1k, K_lng], writes=[t1k])
                I("dve", "tensor_tensor", dict(out=vn, in0=t1, in1=lnb, op=ALU.add), reads=[t1k, K_lnb], writes=[vnk])
                for g in range(8):
                    I("pe", "matmul", dict(out=ps[pb[1]][:, g * 64:(g + 1) * 64], lhsT=sgw[:, g, :], rhs=vn[:, g * 64:(g + 1) * 64],
                                                               start=True, stop=True),
                         reads=[vnk, K_sgw], writes=["ps%d" % pb[1]])
                I("dve", "tensor_tensor", dict(out=t2.rearrange("p (a b) -> p a b", b=64),
                                                              in0=ps[pb[1]][:, :].rearrange("p (a b) -> p a b", b=64),
                                                              in1=sgb.unsqueeze(2).to_broadcast([128, 8, 64]), op=ALU.add),
                     reads=["ps%d" % pb[1], K_sgb], writes=[t2k])
                I("dve", "tensor_tensor", dict(out=t2, in0=t2, in1=ug, op=ALU.mult), reads=[t2k, ugk], writes=[t2k])
                I("act", "activation", dict(out=t1, in_=t2, func=AF.Square, accum_out=bn[:, 12:13]), reads=[t2k], writes=[t1k, bnk])
                I("act", "activation", dict(out=bn[:, 13:14], in_=bn[:, 12:13], func=AF.Sqrt, scale=1.0 / 512, bias=EPS), reads=[bnk], writes=[bnk])
                I("dve", "reciprocal", dict(out=bn[:, 14:15], in_=bn[:, 13:14]), reads=[bnk], writes=[bnk])
                I("dve", "scalar_tensor_tensor", dict(out=a_tok[:, i, :], in0=t2, scalar=bn[:, 14:15], in1=mog0, op0=ALU.mult, op1=ALU.mult),
                     reads=[t2k, bnk, K_mog0], writes=[K_atok + "_%d" % i])


            for s_ in range(18):
                if s_ < 16:
                    stA(s_)
                if 0 <= s_ - 1 < 16:
                    stB(s_ - 1)
                if 0 <= s_ - 2 < 16:
                    stC(s_ - 2)

            dump("q_tok", q_tok, [128, 16, 512], [K_qtok + "_%d" % i for i in range(16)])
            dump("a_tok", a_tok, [128, 16, 512], [K_atok + "_%d" % i for i in range(16)])
            dump("gates", gates, [128, 16, 24], [K_gates + "_%d" % i for i in range(16)])
            if stop in ("P1", "%d:P1" % l):
                break
            P.barrier()
            AR.off = mark_kv
            KE, K_KE = AR.alloc([2, 4096], BF16, "KE")
            KW, K_KW = AR.alloc([4096], BF16, "KW")
            VS, K_VS = AR.alloc([32, 2, 65], BF16, "VS")
            VW, K_VW = AR.alloc([32, 2, 65], BF16, "VW")
            kcT, K_kcT = AR.alloc([2, 256], BF16, "kcT")
            vcov, K_vcov = AR.alloc([2, 2, 128], BF16, "vcov")
            mark_tr = AR.off
            cmpTk, K_cTk = AR.alloc([4096], BF16, "cmpTk")
            cmpTv, K_cTv = AR.alloc([4096], BF16, "cmpTv")
            mark_p3 = AR.off
            W_kv, K_Wkv = AR.alloc([8, 768], BF16, "W_kv")
            kg, K_kg = AR.alloc([256], F32, "kg")
            hs = Rot([AR.alloc([1024], BF16, "h") for _ in range(3)])
            hTs = Rot([AR.alloc([8, 128], BF16, "hT") for _ in range(2)])
            sts = Rot([AR.alloc([32], F32, "st") for _ in range(2)])
            kns = Rot([AR.alloc([256], BF16, "kn") for _ in range(2)])
            cbs = Rot([AR.alloc([256], BF16, "cb") for _ in range(2)])
            t1s = Rot([AR.alloc([256], F32, "t1") for _ in range(1)])
            t2s = t1s

            SK = os.environ.get("SKIP", "") if l == 0 else os.environ.get("SKIP1", "")
            wload(W_kv, D["w_in_p"][l, :, 1560:2328], K_Wkv, nsplit=2)
            gload(l, "kg", kg, K_kg)
            for hh in range(2):
                if "d" not in SK:
                    dma("pool", KE[64:128, hh, :], D["Econst"], writes=[K_KE + "E%d" % hh])
                if "e" not in SK:
                    dma("pool", vcov[:, :, hh, 64:128], D["ovc"], writes=[K_vcov + "ov%d" % hh])
            if "a" not in SK:
                I("dve", "memset", dict(ap=VS[:, :, :, 64:65], constant=1.0), writes=[K_VS + "one"])
                I("dve", "memset", dict(ap=VW[:, :, :, 64:65], constant=1.0), writes=[K_VW + "one"])
            if "f" not in SK:
                I("dve", "memset", dict(ap=kcT[0:64, :, :], constant=0.0), writes=[K_kcT])
                I("dve", "memset", dict(ap=vcov[:, :, :, 0:64], constant=0.0), writes=[K_vcov])

            for G in range(int(os.environ.get("NG", "32")) if l == 0 else int(os.environ.get("NG1", "32"))):
                h, hk = hs.next(); hT, hTk = hTs.next(); st, stk = sts.next()
                kn, knk = kns.next(); cb, cbk = cbs.next(); t1, t1k = t1s.next(); t2, t2k = t2s.next()
                io, rk = G // 2, G % 2
                dma("sp", h, hx_out[io // 4][rk * 512 + (io % 4) * 128: rk * 512 + (io % 4 + 1) * 128, :], reads=["hxout%d" % (io // 4)], writes=[hk],
                    stream="hin%d" % (G % 3))
                pT, pA, pB, pC = (0, 1, 2, 3) if G % 2 == 0 else (7, 4, 5, 6)
                transposes(h, hk, 8, hT, hTk, pT)
                if "g" in SK:
                    continue
                for kc in range(8):
                    I("pe", "matmul", dict(out=ps[pA][:, :], lhsT=hT[:, kc, :], rhs=W_kv[:, kc, 0:512], start=(kc == 0), stop=(kc == 7)),
                      reads=[hTk, K_Wkv], writes=["ps%d" % pA])
                for kc in range(8):
                    I("pe", "matmul", dict(out=ps[pB][:, 0:256], lhsT=hT[:, kc, :], rhs=W_kv[:, kc, 512:768], start=(kc == 0), stop=(kc == 7)),
                      reads=[hTk, K_Wkv], writes=["ps%d" % pB])
                I("act", "activation", dict(out=t1, in_=ps[pA][:, 0:256], func=AF.Square), reads=["ps%d" % pA], writes=[t1k])
                I("dve", "tensor_reduce", dict(out=st[:, 8:12], in_=t1.rearrange("p (a b) -> p a b", b=64), axis=AX.X, op=ALU.add),
                  reads=[t1k], writes=[stk + "q"])
                I("act", "activation", dict(out=st[:, 12:16], in_=st[:, 8:12], func=AF.Sqrt, scale=1.0 / 64, bias=EPS), reads=[stk + "q"], writes=[stk + "q"])
                I("dve", "reciprocal", dict(out=st[:, 16:20], in_=st[:, 12:16]), reads=[stk + "q"], writes=[stk + "q"])
                I("dve", "tensor_tensor", dict(out=t2.rearrange("p (a b) -> p a b", b=64), in0=ps[pA][:, 0:256].rearrange("p (a b) -> p a b", b=64),
                                               in1=st[:, 16:20].unsqueeze(2).to_broadcast([128, 4, 64]), op=ALU.mult),
                  reads=["ps%d" % pA, stk + "q"], writes=[t2k])
                I("dve", "tensor_tensor", dict(out=kn, in0=t2, in1=kg, op=ALU.mult), reads=[t2k, K_kg], writes=[knk])
                if "b" not in SK:
                    I("act", "activation", dict(out=VS[:, G, :, 0:64], in_=ps[pA][:, 256:384].rearrange("p (a b) -> p a b", b=64), func=AF.Copy),
                      reads=["ps%d" % pA], writes=[K_VS + "_%d" % G])
                    I("act", "activation", dict(out=VW[:, G, :, 0:64], in_=ps[pA][:, 384:512].rearrange("p (a b) -> p a b", b=64), func=AF.Copy),
                      reads=["ps%d" % pA], writes=[K_VW + "_%d" % G])
                I("act", "activation", dict(out=cb, in_=ps[pB][:, 0:256], func=AF.Copy), reads=["ps%d" % pB], writes=[cbk])
                if "h" in SK:
                    continue
                pst = ps[pC][:, :].bitcast(BF16)
                for c in range(2):
                    I("pe", "transpose", dict(out=pst[:, c * 128:(c + 1) * 128], in_=kn[:, c * 128:(c + 1) * 128], identity=identb[:]),
                      reads=[knk, "identf"], writes=["ps%d" % pC])
                for c in range(2):
                    I("pe", "transpose", dict(out=pst[:, (2 + c) * 128:(3 + c) * 128], in_=cb[:, c * 128:(c + 1) * 128], identity=identb[:]),
                      reads=[cbk, "identf"], writes=["ps%d" % pC])
                tl = slice(G * 128, (G + 1) * 128)
                if "i" in SK:
                    continue
                I("dve", "tensor_copy", dict(out=KE[0:64, 0, tl], in_=pst[0:64, 0:128]), reads=["ps%d" % pC], writes=[K_KE + "_%d" % G])
                if "c" not in SK:
                    I("dve", "tensor_copy", dict(out=KE[0:64, 1, tl], in_=pst[64:128, 0:128]), reads=["ps%d" % pC], writes=[K_KE + "_%d" % G])
                if "j" in SK:
                    continue
                if "k" not in SK:
                    I("dve", "tensor_copy", dict(out=KW[:, tl], in_=pst[:, 128:256]), reads=["ps%d" % pC], writes=[K_KW + "_%d" % G])
                if "m" not in SK:
                    I("dve", "tensor_copy", dict(out=cmpTk[:, tl], in_=pst[:, 256:384]), reads=["ps%d" % pC], writes=[K_cTk])
                if "n" not in SK:
                    I("dve", "tensor_copy", dict(out=cmpTv[:, tl], in_=pst[:, 384:512]), reads=["ps%d" % pC], writes=[K_cTv])
            dump("KE", KE, [128, 2, 4096], [K_KE + "_%d" % G for G in range(32)] + [K_KE + "E0", K_KE + "E1"])
            dump("KW", KW, [128, 4096], [K_KW + "_%d" % G for G in range(32)])
            dump("VS", VS, [128, 32, 2, 65], [K_VS + "_%d" % G for G in range(32)] + [K_VS + "one"])
            dump("VW", VW, [128, 32, 2, 65], [K_VW + "_%d" % G for G in range(32)] + [K_VW + "one"])
            if stop in ("P2", "%d:P2" % l):
                break
            P.barrier()
            AR.off = mark_p3
            w1L = [AR.alloc([32, 256], BF16, "w1") for _ in range(2)]
            posTL = [AR.alloc([32], BF16, "posT") for _ in range(2)]
            b1T, K_b1T = AR.alloc([2], F32, "b1T")
            c1b, K_c1b = AR.alloc([2], F32, "c1b")
            w2L = [AR.alloc([2, 64], BF16, "w2") for _ in range(2)]
            h1T, K_h1T = AR.alloc([2, 2, 256], BF16, "h1T")
            b2t, K_b2t = AR.alloc([64], F32, "b2t")
            kcgt, K_kcgt = AR.alloc([64], F32, "kcgt")
            t1s = Rot([AR.alloc([64], F32, "t1") for _ in range(1)])
            t2s = Rot([AR.alloc([64], BF16, "t2") for _ in range(2)])
            sts = Rot([AR.alloc([8], F32, "st") for _ in range(2)])
            junk, jk = AR.alloc([64], F32, "junk")
            gload(l, "kcg", kcgt, K_kcgt)
            I("dve", "memset", dict(ap=h1T, constant=0.0), writes=[K_h1T])
            bank = Rot([0, 1, 2, 3, 4, 5, 6, 7])
            for kvi in range(2):
                w1, K_w1 = w1L[kvi]; posT, K_posT = posTL[kvi]; w2, K_w2 = w2L[kvi]
                P.groups[K_w1] = [K_w1 + "#%d_%d" % (cp, s4) for cp in range(2) for s4 in range(4)]
                w1src = D["cmp_w1"][l, kvi].rearrange("(s d) j -> d s j", d=64)
                for s4 in range(4):
                    dma("pool", w1[0:64, s4 * 8:(s4 + 1) * 8, :], w1src[:, s4 * 8:(s4 + 1) * 8, :], writes=[K_w1 + "#0_%d" % s4])
                    dma("sp", w1[64:128, s4 * 8:(s4 + 1) * 8, :], w1[0:64, s4 * 8:(s4 + 1) * 8, :], reads=[K_w1 + "#0_%d" % s4], writes=[K_w1 + "#1_%d" % s4])
                dma("pool", posT[0:64, :], D["cmp_posT"][l, kvi], writes=[K_posT + "a"])
                dma("sp", posT[64:128, :], posT[0:64, :], reads=[K_posT + "a"], writes=[K_posT])
                dma("pool", w2, D["cmp_w2"][l, kvi].rearrange("(jc j) d -> j jc d", j=128), writes=[K_w2])
            for kvi in range(2):
                cT, cTk = (cmpTk, K_cTk) if kvi == 0 else (cmpTv, K_cTv)
                w1, K_w1 = w1L[kvi]; posT, K_posT = posTL[kvi]; w2, K_w2 = w2L[kvi]
                dma("sp", b1T, D["cmp_b1T"][l, kvi], writes=[K_b1T])
                gload(l, "b2k" if kvi == 0 else "b2v", b2t, K_b2t)
                for jc in range(2):
                    b = bank.next()
                    for s in range(32):
                        I("pe", "matmul", dict(out=ps[b][:, 0:1], lhsT=w1[0:64, s, jc * 128:(jc + 1) * 128], rhs=posT[0:64, s:s + 1],
                                               start=(s == 0), stop=(s == 31)), reads=[K_w1, K_posT, K_posT + "a"], writes=["ps%d" % b])
                    I("dve", "tensor_tensor", dict(out=c1b[:, jc:jc + 1], in0=ps[b][:, 0:1], in1=b1T[:, jc:jc + 1], op=ALU.add),
                      reads=["ps%d" % b, K_b1T], writes=[K_c1b])
                for hh in range(2):
                    for jc in range(2):
                        b = bank.next()
                        for s in range(32):
                            I("pe", "matmul", dict(out=ps[b][:, 0:255], lhsT=w1[hh * 64:(hh + 1) * 64, s, jc * 128:(jc + 1) * 128],
                                                   rhs=cT[hh * 64:(hh + 1) * 64, s:s + 16 * 254 + 1:16], start=(s == 0), stop=(s == 31)),
                              reads=[K_w1, cTk], writes=["ps%d" % b])
                        I("act", "activation", dict(out=h1T[:, hh, jc, 0:255], in_=ps[b][:, 0:255], func=AF.Gelu_apprx_tanh, bias=c1b[:, jc:jc + 1]),
                          reads=["ps%d" % b, K_c1b], writes=[K_h1T])
                for hh in range(2):
                    for nt in range(2):
                        b = bank.next()
                        t1, t1k = t1s.next(); t2, t2k = t2s.next(); st, stk = sts.next()
                        for jc in range(2):
                            I("pe", "matmul", dict(out=ps[b][:, 0:64], lhsT=h1T[:, hh, jc, nt * 128:(nt + 1) * 128], rhs=w2[:, jc, :],
                                                   start=(jc == 0), stop=(jc == 1)), reads=[K_h1T, K_w2], writes=["ps%d" % b])
                        if kvi == 1:
                            I("dve", "tensor_tensor", dict(out=vcov[:, nt, hh, 0:64], in0=ps[b][:, 0:64], in1=b2t, op=ALU.add),
                              reads=["ps%d" % b, K_b2t], writes=[K_vcov])
                        else:
                            I("dve", "tensor_tensor", dict(out=t1, in0=ps[b][:, 0:64], in1=b2t, op=ALU.add), reads=["ps%d" % b, K_b2t], writes=[t1k])
                            rstd = rms_rstd(t1, 64, t1k, junk, jk, st, stk, 0)
                            I("dve", "scalar_tensor_tensor", dict(out=t2, in0=t1, scalar=rstd, in1=kcgt, op0=ALU.mult, op1=ALU.mult),
                              reads=[t1k, stk, K_kcgt], writes=[t2k])
                            b2_ = bank.next()
                            pst = ps[b2_][:, :].bitcast(BF16)
                            I("pe", "transpose", dict(out=pst[0:64, 0:128], in_=t2, identity=identb[:]), reads=[t2k, "identf"], writes=["ps%d" % b2_])
                            I("act", "activation", dict(out=kcT[0:64, hh, nt * 128:(nt + 1) * 128], in_=pst[0:64, 0:128], func=AF.Copy),
                              reads=["ps%d" % b2_], writes=[K_kcT])
            dump("kcT", kcT[0:64, :, :], [64, 2, 256], [K_kcT])
            dump("vcov", vcov, [128, 2, 2, 128], [K_vcov, K_vcov + "ov0", K_vcov + "ov1"])
            if stop in ("P3", "%d:P3" % l):
                break
            P.barrier()
            AR.off = mark_tr
            W_out, K_Wout = AR.alloc([8, 1024], BF16, "W_out")
            mog1, K_mog1 = AR.alloc([512], F32, "mog1")
            QBs = [Rot([AR.alloc([512], BF16, "QB%d" % hh) for _ in range(2)]) for hh in range(2)]
            QW1s = Rot([AR.alloc([512], BF16, "QW1") for _ in range(2)])
            PTs = Rot([AR.alloc([512], BF16, "PT") for _ in range(7)])
            oaccs = Rot([AR.alloc([512], F32, "oacc") for _ in range(2)])
            tmps = Rot([AR.alloc([256], F32, "tmp") for _ in range(2)])
            imp3s = Rot([AR.alloc([256], F32, "imp3") for _ in range(2)])
            smalls = Rot([AR.alloc([256], F32, "small") for _ in range(2)])
            bnegs = Rot([AR.alloc([64], BF16, "bneg") for _ in range(2)])
            bns = Rot([AR.alloc([512], BF16, "b_n") for _ in range(2)])
            mixTs = Rot([AR.alloc([8, 128], BF16, "mixT") for _ in range(2)])
            junk, jk = AR.alloc([512], BF16, "junk")
            sts = Rot([AR.alloc([8], F32, "st") for _ in range(2)])
            wload(W_out, D["w_out"][l], K_Wout, nsplit=4)
            gload(l, "mog1", mog1, K_mog1)
            sbank = Rot([0, 1, 2, 7])
            LOOKAHEAD = int(os.environ.get("LOOKAHEAD", "3"))
            pst6 = ps[6][:, :].bitcast(BF16)
            NT4 = int(os.environ.get("NT4", "16"))
            for i in range(NT4):
                xk = "x%d" % i
                oacc, oak = oaccs.next()
                QB = []
                for hh in range(2):
                    QB.append(QBs[hh].next())
                QW1, QW1k = QW1s.next()
                for c in range(4):
                    I("pe", "transpose", dict(out=pst6[:, c * 128:(c + 1) * 128], in_=q_tok[:, i, c * 128:(c + 1) * 128], identity=identb[:]),
                      reads=[K_qtok + "_%d" % i, "identf"], writes=["ps6q"])
                for hh in range(2):
                    qb, qbk = QB[hh]
                    qv = qb[0:64, :].rearrange("p (c e t) -> p c e t", c=2, e=2)
                    src_e = pst6[0:64, hh * 256:(hh + 1) * 256].rearrange("p (c t) -> p c t", c=2)
                    src_o = pst6[64:128, hh * 256:(hh + 1) * 256].rearrange("p (c t) -> p c t", c=2)
                    I("dve", "tensor_copy", dict(out=qv[:, :, 0, :], in_=src_e), reads=["ps6q"], writes=[qbk + "Q"])
                    I("dve", "tensor_copy", dict(out=qv[:, :, 1, :], in_=src_o), reads=["ps6q"], writes=[qbk + "Q"])
                    if hh == 1:
                        qv1 = QW1[64:128, :].rearrange("p (c e t) -> p c e t", c=2, e=2)
                        I("dve", "tensor_copy", dict(out=qv1[:, :, 0, :], in_=src_e), reads=["ps6q"], writes=[QW1k])
                        I("dve", "tensor_copy", dict(out=qv1[:, :, 1, :], in_=src_o), reads=["ps6q"], writes=[QW1k])
                gview = gates[:, i, :].rearrange("p (hd r) -> p hd r", r=3)

                def finish_branch(acc_view64, den_ap, r, hh, first, sm, smk, tmp, tmpk, deps):
                    I("dve", "tensor_scalar", dict(out=sm[:, 8:12], in0=den_ap, scalar1=1e-30, scalar2=None, op0=ALU.max),
                      reads=deps, writes=[smk + "f"])
                    I("dve", "reciprocal", dict(out=sm[:, 12:16], in_=sm[:, 8:12]), reads=[smk + "f"], writes=[smk + "f"])
                    I("dve", "tensor_tensor", dict(out=sm[:, 16:20], in0=sm[:, 12:16], in1=gview[:, 4 * hh:4 * hh + 4, r], op=ALU.mult),
                      reads=[smk + "f", K_gates + "_%d" % i], writes=[smk + "f"])
                    ov_ = oacc[:, hh * 256:(hh + 1) * 256].rearrange("p (g d) -> p g d", d=64)
                    cf = sm[:, 16:20].unsqueeze(2).to_broadcast([128, 4, 64])
                    if first:
                        I("dve", "tensor_tensor", dict(out=ov_, in0=acc_view64, in1=cf, op=ALU.mult), reads=deps + [smk + "f"], writes=[oak + "_%d" % hh])
                    else:
                        tv = tmp.rearrange("p (g d) -> p g d", d=64)
                        I("dve", "tensor_tensor", dict(out=tv, in0=acc_view64, in1=cf, op=ALU.mult), reads=deps + [smk + "f"], writes=[tmpk])
                        I("dve", "tensor_tensor", dict(out=ov_, in0=ov_, in1=tv, op=ALU.add), reads=[tmpk, oak + "_%d" % hh], writes=[oak + "_%d" % hh])

                for hh in range(2):
                    qb, qbk = QB[hh]
                    sm, smk = smalls.next(); tmp, tmpk = tmps.next(); imp3, imp3k = imp3s.next(); bneg, bnegk = bnegs.next()
                    ptc = []
                    for nt in range(2):
                        sbk = sbank.next()
                        w0 = nt * 128 - 16 * i + 240
                        I("pe", "matmul", dict(out=ps[sbk][:, :], lhsT=kcT[0:64, hh, nt * 128:(nt + 1) * 128], rhs=qb[0:64, :], start=True, stop=False),
                          reads=[K_kcT, qbk + "Q"], writes=["ps%d" % sbk])
                        I("pe", "matmul", dict(out=ps[sbk][:, :], lhsT=Astat[0:64, w0:w0 + 128], rhs=Cst[0:64, :], start=False, stop=True),
                          reads=["Astat", "Cst"], writes=["ps%d" % sbk])
                        pt, ptk = PTs.next()
                        I("act", "activation", dict(out=pt, in_=ps[sbk][:, :], func=AF.Exp, scale=0.125), reads=["ps%d" % sbk], writes=[ptk])
                        ptc.append((pt, ptk))
                    for g in range(4):
                        for nt in range(2):
                            pt, ptk = ptc[nt]
                            I("pe", "matmul", dict(out=ps[3][:, g * 128:(g + 1) * 128], lhsT=pt[:, g * 128:(g + 1) * 128], rhs=vcov[:, nt, hh, :],
                                                   start=(nt == 0), stop=(nt == 1)), reads=[ptk, K_vcov, K_vcov + "ov%d" % hh], writes=["ps3"])
                    acc3 = ps[3][:, :].rearrange("p (g c) -> p g c", c=128)
                    I("dve", "tensor_reduce", dict(out=sm[:, 0:4], in_=acc3[:, :, 64:128], axis=AX.X, op=ALU.add), reads=["ps3"], writes=[smk + "d"])
                    I("dve", "tensor_scalar", dict(out=sm[:, 4:8], in0=sm[:, 0:4], scalar1=1e-30, scalar2=None, op0=ALU.max), reads=[smk + "d"], writes=[smk + "d"])
                    I("dve", "reciprocal", dict(out=sm[:, 0:4], in_=sm[:, 4:8]), reads=[smk + "d"], writes=[smk + "d"])
                    i3v = imp3.rearrange("p (g j) -> p g j", j=64)
                    I("dve", "tensor_tensor", dict(out=i3v, in0=acc3[:, :, 64:128], in1=sm[:, 0:4].unsqueeze(2).to_broadcast([128, 4, 64]), op=ALU.mult),
                      reads=["ps3", smk + "d"], writes=[imp3k])
                    I("dve", "tensor_reduce", dict(out=sm[:, 64:128], in_=imp3.rearrange("p (g j) -> p j g", j=64), axis=AX.X, op=ALU.add),
                      reads=[imp3k], writes=[smk + "i"])
                    I("dve", "tensor_tensor", dict(out=sm[:, 64:128], in0=sm[:, 64:128], in1=Mvalid[:, i, :], op=ALU.mult), reads=[smk + "i", "Mvalid"], writes=[smk + "i"])
                    I("dve", "tensor_tensor", dict(out=sm[:, 64:128], in0=sm[:, 64:128], in1=Madd[:, i, :], op=ALU.add), reads=[smk + "i", "Madd"], writes=[smk + "i"])
                    I("dve", "max", dict(out=sm[:, 32:40], in_=sm[:, 64:128]), reads=[smk + "i"], writes=[smk + "m"])
                    I("dve", "match_replace", dict(out=sm[:, 128:192], in_to_replace=sm[:, 32:40], in_values=sm[:, 64:128], imm_value=-1e30),
                      reads=[smk + "i", smk + "m"], writes=[smk + "w"])
                    I("dve", "max", dict(out=sm[:, 40:48], in_=sm[:, 128:192]), reads=[smk + "w"], writes=[smk + "m"])
                    I("dve", "tensor_scalar", dict(out=bneg, in0=sm[:, 64:128], scalar1=sm[:, 47:48], scalar2=NEG, op0=ALU.is_lt, op1=ALU.mult),
                      reads=[smk + "i", smk + "m"], writes=[bnegk])
                    if i == 0:
                        dump("impm%d" % hh, sm[:, 64:128], [128, 64], [smk + "i"])
                        dump("bneg%d" % hh, bneg, [128, 64], [bnegk])
                    blk = 4 + hh
                    I("pe", "transpose", dict(out=pst6[0:64, blk * 128:(blk + 1) * 128], in_=bneg, identity=identb[:]), reads=[bnegk, "identf"], writes=["ps6b%d" % hh])
                    I("dve", "tensor_copy", dict(out=qb[64:128, :].rearrange("p (g t) -> p g t", g=4),
                                                 in_=pst6[0:64, blk * 128:(blk + 1) * 128].unsqueeze(1).to_broadcast([64, 4, 128])),
                      reads=["ps6b%d" % hh], writes=[qbk + "B"])
                    finish_branch(acc3[:, :, 0:64], sm[:, 4:8], 0, hh, True, sm, smk, tmp, tmpk, ["ps3", smk + "d"])
                    def run_branch(accb, tiles, qk, mask_of, vsrc, vkeys):
                        I("pe", "matmul", dict(out=ps[accb][:, 0:260], lhsT=zerob[:, 0:128], rhs=zerob[:, 0:260], start=True, stop=False),
                          reads=["zerob"], writes=["ps%d" % accb])
                        pend = []
                        for n_, kt in enumerate(tiles):
                            sbk = sbank.next()
                            lhsT, lreads, rhs, rreads = qk(kt)
                            I("pe", "matmul", dict(out=ps[sbk][:, :], lhsT=lhsT, rhs=rhs, start=True, stop=True), reads=lreads + rreads, writes=["ps%d" % sbk])
                            pt, ptk = PTs.next()
                            I("act", "activation", dict(out=pt, in_=ps[sbk][:, :], func=AF.Exp, scale=0.125), reads=["ps%d" % sbk], writes=[ptk])
                            mk = mask_of(n_, kt)
                            if mk is not None:
                                I("pool", "tensor_tensor", dict(out=pt, in0=pt, in1=mk[0], op=ALU.mult), reads=[ptk, mk[1]], writes=[ptk])
                            pend.append((pt, ptk, kt))
                            if len(pend) > LOOKAHEAD:
                                pv(accb, pend.pop(0), vsrc, vkeys, False)
                        while pend:
                            pv(accb, pend.pop(0), vsrc, vkeys, len(pend) == 0)

                    def pv(accb, pend, vsrc, vkeys, last):
                        pt, ptk, kt = pend
                        for g in range(4):
                            I("pe", "matmul", dict(out=ps[accb][:, g * 65:(g + 1) * 65], lhsT=pt[:, g * 128:(g + 1) * 128], rhs=vsrc[:, kt, hh, :],
                                                   start=False, stop=(last and g == 3)),
                              reads=[ptk, vkeys[0] + "_%d" % kt, vkeys[0] + "one"], writes=["ps%d" % accb])

                    kts = [(o, 2 * i - 4 + o) for o in range(6) if 2 * i - 4 + o >= 0]
                    o_of = {kt: o for (o, kt) in kts}
                    rq = qb[0:64, :] if hh == 0 else QW1[64:128, :]
                    rqk = (qbk + "Q") if hh == 0 else QW1k

                    def qk_w(kt):
                        return KW[hh * 64:(hh + 1) * 64, kt * 128:(kt + 1) * 128], [K_KW + "_%d" % kt], rq, [rqk]

                    def mask_w(n_, kt):
                        o = o_of[kt]
                        if o in (0, 1, 4, 5):
                            return maskW[:, {0: 0, 1: 1, 4: 2, 5: 3}[o], :], "maskW"
                        return None
                    run_branch(5, [kt for (_, kt) in kts], qk_w, mask_w, VW, [K_VW])
                    acc5 = ps[5][:, 0:260].rearrange("p (g c) -> p g c", c=65)
                    finish_branch(acc5[:, :, 0:64], acc5[:, :, 64], 2, hh, False, sm, smk, tmp, tmpk, ["ps5"])
                    nkt = 2 * i + 2

                    def qk_s(kt):
                        return KE[:, hh, kt * 128:(kt + 1) * 128], [K_KE + "_%d" % kt, K_KE + "E%d" % hh], qb[:, :], [qbk + "Q", qbk + "B"]

                    def mask_s(n_, kt):
                        if kt >= 2 * i:
                            return maskD[:, kt - 2 * i, :], "maskD"
                        return None
                    run_branch(4, list(range(nkt)), qk_s, mask_s, VS, [K_VS])
                    acc4 = ps[4][:, 0:260].rearrange("p (g c) -> p g c", c=65)
                    finish_branch(acc4[:, :, 0:64], acc4[:, :, 64], 1, hh, False, sm, smk, tmp, tmpk, ["ps4"])
                if i == 0 or dbg is not None:
                    pass
                b_n, bnk = bns.next(); mixT, mixTk = mixTs.next(); st, stk = sts.next()
                okeys = [oak + "_0", oak + "_1"]
                I("act", "activation", dict(out=junk, in_=oacc, func=AF.Square, accum_out=st[:, 0:1]), reads=okeys, writes=[jk, stk])
                I("act", "activation", dict(out=st[:, 1:2], in_=st[:, 0:1], func=AF.Sqrt, scale=1.0 / 512, bias=EPS), reads=[stk], writes=[stk])
                I("dve", "reciprocal", dict(out=st[:, 2:3], in_=st[:, 1:2]), reads=[stk], writes=[stk])
                I("dve", "scalar_tensor_tensor", dict(out=b_n, in0=oacc, scalar=st[:, 2:3], in1=mog1, op0=ALU.mult, op1=ALU.mult),
                  reads=okeys + [stk, K_mog1], writes=[bnk])
                if dbg is not None and "b_tok" in dbg:
                    if i == 0:
                        dbg_b = nc.dram_tensor("dbg_b_tok", [128, 16, 512], F32, kind="ExternalOutput").ap()
                    dma("sp", dbg_b[:, i, :], oacc, reads=okeys, stream="dbg")
                k6 = ["ps6q", "ps6b0", "ps6b1"]
                for c in range(4):
                    I("pe", "transpose", dict(out=pst6[:, c * 128:(c + 1) * 128], in_=a_tok[:, i, c * 128:(c + 1) * 128], identity=identb[:]),
                      reads=[K_atok + "_%d" % i, "identf"], writes=k6)
                for c in range(4):
                    I("pe", "transpose", dict(out=pst6[:, (4 + c) * 128:(5 + c) * 128], in_=b_n[:, c * 128:(c + 1) * 128], identity=identb[:]),
                      reads=[bnk, "identf"], writes=k6)
                I("dve", "tensor_copy", dict(out=mixT, in_=pst6[:, :].rearrange("p (a b) -> p a b", b=128)), reads=k6, writes=[mixTk])
                for hf in range(2):
                    yb = sbank.next()
                    for kc in range(8):
                        I("pe", "matmul", dict(out=ps[yb][:, :], lhsT=mixT[:, kc, :], rhs=W_out[:, kc, hf * 512:(hf + 1) * 512], start=(kc == 0), stop=(kc == 7)),
                          reads=[mixTk, K_Wout], writes=["ps%d" % yb])
                    I("dve", "tensor_tensor", dict(out=x_res[:, i, hf * 512:(hf + 1) * 512], in0=ps[yb][:, :], in1=x_res[:, i, hf * 512:(hf + 1) * 512], op=ALU.add),
                      reads=["ps%d" % yb, xk], writes=[xk])
            dump("x1", x_res[:, :, :], [128, 16, 1024], ["x%d" % i for i in range(16)])
            if stop in ("P4", "%d:P4" % l):
                break
            P.barrier()
            AR.off = 0
            W_mkv, K_Wmkv = AR.alloc([8, 1024], BF16, "W_mkv")
            W_mq, K_Wmq = AR.alloc([8, 512], BF16, "W_mq")
            W_mo, K_Wmo = AR.alloc([4, 1024], BF16, "W_mo")
            gmkv, K_gmkv = AR.alloc([1024], F32, "gmkv")
            gmem, K_gmem = AR.alloc([1024], F32, "gmem")
            mqg, K_mqg = AR.alloc([512], F32, "mqg")
            mkg, K_mkg = AR.alloc([512], F32, "mkg")
            hTm, K_hTm = AR.alloc([8, 256], BF16, "hTm")
            KmT, K_KmT = AR.alloc([4, 256], BF16, "KmT")
            Vm, K_Vm = AR.alloc([2, 4, 129], BF16, "Vm")
            xins = Rot([AR.alloc([1024], F32, "xin") for _ in range(2)])
            hs = Rot([AR.alloc([1024], BF16, "h") for _ in range(2)])
            hTs = Rot([AR.alloc([8, 128], BF16, "hT") for _ in range(3)])
            junk, jk = AR.alloc([1024], BF16, "junk")
            sts = Rot([AR.alloc([32], F32, "st") for _ in range(4)])
            t1s = Rot([AR.alloc([512], F32, "t1") for _ in range(3)])
            qns = Rot([AR.alloc([512], BF16, "qn") for _ in range(3)])
            QmTs = Rot([AR.alloc([4, 128], BF16, "QmT") for _ in range(3)])
            PTs = Rot([AR.alloc([512], BF16, "PTm") for _ in range(4)])
            oms = Rot([AR.alloc([512], BF16, "om") for _ in range(2)])
            omTs = Rot([AR.alloc([4, 128], BF16, "omT") for _ in range(2)])
            wload(W_mkv, D["w_mkv"][l], K_Wmkv, nsplit=4)
            wload(W_mq, D["w_mq"][l], K_Wmq, nsplit=2)
            wload(W_mo, D["w_mo"][l], K_Wmo, nsplit=2)
            gload(l, "mkvg", gmkv, K_gmkv); gload(l, "memg", gmem, K_gmem); gload(l, "mqg", mqg, K_mqg); gload(l, "mkg", mkg, K_mkg)
            I("dve", "memset", dict(ap=Vm[:, :, :, 128:129], constant=1.0), writes=[K_Vm + "one"])

            def head_rms128(psb, gt, gk, st, stk, t1, t1k, out_bf, outk):
                I("act", "activation", dict(out=t1, in_=ps[psb][:, :], func=AF.Square), reads=["ps%d" % psb], writes=[t1k])
                I("dve", "tensor_reduce", dict(out=st[:, 8:12], in_=t1.rearrange("p (a b) -> p a b", b=128), axis=AX.X, op=ALU.add), reads=[t1k], writes=[stk + "q"])
                I("act", "activation", dict(out=st[:, 12:16], in_=st[:, 8:12], func=AF.Sqrt, scale=1.0 / 128, bias=EPS), reads=[stk + "q"], writes=[stk + "q"])
                I("dve", "reciprocal", dict(out=st[:, 16:20], in_=st[:, 12:16]), reads=[stk + "q"], writes=[stk + "q"])
                I("dve", "tensor_tensor", dict(out=t1.rearrange("p (a b) -> p a b", b=128), in0=ps[psb][:, :].rearrange("p (a b) -> p a b", b=128),
                                               in1=st[:, 16:20].unsqueeze(2).to_broadcast([128, 4, 128]), op=ALU.mult), reads=["ps%d" % psb, stk + "q"], writes=[t1k])
                I("dve", "tensor_tensor", dict(out=out_bf, in0=t1, in1=gt, op=ALU.mult), reads=[t1k, gk], writes=[outk])

            for mt in range(2):
                xin, xk_ = xins.next(); h, hk = hs.next(); hT, hTk = hTs.next(); st, stk = sts.next(); t1, t1k = t1s.next(); qn, qnk = qns.next()
                dma("sp", xin, D["mem"][mt * 128:(mt + 1) * 128, :], writes=[xk_], stream="xin%d" % mt)
                norm_to_hT(xin, xk_, gmkv, K_gmkv, h, hk, hT, hTk, junk, jk, st, stk, 6)
                I("dve", "tensor_copy", dict(out=hTm[:, :, mt * 128:(mt + 1) * 128], in_=hT), reads=[hTk], writes=[K_hTm])
                for hf in range(2):
                    for kc in range(8):
                        I("pe", "matmul", dict(out=ps[hf][:, :], lhsT=hTm[:, kc, mt * 128:(mt + 1) * 128], rhs=W_mkv[:, kc, hf * 512:(hf + 1) * 512],
                                               start=(kc == 0), stop=(kc == 7)), reads=[K_hTm, K_Wmkv], writes=["ps%d" % hf])
                head_rms128(0, mkg, K_mkg, st, stk, t1, t1k, qn, qnk)
                pst = ps[7][:, :].bitcast(BF16)
                for c in range(4):
                    I("pe", "transpose", dict(out=pst[:, c * 128:(c + 1) * 128], in_=qn[:, c * 128:(c + 1) * 128], identity=identb[:]), reads=[qnk, "identf"], writes=["ps7"])
                I("dve", "tensor_copy", dict(out=KmT[:, :, mt * 128:(mt + 1) * 128], in_=pst[:, 0:512].rearrange("p (a b) -> p a b", b=128)), reads=["ps7"], writes=[K_KmT])
                I("act", "activation", dict(out=Vm[:, mt, :, 0:128], in_=ps[1][:, :].rearrange("p (a b) -> p a b", b=128), func=AF.Copy), reads=["ps1"], writes=[K_Vm])
            sc_m = 128.0 ** -0.5
            ctx5 = {}
            for i in range(16):
                h, hk = hs.next(); hT, hTk = hTs.next(); st, stk = sts.next(); t1, t1k = t1s.next(); qn, qnk = qns.next()
                QmT, QmTk = QmTs.next(); om, omk = oms.next(); omT, omTk = omTs.next()
                ctx5[i] = ("x%d" % i, h, hk, hT, hTk, st, stk, t1, t1k, qn, qnk, QmT, QmTk, om, omk, omT, omTk)

            def s5A(i):
                (xk, h, hk, hT, hTk, st, stk, t1, t1k, qn, qnk, QmT, QmTk, om, omk, omT, omTk) = ctx5[i]
                norm_to_hT(x_res[:, i, :], xk, gmem, K_gmem, h, hk, hT, hTk, junk, jk, st, stk, 6)

            def s5B(i):
                (xk, h, hk, hT, hTk, st, stk, t1, t1k, qn, qnk, QmT, QmTk, om, omk, omT, omTk) = ctx5[i]
                for kc in range(8):
                    I("pe", "matmul", dict(out=ps[0][:, :], lhsT=hT[:, kc, :], rhs=W_mq[:, kc, :], start=(kc == 0), stop=(kc == 7)), reads=[hTk, K_Wmq], writes=["ps0"])
                head_rms128(0, mqg, K_mqg, st, stk, t1, t1k, qn, qnk)
                pst = ps[7][:, :].bitcast(BF16)
                for c in range(4):
                    I("pe", "transpose", dict(out=pst[:, c * 128:(c + 1) * 128], in_=qn[:, c * 128:(c + 1) * 128], identity=identb[:]), reads=[qnk, "identf"], writes=["ps7"])
                I("dve", "tensor_copy", dict(out=QmT, in_=pst[:, 0:512].rearrange("p (a b) -> p a b", b=128)), reads=["ps7"], writes=[QmTk])

            def s5C(i):
                (xk, h, hk, hT, hTk, st, stk, t1, t1k, qn, qnk, QmT, QmTk, om, omk, omT, omTk) = ctx5[i]
                ptm = []
                for mt in range(2):
                    sbk = 1 + mt
                    for hd in range(4):
                        I("pe", "matmul", dict(out=ps[sbk][:, hd * 128:(hd + 1) * 128], lhsT=KmT[:, hd, mt * 128:(mt + 1) * 128], rhs=QmT[:, hd, :], start=True, stop=True),
                          reads=[K_KmT, QmTk], writes=["ps%d" % sbk])
                    pt, ptk = PTs.next()
                    I("act", "activation", dict(out=pt, in_=ps[sbk][:, :], func=AF.Exp, scale=sc_m), reads=["ps%d" % sbk], writes=[ptk])
                    ptm.append((pt, ptk))
                for hd in range(4):
                    ab = 3 + hd // 2
                    c0 = (hd % 2) * 129
                    for mt in range(2):
                        pt, ptk = ptm[mt]
                        I("pe", "matmul", dict(out=ps[ab][:, c0:c0 + 129], lhsT=pt[:, hd * 128:(hd + 1) * 128], rhs=Vm[:, mt, hd, :], start=(mt == 0), stop=(mt == 1)),
                          reads=[ptk, K_Vm, K_Vm + "one"], writes=["ps%d" % ab])
                for ab in (3, 4):
                    accv = ps[ab][:, 0:258].rearrange("p (a c) -> p a c", c=129)
                    so = 20 + (ab - 3) * 2
                    I("dve", "reciprocal", dict(out=st[:, so:so + 2], in_=accv[:, :, 128]), reads=["ps%d" % ab], writes=[stk + "m%d" % ab])
                    I("dve", "tensor_tensor", dict(out=om[:, (ab - 3) * 256:(ab - 2) * 256].rearrange("p (a c) -> p a c", c=128), in0=accv[:, :, 0:128],
                                                   in1=st[:, so:so + 2].unsqueeze(2).to_broadcast([128, 2, 128]), op=ALU.mult),
                      reads=["ps%d" % ab, stk + "m%d" % ab], writes=[omk])
                pst5 = ps[5][:, :].bitcast(BF16)
                for c in range(4):
                    I("pe", "transpose", dict(out=pst5[:, c * 128:(c + 1) * 128], in_=om[:, c * 128:(c + 1) * 128], identity=identb[:]), reads=[omk, "identf"], writes=["ps5"])
                I("dve", "tensor_copy", dict(out=omT, in_=pst5[:, 0:512].rearrange("p (a b) -> p a b", b=128)), reads=["ps5"], writes=[omTk])
                for hf in range(2):
                    yb = 1 + hf
                    for kc in range(4):
                        I("pe", "matmul", dict(out=ps[yb][:, :], lhsT=omT[:, kc, :], rhs=W_mo[:, kc, hf * 512:(hf + 1) * 512], start=(kc == 0), stop=(kc == 3)),
                          reads=[omTk, K_Wmo], writes=["ps%d" % yb])
                    I("dve", "tensor_tensor", dict(out=x_res[:, i, hf * 512:(hf + 1) * 512], in0=ps[yb][:, :], in1=x_res[:, i, hf * 512:(hf + 1) * 512], op=ALU.add),
                      reads=["ps%d" % yb, xk], writes=[xk])


            for s_ in range(18):
                if s_ < 16:
                    s5A(s_)
                if 0 <= s_ - 1 < 16:
                    s5B(s_ - 1)
                if 0 <= s_ - 2 < 16:
                    s5C(s_ - 2)

            dump("x2", x_res[:, :, :], [128, 16, 1024], ["x%d" % i for i in range(16)])
            if stop in ("P5", "%d:P5" % l):
                break
            P.barrier()
            AR.off = 0
            hTa, K_hTa = AR.alloc([8, 2048], BF16, "hTa")
            hidT, K_hidT = AR.alloc([4, 2048], BF16, "hidT")
            W1gs = Rot([AR.alloc([8, 512], BF16, "W1g") for _ in range(2)])
            W2gs = Rot([AR.alloc([4, 1024], BF16, "W2g") for _ in range(2)])
            rts = Rot([AR.alloc([512], F32, "rt") for _ in range(3)])
            gffn, K_gffn = AR.alloc([1024], F32, "gffn")
            hs = Rot([AR.alloc([1024], BF16, "h") for _ in range(2)])
            junk, jk = AR.alloc([1024], BF16, "junk")
            sts = Rot([AR.alloc([8], F32, "st") for _ in range(2)])
            gload(l, "ffng", gffn, K_gffn)
            w1v = D["w_ff1"][l].rearrange("(kc p) n -> p kc n", p=128)
            w2v = D["w_ff2"][l].rearrange("(jc j) n -> j jc n", j=128)
            wg = {}

            def load_group(grp):
                W1g, W1gk = W1gs.next(); W2g, W2gk = W2gs.next()
                for hf2 in range(2):
                    dma("pool", W1g[:, hf2 * 4:(hf2 + 1) * 4, :], w1v[:, hf2 * 4:(hf2 + 1) * 4, grp * 512:(grp + 1) * 512], writes=[W1gk + "_%d" % hf2],
                        stream="w1g%d_%d" % (grp % 2, hf2))
                    dma("pool", W2g[:, hf2 * 2:(hf2 + 1) * 2, :], w2v[:, grp * 4 + hf2 * 2:grp * 4 + (hf2 + 1) * 2, :], writes=[W2gk + "_%d" % hf2],
                        stream="w2g%d_%d" % (grp % 2, hf2))
                wg[grp] = (W1g, W1gk, W2g, W2gk)
            load_group(0)
            for i in range(16):
                h, hk = hs.next(); st, stk = sts.next()
                pT = 0 if i % 2 == 0 else 7
                rstd = rms_rstd(x_res[:, i, :], 1024, "x%d" % i, junk, jk, st, stk, 0)
                I("dve", "scalar_tensor_tensor", dict(out=h, in0=x_res[:, i, :], scalar=rstd, in1=gffn, op0=ALU.mult, op1=ALU.mult),
                  reads=["x%d" % i, stk, K_gffn], writes=[hk])
                transposes(h, hk, 8, hTa[:, :, i * 128:(i + 1) * 128], K_hTa + "_%d" % i, pT, evac="dve")
            hbank = Rot([1, 2, 3, 4])
            ybank = Rot([5, 6])
            for grp in range(8):
                if grp + 1 < 8:
                    load_group(grp + 1)
                W1g, W1gk, W2g, W2gk = wg[grp]
                for jc in range(4):
                    for tb in range(4):
                        hb = hbank.next()
                        for kc in range(8):
                            I("pe", "matmul", dict(out=ps[hb][:, :], lhsT=W1g[:, kc, jc * 128:(jc + 1) * 128], rhs=hTa[:, kc, tb * 512:(tb + 1) * 512],
                                                   start=(kc == 0), stop=(kc == 7)),
                              reads=[W1gk + "_%d" % (kc // 4)] + [K_hTa + "_%d" % t_ for t_ in range(tb * 4, tb * 4 + 4)], writes=["ps%d" % hb])
                        rt, rtk = rts.next()
                        I("act", "activation", dict(out=rt, in_=ps[hb][:, :], func=AF.Relu), reads=["ps%d" % hb], writes=[rtk])
                        I("pool", "tensor_tensor", dict(out=hidT[:, jc, tb * 512:(tb + 1) * 512], in0=rt, in1=rt, op=ALU.mult), reads=[rtk],
                          writes=[K_hidT + "_%d_%d" % (jc, tb)])
                for i in range(16):
                    for hf in range(2):
                        yb = ybank.next()
                        for jc in range(4):
                            I("pe", "matmul", dict(out=ps[yb][:, :], lhsT=hidT[:, jc, i * 128:(i + 1) * 128], rhs=W2g[:, jc, hf * 512:(hf + 1) * 512],
                                                   start=(jc == 0), stop=(jc == 3)),
                              reads=[K_hidT + "_%d_%d" % (jc, i // 4), W2gk + "_%d" % (jc // 2)], writes=["ps%d" % yb])
                        I("dve", "tensor_tensor", dict(out=x_res[:, i, hf * 512:(hf + 1) * 512], in0=ps[yb][:, :], in1=x_res[:, i, hf * 512:(hf + 1) * 512], op=ALU.add),
                          reads=["ps%d" % yb, "x%d" % i], writes=["x%d" % i])
            P.barrier()

        yo = D["y"].rearrange("(i p) d -> p i d", p=128)
        for i4 in range(4):
            dma("sp", yo[:, i4 * 4:(i4 + 1) * 4, :], x_res[:, i4 * 4:(i4 + 1) * 4, :],
                reads=["x%d" % i for i in range(i4 * 4, i4 * 4 + 4)], stream="yo%d" % i4)
        P.emit()
    return nc, dbg_out


_NC_CACHE = {}


def _assemble(ys):
    out = np.empty((4, 32, 128, 1024), np.float32)
    for c in range(8):
        out[c // 2, (c % 2)::2] = ys[c].reshape(16, 128, 1024)
    return out.reshape(4, 4096, 1024)


def kernel(**inputs):
    inp = {k: np.asarray(v) for k, v in inputs.items()}
    x = np.ascontiguousarray(inp["x"], dtype=np.float32)
    mem = np.ascontiguousarray(inp["mem"], dtype=np.float32)
    consts = [host_consts(h) for h in (0, 1)]
    NL = 2
    if "nc" not in _NC_CACHE:
        _NC_CACHE["nc"] = build(nl=NL)[0]
    nc = _NC_CACHE["nc"]
    ws = [host_layer_weights(inp, l) for l in range(NL)]
    wst = {k: np.stack([w[k] for w in ws]) for k in ws[0]}
    in_maps = []
    for c in range(8):
        b, half = c // 2, c % 2
        m = {}
        xt = x[b].reshape(32, 128, 1024)
        m["x_own"] = np.ascontiguousarray(xt[half::2].reshape(2048, 1024))
        m["mem"] = mem[b]
        m.update(wst)
        m.update(consts[half])
        in_maps.append(m)
    res = run_bass_kernel_spmd(nc, in_maps, core_ids=list(range(8)))
    return _assemble([res.results[c]["y"] for c in range(8)]).astype(np.float32)
```
